# Optimizing a Trainium2 kernel written in Bass

```python
import jax
import jax.numpy as jnp
from jax import lax
import numpy as np

D_MODEL = 1024
BATCH = 16
SEQ = 2048
DEPTH = 2

CTX_LEN = 256
GRID_W = 64
D_MIX = D_MODEL
HEAD_DIM = 64
FT_GROUPS = 4
FT_GDIM = 64
FT_CH = FT_GROUPS * FT_GDIM
ATT_HQ = 8
ATT_HKV = 2
ATT_GROUP = ATT_HQ // ATT_HKV
ATT_CH = ATT_HQ * HEAD_DIM
DN_HEADS = 4
DN_DIM = 64
DN_CH = DN_HEADS * DN_DIM
CONV_K = 3
CONV_CH = 3 * DN_CH
CHUNK = 64
QBLK = 128
D_FF = 4 * D_MODEL
ROPE_THETA = 10000.0
AXIS_DIM = HEAD_DIM // 2
ROT_PAIRS = AXIS_DIM // 2
EPS = 1e-6
_S_F = FT_CH
_S_AQ = _S_F + ATT_CH
_S_AK = _S_AQ + ATT_HKV * HEAD_DIM
_S_AV = _S_AK + ATT_HKV * HEAD_DIM
_S_DQKV = _S_AV + CONV_CH
_S_DZ = _S_DQKV + DN_CH
PROJ = _S_DZ + 4 * DN_HEADS
SPLITS = (_S_F, _S_AQ, _S_AK, _S_AV, _S_DQKV, _S_DZ)

kernel_name = 'hybrid_fourier_gqa_gdn_prefix_block'


def _rms(x, g):
    xf = x.astype(jnp.float32)
    y = xf * lax.rsqrt(jnp.mean(xf * xf, axis=-1, keepdims=True) + EPS)
    return (y * g.astype(jnp.float32)).astype(x.dtype)


def _l2(x):
    xf = x.astype(jnp.float32)
    return xf * lax.rsqrt(jnp.sum(xf * xf, axis=-1, keepdims=True) + EPS)


def _rope2d(x, cos, sin):
    xs = x.reshape(x.shape[:-1] + (2, 2, ROT_PAIRS))
    rot = jnp.stack([-xs[..., 1, :], xs[..., 0, :]], axis=-2).reshape(x.shape)
    return (x.astype(jnp.float32) * cos + rot.astype(jnp.float32) * sin).astype(x.dtype)


def _dwconv(u, w):
    return lax.conv_general_dilated(u, w[:, None, :].astype(u.dtype), window_strides=(1,),
                                    padding=[(CONV_K // 2, CONV_K // 2)],
                                    dimension_numbers=('NWC', 'WIO', 'NWC'),
                                    feature_group_count=u.shape[-1])


def _fourier(f_in):
    B, L, _ = f_in.shape
    f = f_in.astype(jnp.float32).reshape(B, L, FT_GROUPS, FT_GDIM)
    y = jnp.fft.fftn(f, axes=(1, 3), norm='ortho').real
    return y.reshape(B, L, FT_CH).astype(f_in.dtype)


def _attend(q, k, v):
    B, L, _, D = q.shape
    nb = L // QBLK
    qb = q.reshape(B, nb, QBLK, ATT_HKV, ATT_GROUP, D).transpose(1, 0, 2, 3, 4, 5)
    scale = D ** -0.5

    def blk(qi):
        s = jnp.einsum('bqhgd,bkhd->bhgqk', qi, k, preferred_element_type=jnp.float32) * scale
        p = jax.nn.softmax(s, axis=-1).astype(v.dtype)
        return jnp.einsum('bhgqk,bkhd->bqhgd', p, v)

    o = lax.map(blk, qb)
    return o.transpose(1, 0, 2, 3, 4, 5).reshape(B, L, ATT_HQ * D)


def _chunk_gdn(q, k, v, g, beta, s0):
    B, L, H, DK = q.shape
    DV = v.shape[-1]
    N = L // CHUNK
    ch = lambda t: t.astype(jnp.float32).reshape((B, N, CHUNK, H) + t.shape[3:]).swapaxes(2, 3).swapaxes(0, 1)
    qc, kc, vc = ch(q), ch(k), ch(v)
    gc = jnp.cumsum(ch(g), axis=-1)
    bc = ch(beta)
    idx = jnp.arange(CHUNK)
    tril = idx[:, None] >= idx[None, :]
    strict = idx[:, None] > idx[None, :]
    decay = jnp.exp(jnp.where(tril, gc[..., :, None] - gc[..., None, :], -jnp.inf))
    kb = kc * bc[..., None]
    vb = vc * bc[..., None]
    lmat = jnp.where(strict, jnp.einsum('nbhid,nbhjd->nbhij', kb, kc) * decay, 0.0)
    eye = jnp.eye(CHUNK, dtype=jnp.float32)
    tmat = lax.linalg.triangular_solve(eye + lmat, jnp.broadcast_to(eye, lmat.shape),
                                       left_side=True, lower=True)
    u = jnp.einsum('nbhij,nbhjd->nbhid', tmat, vb)
    w = jnp.einsum('nbhij,nbhjd->nbhid', tmat, kb * jnp.exp(gc)[..., None])
    qk = jnp.where(tril, jnp.einsum('nbhid,nbhjd->nbhij', qc, kc) * decay, 0.0)

    def step(S, inp):
        q_i, k_i, u_i, w_i, g_i, qk_i = inp
        v_new = u_i - jnp.einsum('bhcd,bhde->bhce', w_i, S)
        o = (jnp.einsum('bhcd,bhde->bhce', q_i * jnp.exp(g_i)[..., None], S)
             + jnp.einsum('bhij,bhje->bhie', qk_i, v_new))
        g_last = g_i[..., -1]
        S = (S * jnp.exp(g_last)[..., None, None]
             + jnp.einsum('bhcd,bhce->bhde', k_i * jnp.exp(g_last[..., None] - g_i)[..., None], v_new))
        return S, o

    s_fin, o = lax.scan(step, s0, (qc, kc, u, w, gc, qk))
    return o.transpose(1, 0, 3, 2, 4).reshape(B, L, H, DV), s_fin


def _bidir_gdn(P, s0_f, s0_b):
    fl = lambda t: t[:, ::-1]
    o_f, s_f = _chunk_gdn(P['dq'], P['dk'], P['dv'], P['g'][:, :, 0], P['beta'][:, :, 0], s0_f)
    o_b, s_b = _chunk_gdn(fl(P['dq']), fl(P['dk']), fl(P['dv']), fl(P['g'][:, :, 1]),
                          fl(P['beta'][:, :, 1]), s0_b)
    return o_f + fl(o_b), s_f, s_b


def _project(h, lp, rope):
    B, L, _ = h.shape
    z = h @ lp['w_in']
    f, aq, ak, av, dqkv, dz, gates = jnp.split(z, SPLITS, axis=-1)
    aq = _rms(aq.reshape(B, L, ATT_HQ, HEAD_DIM), lp['q_norm'])
    ak = _rms(ak.reshape(B, L, ATT_HKV, HEAD_DIM), lp['k_norm'])
    if rope is not None:
        aq = _rope2d(aq, rope[0], rope[1])
        ak = _rope2d(ak, rope[0], rope[1])
    dqkv = jax.nn.silu(_dwconv(dqkv, lp['conv_w'])).reshape(B, L, 3, DN_HEADS, DN_DIM)
    gates = gates.astype(jnp.float32).reshape(B, L, 4, DN_HEADS)
    beta = jax.nn.sigmoid(gates[:, :, 0:2])
    g = -jnp.exp(lp['a_log'].astype(jnp.float32)) * jax.nn.softplus(
        gates[:, :, 2:4] + lp['dt_bias'].astype(jnp.float32))
    return dict(f=f, aq=aq, ak=ak, av=av.reshape(B, L, ATT_HKV, HEAD_DIM),
                dq=_l2(dqkv[:, :, 0]) * (DN_DIM ** -0.5), dk=_l2(dqkv[:, :, 1]),
                dv=dqkv[:, :, 2].astype(jnp.float32), dz=dz, beta=beta, g=g)


def _mix_out(P, o_att, o_dn, w_out, o_norm_g):
    B, L, _ = o_att.shape
    gate = jax.nn.silu(P['dz'].astype(jnp.float32)).reshape(B, L, DN_HEADS, DN_DIM)
    dn = (_rms(o_dn, o_norm_g) * gate).reshape(B, L, DN_CH).astype(o_att.dtype)
    cat = jnp.concatenate([_fourier(P['f']), o_att, dn], axis=-1)
    return cat @ w_out


def _mlp(h, w1, w2):
    u = jnp.maximum(h @ w1, 0.0)
    return (u * u) @ w2


def _layer(x, xc, c, c_ctx, rope, lp, ctx_out):
    mod = jax.nn.silu(c) @ lp['w_mod'] + lp['b_mod']
    modc = jax.nn.silu(c_ctx) @ lp['w_mod'] + lp['b_mod']
    sh1, sc1, g1, sh2, sc2, g2 = jnp.split(mod[:, None, :], 6, axis=-1)
    csh1, csc1, cg1, csh2, csc2, cg2 = jnp.split(modc, 6)
    h = _rms(x, lp['norm1']) * (1 + sc1) + sh1
    hc = _rms(xc, lp['norm1']) * (1 + csc1) + csh1
    P = _project(h, lp, rope)
    Pc = _project(hc, lp, None)
    s0 = jnp.zeros((x.shape[0], DN_HEADS, DN_DIM, DN_DIM), jnp.float32)
    oc_dn, s_f, s_b = _bidir_gdn(Pc, s0, s0)
    k_all = jnp.concatenate([P['ak'], Pc['ak']], axis=1)
    v_all = jnp.concatenate([P['av'], Pc['av']], axis=1)
    o_att = _attend(P['aq'], k_all, v_all)
    o_dn, _, _ = _bidir_gdn(P, s_f, s_b)
    x = x + g1 * _mix_out(P, o_att, o_dn, lp['w_out'], lp['o_norm'])
    x = x + g2 * _mlp(_rms(x, lp['norm2']) * (1 + sc2) + sh2, lp['w_ff1'], lp['w_ff2'])
    if ctx_out:
        oc_att = _attend(Pc['aq'], Pc['ak'], Pc['av'])
        xc = xc + cg1 * _mix_out(Pc, oc_att, oc_dn, lp['w_out'], lp['o_norm'])
        xc = xc + cg2 * _mlp(_rms(xc, lp['norm2']) * (1 + csc2) + csh2, lp['w_ff1'], lp['w_ff2'])
    return x, xc


def setup_inputs(seed: int = 0) -> dict:
    key = jax.random.key(seed)
    ks = jax.random.split(key, 18)
    f32 = jnp.float32
    nrm = lambda k, shape, s: jax.random.normal(k, shape, f32) * s
    x = nrm(ks[0], (BATCH, SEQ, D_MODEL), 1.0)
    c = nrm(ks[1], (BATCH, D_MODEL), 1.0)
    ctx = nrm(ks[2], (BATCH, CTX_LEN, D_MODEL), 1.0)
    c_ctx = nrm(ks[3], (D_MODEL,), 1.0)
    norm1_g = 1.0 + nrm(ks[4], (DEPTH, D_MODEL), 0.02)
    norm2_g = 1.0 + nrm(ks[5], (DEPTH, D_MODEL), 0.02)
    w_mod = nrm(ks[6], (DEPTH, D_MODEL, 6 * D_MODEL), 0.5 * D_MODEL ** -0.5)
    b_mod = nrm(ks[7], (DEPTH, 6 * D_MODEL), 0.02)
    w_in = nrm(ks[8], (DEPTH, D_MODEL, PROJ), D_MODEL ** -0.5)
    conv_w = nrm(ks[9], (DEPTH, CONV_K, CONV_CH), CONV_K ** -0.5)
    q_norm_g = 1.0 + nrm(ks[10], (DEPTH, HEAD_DIM), 0.02)
    k_norm_g = 1.0 + nrm(ks[11], (DEPTH, HEAD_DIM), 0.02)
    a_log = jnp.log(jax.random.uniform(ks[12], (DEPTH, 2, DN_HEADS), f32, 1.0, 16.0))
    dt = jnp.exp(jax.random.uniform(ks[13], (DEPTH, 2, DN_HEADS), f32,
                                    float(np.log(1e-3)), float(np.log(1e-1))))
    dt_bias = dt + jnp.log(-jnp.expm1(-dt))
    o_norm_g = 1.0 + nrm(ks[14], (DEPTH, DN_DIM), 0.02)
    w_out = nrm(ks[15], (DEPTH, D_MIX, D_MODEL), D_MIX ** -0.5)
    w_ff1 = nrm(ks[16], (DEPTH, D_MODEL, D_FF), D_MODEL ** -0.5)
    w_ff2 = nrm(ks[17], (DEPTH, D_FF, D_MODEL), D_FF ** -0.5)
    return {'x': x, 'c': c, 'ctx': ctx, 'c_ctx': c_ctx, 'norm1_g': norm1_g, 'norm2_g': norm2_g,
            'w_mod': w_mod, 'b_mod': b_mod, 'w_in': w_in, 'conv_w': conv_w,
            'q_norm_g': q_norm_g, 'k_norm_g': k_norm_g, 'a_log': a_log, 'dt_bias': dt_bias,
            'o_norm_g': o_norm_g, 'w_out': w_out, 'w_ff1': w_ff1, 'w_ff2': w_ff2}


def reference(x, c, ctx, c_ctx, norm1_g, norm2_g, w_mod, b_mod, w_in, conv_w, q_norm_g, k_norm_g,
              a_log, dt_bias, o_norm_g, w_out, w_ff1, w_ff2):
    S = x.shape[1]
    rows = S // GRID_W
    t_row = jnp.repeat(jnp.arange(rows), GRID_W).astype(jnp.float32)
    t_col = jnp.tile(jnp.arange(GRID_W), rows).astype(jnp.float32)
    inv_freq = ROPE_THETA ** (-jnp.arange(ROT_PAIRS, dtype=jnp.float32) * 2.0 / AXIS_DIM)
    ang_r = t_row[:, None] * inv_freq
    ang_c = t_col[:, None] * inv_freq
    ang = jnp.concatenate([ang_r, ang_r, ang_c, ang_c], axis=-1)
    rope = (jnp.cos(ang)[:, None, :], jnp.sin(ang)[:, None, :])
    x_lat, x_ctx = x, ctx
    for l in range(DEPTH):
        lp = dict(norm1=norm1_g[l], norm2=norm2_g[l], w_mod=w_mod[l], b_mod=b_mod[l],
                  w_in=w_in[l], conv_w=conv_w[l], q_norm=q_norm_g[l], k_norm=k_norm_g[l],
                  a_log=a_log[l], dt_bias=dt_bias[l], o_norm=o_norm_g[l], w_out=w_out[l],
                  w_ff1=w_ff1[l], w_ff2=w_ff2[l])
        x_lat, x_ctx = _layer(x_lat, x_ctx, c, c_ctx, rope, lp, l < DEPTH - 1)
    return x_lat
```

```python
import bisect
import numpy as np
import ml_dtypes
import concourse.bass as bass
import concourse.mybir as mybir
from concourse.bass_utils import run_bass_kernel_spmd

F32 = mybir.dt.float32
BF16 = mybir.dt.bfloat16
AF = mybir.ActivationFunctionType
ALU = mybir.AluOpType
AX = mybir.AxisListType

EPOCH = 20000
NDS = 12
EPS = 1e-6
D = 1024
L = 2048
LC = 256
NCORES = 8


class _Op:
    __slots__ = ("waits", "fn", "sig", "dma")

    def __init__(self, fn):
        self.waits = []
        self.fn = fn
        self.sig = None
        self.dma = None


class Sched:
    CE = ("pe", "act", "dve", "pool")
    ALL = ("pe", "act", "dve", "pool", "sp")

    def __init__(self, nc):
        self.nc = nc
        self.ops = {e: [] for e in self.ALL}
        self.sems = {e: [] for e in self.CE}
        self.nsig = {e: 0 for e in self.CE}
        self.sig_idx = {e: [] for e in self.CE}
        self.waited = {e: {} for e in self.ALL}
        self.last_w = {}
        self.readers = {}
        self.dsem = {}
        self.dval = {}
        self.dnext = {}

    def _csem(self, e, k):
        while len(self.sems[e]) <= k:
            self.sems[e].append(self.nc.alloc_semaphore(name=f"s_{e}_{len(self.sems[e])}"))
        return self.sems[e][k]

    def _resolve(self, tok):
        if tok[0] == "d":
            return tok[1], tok[2]
        _, e, idx = tok
        lst = self.sig_idx[e]
        p = bisect.bisect_left(lst, idx)
        if p < len(lst):
            return self.ops[e][lst[p]].sig
        op = self.ops[e][idx]
        n = self.nsig[e]
        self.nsig[e] = n + 1
        op.sig = (self._csem(e, n // EPOCH), (n % EPOCH) + 1)
        lst.append(idx)
        return op.sig

    def _add_wait(self, eng, op, tok):
        if tok is None:
            return
        if tok[0] == "c" and tok[1] == eng and eng == "pe":
            return
        sem, val = self._resolve(tok)
        w = self.waited[eng]
        key = id(sem)
        if w.get(key, 0) >= val:
            return
        w[key] = val
        op.waits.append((sem, val))

    def _deps(self, eng, op, reads, writes):
        toks = []
        for k in reads:
            toks.append(self.last_w.get(k))
        for k in writes:
            toks.append(self.last_w.get(k))
            for r in self.readers.get(k, ()):
                if r[0] == "c" and r[1] == eng:
                    continue
                toks.append(r)
        for t in toks:
            self._add_wait(eng, op, t)

    def op(self, eng, fn, reads=(), writes=(), nosig=False):
        if eng != "pe":
            xs = [k for k in reads if k.startswith("ps") and k not in writes]
            if xs:
                writes = list(writes) + xs
        o = _Op(fn)
        self._deps(eng, o, reads, writes)
        idx = len(self.ops[eng])
        self.ops[eng].append(o)
        if not nosig:
            n = self.nsig[eng]
            self.nsig[eng] = n + 1
            o.sig = (self._csem(eng, n // EPOCH), (n % EPOCH) + 1)
            self.sig_idx[eng].append(idx)
        tok = ("c", eng, idx)
        for k in reads:
            self.readers.setdefault(k, []).append(tok)
        for k in writes:
            self.last_w[k] = tok
            self.readers[k] = []
        return o

    def dma(self, fn, reads=(), writes=(), eng="sp"):
        o = _Op(fn)
        if eng not in self.dsem:
            self.dsem[eng] = [self.nc.alloc_semaphore(name=f"d_{eng}_{i}") for i in range(NDS)]
            self.dval[eng] = [0] * NDS
            self.dnext[eng] = 0
        k = self.dnext[eng]
        self.dnext[eng] = (k + 1) % NDS
        sem = self.dsem[eng][k]
        if self.dval[eng][k] > 0:
            self._add_wait(eng, o, ("d", sem, self.dval[eng][k]))
        self._deps(eng, o, reads, writes)
        self.dval[eng][k] += 16
        o.dma = sem
        self.ops[eng].append(o)
        tok = ("d", sem, self.dval[eng][k])
        for kk in reads:
            self.readers.setdefault(kk, []).append(tok)
        for kk in writes:
            self.last_w[kk] = tok
            self.readers[kk] = []
        return o

    def wait_keys(self, eng, keys):
        o = _Op(None)
        for k in keys:
            self._add_wait(eng, o, self.last_w.get(k))
        self.ops[eng].append(o)

    def finish(self):
        o = _Op(None)
        for eng in self.dsem:
            for k, sem in enumerate(self.dsem[eng]):
                if self.dval[eng][k] > 0:
                    o.waits.append((sem, self.dval[eng][k]))
        self.ops["sp"].append(o)

    def emit(self):
        nc = self.nc
        sched = self
        allsems = [s for e in self.CE for s in self.sems[e]] + [s for e in self.dsem for s in self.dsem[e]]

        with nc.Block() as block0:
            @block0.gpsimd
            def _(e):
                for s in allsems:
                    e.sem_clear(s)

        def run(name, e):
            for o in sched.ops[name]:
                for sem, val in o.waits:
                    e.wait_ge(sem, val)
                if o.fn is None:
                    continue
                ins = o.fn(e)
                if o.sig is not None:
                    ins.then_inc(o.sig[0], 1)
                if o.dma is not None:
                    ins.then_inc(o.dma, 16)

        with nc.Block() as block:
            @block.tensor
            def _(e):
                run("pe", e)

            @block.scalar
            def _(e):
                run("act", e)

            @block.vector
            def _(e):
                run("dve", e)

            @block.gpsimd
            def _(e):
                run("pool", e)

            @block.sync
            def _(e):
                run("sp", e)

    def stats(self):
        return {e: (len(self.ops[e]), sum(len(o.waits) for o in self.ops[e])) for e in self.ALL}


def _host_consts():
    c = {}
    c["ident"] = np.eye(128, dtype=np.float32)
    blk = np.zeros((128, 128), np.float32)
    blk[:64, :64] = 1.0
    blk[64:, 64:] = 1.0
    c["blk1"] = blk
    rm = np.zeros((64, 64), np.float32)
    for a in range(2):
        for p in range(16):
            d0 = a * 32 + p
            d1 = a * 32 + 16 + p
            rm[d1, d0] = -1.0
            rm[d0, d1] = 1.0
    R = np.zeros((128, 128), np.float32)
    R[:64, :64] = rm
    R[64:, 64:] = rm
    c["rotm"] = R
    t = np.arange(L)
    t_row = (t // 64).astype(np.float64)
    t_col = (t % 64).astype(np.float64)
    inv_freq = (10000.0 ** (-np.arange(16, dtype=np.float64) * 2.0 / 32.0))
    inv_freq = inv_freq.astype(np.float32).astype(np.float64)
    ang_r = (t_row[:, None].astype(np.float32) * inv_freq[None, :].astype(np.float32)).astype(np.float64)
    ang_c = (t_col[:, None].astype(np.float32) * inv_freq[None, :].astype(np.float32)).astype(np.float64)
    ang = np.concatenate([ang_r, ang_r, ang_c, ang_c], axis=-1)
    c["ropec"] = np.ascontiguousarray(np.tile(np.cos(ang).T, (2, 1))).astype(np.float32)
    c["ropes"] = np.ascontiguousarray(np.tile(np.sin(ang).T, (2, 1))).astype(np.float32)
    k = np.arange(64)
    a64 = 2.0 * np.pi * ((k[:, None] * k[None, :]) % 64) / 64.0
    chd = np.zeros((128, 256), np.float64)
    for g in range(2):
        chd[g * 64:(g + 1) * 64, g * 64:(g + 1) * 64] = np.cos(a64) / 8.0
        chd[g * 64:(g + 1) * 64, 128 + g * 64:128 + (g + 1) * 64] = np.sin(a64) / 8.0
    c["chd"] = chd.astype(ml_dtypes.bfloat16)

    def dft(n):
        i = np.arange(n)
        a = 2.0 * np.pi * ((i[:, None] * i[None, :]) % n) / n
        return ((np.cos(a) / np.sqrt(n)).astype(ml_dtypes.bfloat16),
                ((-np.sin(a)) / np.sqrt(n)).astype(ml_dtypes.bfloat16))
    c["cl"], c["sl"] = dft(L)
    c["cc"], c["scx"] = dft(LC)
    i = np.arange(64)
    ge = (i[:, None] >= i[None, :]).astype(np.float32)
    le = (i[:, None] <= i[None, :]).astype(np.float32)
    eye = np.eye(64, dtype=np.float32)
    g = np.zeros((64, 5, 4, 64), np.float32)
    for p in range(4):
        fwd = (p % 2 == 0)
        g[:, 0, p, :] = le if fwd else ge
        g[:, 1, p, :] = ge if fwd else le
        g[:, 2, p, :] = le if fwd else ge
        g[:, 3, p, :] = eye
        g[:, 4, p, :] = 1.0 - (le if fwd else ge)
    c["gcon"] = g
    return c


_CONSTS = None


def _consts():
    global _CONSTS
    if _CONSTS is None:
        _CONSTS = _host_consts()
    return _CONSTS


class Prog:
    def __init__(self, cfg):
        self.cfg = cfg
        self.nc = bass.Bass("TRN2", target_bir_lowering=False)
        self.S = Sched(self.nc)
        self.ps_i = 0
        self.uid = 0

    def din(self, name, shape, dt=F32):
        return self.nc.dram_tensor(name, list(shape), dt, kind="ExternalInput").ap()

    def dout(self, name, shape, dt=F32):
        return self.nc.dram_tensor(name, list(shape), dt, kind="ExternalOutput").ap()

    def sb(self, name, shape, dt, off):
        self.uid += 1
        n = int(np.prod(shape[1:])) * (2 if dt == BF16 else 4)
        assert off % 32 == 0, (name, off)
        assert off + n <= 229376 - 2048, (name, off, n)
        t = self.nc.alloc_sbuf_tensor_at(f"{name}_{self.uid}", list(shape), dt, offset=off)
        return t

    def ps(self):
        i = self.ps_i
        self.ps_i = (i + 1) % 8
        return self.psum[i], f"ps{i}"

    def mm(self, out, lhsT, rhs, start=True, stop=True, r=(), w=(), nosig=False):
        def fn(e):
            return e.matmul(out, lhsT=lhsT, rhs=rhs, start=start, stop=stop)
        return self.S.op("pe", fn, reads=r, writes=w, nosig=nosig)

    def act(self, out, in_, func, r=(), w=(), scale=1.0, bias=0.0, eng="act"):
        def fn(e):
            return e.activation(out=out, in_=in_, func=func, scale=scale, bias=bias)
        return self.S.op("act", fn, reads=r, writes=w)

    def tt(self, eng, out, in0, in1, op, r=(), w=()):
        def fn(e):
            return e.tensor_tensor(out=out, in0=in0, in1=in1, op=op)
        return self.S.op(eng, fn, reads=r, writes=w)

    def ts(self, eng, out, in0, s1, op0, s2=None, op1=None, r=(), w=()):
        def fn(e):
            if op1 is None:
                return e.tensor_scalar(out=out, in0=in0, scalar1=s1, scalar2=None, op0=op0)
            return e.tensor_scalar(out=out, in0=in0, scalar1=s1, scalar2=s2, op0=op0, op1=op1)
        return self.S.op(eng, fn, reads=r, writes=w)

    def stt(self, eng, out, in0, scalar, in1, op0, op1, r=(), w=()):
        eng = "dve"

        def fn(e):
            return e.scalar_tensor_tensor(out=out, in0=in0, scalar=scalar, in1=in1, op0=op0, op1=op1)
        return self.S.op(eng, fn, reads=r, writes=w)

    def cp(self, eng, out, in_, r=(), w=()):
        if eng == "act":
            return self.act(out, in_, AF.Copy, r=r, w=w)

        def fn(e):
            return e.tensor_copy(out=out, in_=in_)
        return self.S.op(eng, fn, reads=r, writes=w)

    def memset(self, eng, ap, val, w=()):
        def fn(e):
            return e.memset(ap, val)
        return self.S.op(eng, fn, writes=w)

    def dma(self, out, in_, r=(), w=(), eng="sp"):
        def fn(e):
            return e.dma_start(out=out, in_=in_)
        return self.S.dma(fn, reads=r, writes=w, eng=eng)

    def build(self):
        nc = self.nc
        cfg = self.cfg
        NB = cfg.get("nb", 2)
        NL = cfg.get("nl", 2)
        FAM = cfg.get("fam", "GFAM")
        d = {}
        d["x"] = self.din("x", [2, L, D])
        d["ctx"] = self.din("ctx", [2, LC, D])
        d["cT"] = self.din("cT", [128, 8, 3])
        d["w_mod"] = self.din("w_mod", [2, D, 6 * D])
        d["b_modT"] = self.din("b_modT", [128, 2, 48])
        d["n1T"] = self.din("n1T", [128, 2, 8])
        d["n2T"] = self.din("n2T", [128, 2, 8])
        d["w_in"] = self.din("w_in", [2, D, 2064])
        d["convT"] = self.din("convT", [128, 2, 6, 3])
        d["qgT"] = self.din("qgT", [128, 2])
        d["kgT"] = self.din("kgT", [128, 2])
        d["ogT"] = self.din("ogT", [128, 2])
        d["alog_b"] = self.din("alog_b", [64, 2, 8])
        d["dtb_b"] = self.din("dtb_b", [64, 2, 8])
        d["w_out"] = self.din("w_out", [2, D, D])
        d["w_ff1"] = self.din("w_ff1", [2, D, 4 * D])
        d["w_ff2"] = self.din("w_ff2", [2, 4 * D, D])
        d["ident"] = self.din("ident", [128, 128])
        d["blk1"] = self.din("blk1", [128, 128])
        d["rotm"] = self.din("rotm", [128, 128])
        d["chd"] = self.din("chd", [128, 256], BF16)
        d["cl"] = self.din("cl", [L, L], BF16)
        d["sl"] = self.din("sl", [L, L], BF16)
        d["cc"] = self.din("cc", [LC, LC], BF16)
        d["scx"] = self.din("scx", [LC, LC], BF16)
        d["ropec"] = self.din("ropec", [128, L])
        d["ropes"] = self.din("ropes", [128, L])
        d["gcon"] = self.din("gcon", [64, 5, 4, 64])
        d["out"] = self.dout("out", [2, L, D])
        self.d = d
        self.psum = [nc.alloc_psum_tensor(f"psum{i}", [128, 512], F32) for i in range(8)]

        o = 16640
        G = {}

        def galloc(name, shape, dt=F32):
            nonlocal o
            n = int(np.prod(shape[1:])) * (2 if dt == BF16 else 4)
            t = self.sb(name, shape, dt, o)
            o += (n + 31) // 32 * 32
            G[name] = t
            return t
        ident = galloc("ident", [128, 128])
        identb = galloc("identb", [128, 128], BF16)
        onesf = galloc("onesf", [128, 128])
        negones = galloc("negones", [64, 64])
        blk1 = galloc("blk1", [128, 128])
        rotm = galloc("rotm", [128, 128])
        chd = galloc("chd", [128, 256], BF16)
        gcon = galloc("gcon", [64, 5, 4, 64])
        modp = galloc("modp", [128, 2, 6, 8, 3])
        n1T = galloc("n1T", [128, 2, 8])
        n2T = galloc("n2T", [128, 2, 8])
        convT = galloc("convT", [128, 2, 6, 3])
        qgT = galloc("qgT", [128, 2])
        kgT = galloc("kgT", [128, 2])
        ogT = galloc("ogT", [128, 2])
        alogb = galloc("alogb", [64, 2, 8])
        dtbb = galloc("dtbb", [64, 2, 8])
        nexpa = galloc("nexpa", [64, 2, 8])
        self.epsb = galloc("epsb", [128, 1])
        self.scr = galloc("scr", [128, 8])
        assert o <= 16640 + 10240, o
        XB = 16640 + 10240
        xT = self.sb("xT", [128, 8, L], F32, XB)
        xcT = self.sb("xcT", [128, 8, LC], F32, XB + 8 * L * 4)
        PB = XB + 8 * L * 4 + 8 * LC * 4
        self.PB = PB
        self.G = G
        self.xT, self.xcT = xT, xcT

        for nm in ["ident", "blk1", "rotm", "chd", "gcon", "n1T", "n2T", "convT", "qgT", "kgT", "ogT"]:
            self.dma(G[nm][:], d[nm], w=[nm])
        self.dma(alogb[:], d["alog_b"], w=["alogb"])
        self.dma(dtbb[:], d["dtb_b"], w=["dtbb"])
        self.cp("pool", identb[:], ident[:], r=["ident"], w=["identb"])
        self.memset("pool", onesf[:], 1.0, w=["onesf"])
        self.memset("pool", negones[:], -1.0, w=["negones"])
        self.act(nexpa[:], alogb[:], AF.Exp, r=["alogb"], w=["nexpa"])
        self.ts("pool", nexpa[:], nexpa[:], -1.0, ALU.mult, r=["nexpa"], w=["nexpa"])

        self.phase_mod()
        self.barrier()
        for b in range(NB):
            self.load_x(b)
            self.barrier()
            for l in range(NL):
                last = (l == 1)
                if "G" in FAM:
                    self.phase_gdn(b, l, last)
                    self.barrier()
                self.phase_h(b, l, last, gdn=("G" in FAM and cfg.get("gstage", 0) in (0, 5)))
                self.barrier()
                if "F" in FAM:
                    self.phase_fourier(b, l, last)
                    self.barrier()
                if "A" in FAM:
                    self.phase_attn(b, l, last)
                    self.barrier()
                if "M" in FAM:
                    self.phase_mlp(b, l, last)
                    self.barrier()
            self.store_x(b)
            self.barrier()
        self.S.finish()
        self.S.emit()
        return nc

    def barrier(self):
        scr, ident = self.scr, self.G["ident"]
        self.mm(self.psum[7][0:1, 0:1], ident[0:1, 0:1], ident[0:1, 0:1], r=["ident"], w=["ps7", "bar_pe"])
        self.memset("dve", scr[:, 0:1], 0.0, w=["bar_dve"])
        self.memset("pool", scr[:, 1:2], 0.0, w=["bar_pool"])
        self.act(scr[:, 2:3], self.epsb[:, 0:1], AF.Copy, r=["epsb"], w=["bar_act"])
        for e in Sched.ALL:
            self.S.wait_keys(e, ["bar_pe", "bar_dve", "bar_pool", "bar_act"])

    def xs(self, isctx, c, t0, n):
        return (self.xcT if isctx else self.xT)[:, c, t0:t0 + n]

    def xkey(self, isctx, t0):
        return ("xc" if isctx else f"x{t0 // 512}")

    def load_w(self, dst, dkey, src, stg, kc, ncols, c0, eng="pool"):
        srcv = src.rearrange("(k p) n -> p k n", p=128)
        per = max(1, 2048 // ncols)
        k = 0
        while k < kc:
            kk = min(per, kc - k)
            st, skey = stg[self.stg_i % len(stg)]
            self.stg_i += 1
            sv = st[:, 0:kk * ncols].rearrange("p (k n) -> p k n", n=ncols)
            self.dma(sv, srcv[:, k:k + kk, c0:c0 + ncols], w=[skey])
            self.cp(eng, dst[:, k:k + kk, 0:ncols], sv, r=[skey], w=[dkey])
            k += kk

    def norm_block(self, b, l, which, isctx, t0, n, hout, hkey, tmp):
        G = self.G
        r = 2 if isctx else b
        kA, kB = (0, 1) if which == 1 else (3, 4)
        modp = G["modp"]
        xk = self.xkey(isctx, t0)
        xk2 = self.xkey(isctx, t0 + n - 1)
        xr = [xk] if xk == xk2 else [xk, xk2]
        sq, rstd, xn = tmp["sq"], tmp["rstd"], tmp["xn"]
        pt, pk = self.ps()
        for c in range(8):
            s, sk = sq[c % len(sq)]
            self.act(s[:, 0:n], self.xs(isctx, c, t0, n), AF.Square, r=xr, w=[sk])
            self.mm(pt[:, 0:n], G["onesf"][:], s[:, 0:n], start=(c == 0), stop=(c == 7), r=[sk, "onesf"], w=[pk],
                    nosig=(c != 7))
        rs, rk = rstd
        self.act(rs[:, 0:n], pt[:, 0:n], AF.Ln, r=[pk], w=[rk], scale=1.0 / D, bias=self.epsb[:, 0:1])
        self.act(rs[:, 0:n], rs[:, 0:n], AF.Exp, r=[rk], w=[rk], scale=-0.5)
        for c in range(8):
            t, tk = xn[c % len(xn)]
            self.tt("dve", t[:, 0:n], self.xs(isctx, c, t0, n), rs[:, 0:n], ALU.mult, r=xr + [rk], w=[tk])
            self.ts("pool", hout[:, c, 0:n], t[:, 0:n], modp[:, l, kA, c, r:r + 1], ALU.mult,
                    modp[:, l, kB, c, r:r + 1], ALU.add, r=[tk, "modp"], w=[hkey])

    def phase_mod(self):
        G, d, PB = self.G, self.d, self.PB
        modp = G["modp"]
        self.memset("pool", self.epsb[:], EPS, w=["epsb"])
        cT = self.sb("cT", [128, 8, 3], F32, PB + 64)
        bmT = self.sb("bmT", [128, 2, 48], F32, PB + 256)
        mraw = self.sb("mraw", [128, 2, 48, 3], F32, PB + 1024)
        stg = [(self.sb(f"mstg{i}", [128, 8, 256], F32, PB + 4096 + i * 8192), f"mstg{i}") for i in range(4)]
        self.dma(cT[:], d["cT"], w=["cT"])
        self.dma(bmT[:], d["b_modT"], w=["bmT"])
        self.act(cT[:], cT[:], AF.Silu, r=["cT"], w=["cT"])
        si = 0
        for l in range(2):
            wv = d["w_mod"][l].rearrange("(k p) n -> p k n", p=128)
            pt, pk = self.ps()
            for piece in range(24):
                st, sk = stg[si % 4]
                si += 1
                self.dma(st[:], wv[:, :, piece * 256:(piece + 1) * 256], w=[sk])
                for jj in range(2):
                    j = piece * 2 + jj
                    for k in range(8):
                        self.mm(pt[:, j * 3:(j + 1) * 3], st[:, k, jj * 128:(jj + 1) * 128], cT[:, k, :],
                                start=(k == 0), stop=(k == 7), r=[sk, "cT"], w=[pk], nosig=not (k == 7 and jj == 1))
            self.tt("dve", mraw[:, l, :, :], pt[:, 0:144].rearrange("p (j r) -> p j r", r=3),
                    bmT[:, l, :].unsqueeze(2).broadcast_to([128, 48, 3]), ALU.add, r=[pk, "bmT"], w=["mraw"])
            for (kind, scj, shj, gj, nrm) in ((0, 8, 0, 16, "n1T"), (3, 32, 24, 40, "n2T")):
                self.ts("dve", modp[:, l, kind, :, :], mraw[:, l, scj:scj + 8, :], 1.0, ALU.add, r=["mraw"], w=["modp"])
                self.tt("dve", modp[:, l, kind, :, :], modp[:, l, kind, :, :],
                        G[nrm][:, l, :].unsqueeze(2).broadcast_to([128, 8, 3]), ALU.mult, r=["modp", nrm], w=["modp"])
                self.cp("dve", modp[:, l, kind + 1, :, :], mraw[:, l, shj:shj + 8, :], r=["mraw"], w=["modp"])
                self.cp("dve", modp[:, l, kind + 2, :, :], mraw[:, l, gj:gj + 8, :], r=["mraw"], w=["modp"])

    def load_x(self, b):
        G, d, PB = self.G, self.d, self.PB
        stg = [(self.sb(f"xstg{i}", [128, D], F32, PB + 64 + i * 4096), f"xstg{i}") for i in range(4)]
        si = 0
        for isctx, n_t in ((False, L), (True, LC)):
            src = d["ctx"][b] if isctx else d["x"][b]
            for t0 in range(0, n_t, 512):
                nt = min(512, n_t - t0)
                tiles = []
                for j in range(nt // 128):
                    st, sk = stg[si % 4]
                    si += 1
                    self.dma(st[:], src[t0 + j * 128:t0 + (j + 1) * 128, :], w=[sk])
                    tiles.append((st, sk))
                for c in range(8):
                    pt, pk = self.ps()
                    for j, (st, sk) in enumerate(tiles):
                        self.mm(pt[:, j * 128:(j + 1) * 128], st[:, c * 128:(c + 1) * 128], G["ident"][:],
                                r=[sk, "ident"], w=[pk], nosig=(j != len(tiles) - 1))
                    eng = "dve" if c % 2 == 0 else "act"
                    self.cp(eng, self.xs(isctx, c, t0, nt), pt[:, 0:nt], r=[pk], w=[self.xkey(isctx, t0)])

    def store_x(self, b):
        G, d, PB = self.G, self.d, self.PB
        stg = [(self.sb(f"ostg{i}", [128, D], F32, PB + 64 + i * 4096), f"xstg{i}") for i in range(4)]
        si = 0
        for tt in range(L // 128):
            st, sk = stg[si % 4]
            si += 1
            for half in range(2):
                pt, pk = self.ps()
                for cc in range(4):
                    c = half * 4 + cc
                    self.mm(pt[:, cc * 128:(cc + 1) * 128], self.xT[:, c, tt * 128:(tt + 1) * 128], G["ident"][:],
                            r=[self.xkey(False, tt * 128), "ident"], w=[pk], nosig=(cc != 3))
                eng = "dve" if half == 0 else "act"
                self.cp(eng, st[:, half * 512:(half + 1) * 512], pt[:, :], r=[pk], w=[sk])
            self.dma(d["out"][b][tt * 128:(tt + 1) * 128, :], st[:], r=[sk])

    TB = ((False, 0, 512), (False, 512, 512), (False, 1024, 512), (False, 1536, 512), (True, 0, 256))

    @staticmethod
    def hoff(isctx, t0):
        return (L + t0) if isctx else t0

    def norm_tmp(self, base):
        sq = [(self.sb(f"sq{i}", [128, 512], F32, base + i * 2048), f"sq{i}") for i in range(3)]
        rstd = (self.sb("rstd", [128, 512], F32, base + 6144), "rstd")
        xn = [(self.sb(f"xn{i}", [128, 512], F32, base + 8192 + i * 2048), f"xn{i}") for i in range(2)]
        return {"sq": sq, "rstd": rstd, "xn": xn}

    def residual(self, l, gkind, r, isctx, t0, n, pt, pk, c, eng="dve"):
        xa = self.xs(isctx, c, t0, n)
        xk = self.xkey(isctx, t0)
        self.stt(eng, xa, pt, self.G["modp"][:, l, gkind, c, r:r + 1], xa, ALU.mult, ALU.add,
                 r=[pk, "modp", xk], w=[xk])

    def phase_h(self, b, l, last, gdn):
        PB, d = self.PB, self.d
        hT = self.sb("hT", [128, 8, L + LC], BF16, PB)
        self.hT = hT
        o = PB + 36864
        tmp = self.norm_tmp(o)
        o += 12288
        for (isctx, t0, n) in self.TB:
            self.norm_block(b, l, 1, isctx, t0, n, hT[:, :, self.hoff(isctx, t0):self.hoff(isctx, t0) + n],
                            f"hT{self.hoff(isctx, t0) // 512}", tmp)
        if gdn:
            catD = self.catD
            self.stg_i = 0
            stg = [(self.sb(f"hstg{i}", [128, 2048], F32, o + i * 8192), f"stg{i}") for i in range(2)]
            o += 16384
            wo = self.sb("wo_d", [128, 2, D], BF16, o)
            self.load_w(wo, "wo", d["w_out"][l][768:1024, :], stg, 2, 1024, 0)
            self.out_proj(b, l, last, wo, "wo", 2, lambda j, off, n: catD[:, j, off:off + n], ["catD"])

    def out_proj(self, b, l, last, wo, wokey, nk, catfn, catkeys, blocks=None):
        for (isctx, t0, n) in (blocks or self.TB):
            if isctx and last:
                continue
            r = 2 if isctx else b
            off = self.hoff(isctx, t0)
            for c in range(8):
                pt, pk = self.ps()
                for j in range(nk):
                    self.mm(pt[:, 0:n], wo[:, j, c * 128:(c + 1) * 128], catfn(j, off, n), start=(j == 0),
                            stop=(j == nk - 1), r=[wokey] + catkeys, w=[pk], nosig=(j != nk - 1))
                self.residual(l, 2, r, isctx, t0, n, pt[:, 0:n], pk, c, eng="dve")

    def phase_mlp(self, b, l, last):
        PB, d = self.PB, self.d
        h2 = self.sb("h2T", [128, 8, L + LC], BF16, PB)
        o = PB + 36864
        tmp = self.norm_tmp(o)
        o += 12288
        self.stg_i = 0
        stg = [(self.sb(f"mstg{i}", [128, 2048], F32, o + i * 8192), f"stg{i}") for i in range(2)]
        o += 16384
        w1 = [(self.sb(f"w1e{i}", [128, 8, 512], BF16, o + i * 8192), f"w1e{i}") for i in range(2)]
        o += 16384
        w2 = [(self.sb(f"w2e{i}", [128, 4, D], BF16, o + i * 8192), f"w2e{i}") for i in range(2)]
        o += 16384
        uT = [(self.sb(f"uT{i}", [128, 4, 512], BF16, o + i * 4096), f"uT{i}") for i in range(2)]
        o += 8192
        blocks = [tb for tb in self.TB if not (tb[0] and last)]
        for (isctx, t0, n) in blocks:
            off = self.hoff(isctx, t0)
            self.norm_block(b, l, 2, isctx, t0, n, h2[:, :, off:off + n], f"h2T{off // 512}", tmp)
        ui = 0
        for e8 in range(8):
            w1t, w1k = w1[e8 % 2]
            w2t, w2k = w2[e8 % 2]
            self.load_w(w1t, w1k, d["w_ff1"][l], stg, 8, 512, e8 * 512)
            self.load_w(w2t, w2k, d["w_ff2"][l][e8 * 512:(e8 + 1) * 512, :], stg, 4, 1024, 0)
            for (isctx, t0, n) in blocks:
                off = self.hoff(isctx, t0)
                r = 2 if isctx else b
                ut, uk = uT[ui % 2]
                ui += 1
                for fc in range(4):
                    pt, pk = self.ps()
                    for k in range(8):
                        self.mm(pt[:, 0:n], w1t[:, k, fc * 128:(fc + 1) * 128], h2[:, k, off:off + n], start=(k == 0),
                                stop=(k == 7), r=[w1k, f"h2T{off // 512}"], w=[pk], nosig=(k != 7))
                    tq, tk = tmp["sq"][fc % 3]
                    self.act(tq[:, 0:n], pt[:, 0:n], AF.Relu, r=[pk], w=[tk])
                    self.tt("pool", ut[:, fc, 0:n], tq[:, 0:n], tq[:, 0:n], ALU.mult, r=[tk], w=[uk])
                for c in range(8):
                    pt, pk = self.ps()
                    for fc in range(4):
                        self.mm(pt[:, 0:n], w2t[:, fc, c * 128:(c + 1) * 128], ut[:, fc, 0:n], start=(fc == 0),
                                stop=(fc == 3), r=[w2k, uk], w=[pk], nosig=(fc != 3))
                    self.residual(l, 5, r, isctx, t0, n, pt[:, 0:n], pk, c)

    def phase_fourier(self, b, l, last):
        PB, d, G = self.PB, self.d, self.G
        hT = self.hT
        o = PB + 36864
        self.stg_i = 0
        stg = [(self.sb(f"fstg{i}", [128, 2048], F32, o + i * 8192), f"stg{i}") for i in range(2)]
        o += 16384
        wf = self.sb("wf", [128, 8, 256], BF16, o)
        o += 4096
        wo = self.sb("wo_f", [128, 2, D], BF16, o)
        o += 4096
        fT = [(self.sb(f"fT{i}", [128, 2, 512], BF16, o + i * 2048), f"fT{i}") for i in range(2)]
        o += 4096
        Gt = self.sb("Gt", [128, 18, 2, 256], BF16, o)
        o += 18432
        dft = [(self.sb(f"dft{i}", [128, 2, 16, 256], BF16, o + i * 16384), f"dft{i}") for i in range(2)]
        o += 32768
        catF = [(self.sb(f"catF{i}", [128, 2, 256], BF16, o + i * 1024), f"catF{i}") for i in range(2)]
        o += 2048
        assert o <= 229376
        self.load_w(wf, "wf", d["w_in"][l], stg, 8, 256, 0)
        self.load_w(wo, "wo", d["w_out"][l][0:256, :], stg, 2, 1024, 0)
        blocks = [tb for tb in self.TB if not (tb[0] and last)]
        fi = 0
        for (isctx, t0, n) in blocks:
            off = self.hoff(isctx, t0)
            ft, fk = fT[fi % 2]
            fi += 1
            for ch in range(2):
                pt, pk = self.ps()
                for k in range(8):
                    self.mm(pt[:, 0:n], wf[:, k, ch * 128:(ch + 1) * 128], hT[:, k, off:off + n], start=(k == 0),
                            stop=(k == 7), r=["wf", f"hT{off // 512}"], w=[pk], nosig=(k != 7))
                self.cp("act" if ch == 0 else "dve", ft[:, ch, 0:n], pt[:, 0:n], r=[pk], w=[fk])
            for j in range(n // 128):
                tile = off // 128 + j
                pt, pk = self.ps()
                for ch in range(2):
                    self.mm(pt[:, ch * 256:(ch + 1) * 256], ft[:, ch, j * 128:(j + 1) * 128], G["chd"][:],
                            r=[fk, "chd"], w=[pk], nosig=(ch == 0))
                self.cp("act" if j % 2 == 0 else "dve", Gt[:, tile, :, :].rearrange("p c n -> p (c n)"), pt[:, :],
                        r=[pk], w=["Gt"])
        di = 0
        ci = 0
        for (isctx, nl, tile0, ctab, stab) in ((False, L, 0, d["cl"], d["sl"]), (True, LC, 16, d["cc"], d["scx"])):
            if isctx and last:
                continue
            nlt = nl // 128
            cv = ctab.rearrange("(k p) n -> p k n", p=128)
            sv = stab.rearrange("(k p) n -> p k n", p=128)
            for lb in range(nl // 256):
                dt_, dk = dft[di % 2]
                di += 1
                self.dma(dt_[:, 0, 0:nlt, :], cv[:, :, lb * 256:(lb + 1) * 256], w=[dk])
                self.dma(dt_[:, 1, 0:nlt, :], sv[:, :, lb * 256:(lb + 1) * 256], w=[dk])
                ct, ck = catF[ci % 2]
                ci += 1
                for ch in range(2):
                    pt, pk = self.ps()
                    nmm = 2 * nlt
                    i = 0
                    for lt in range(nlt):
                        for cs in range(2):
                            self.mm(pt[:, 0:256], Gt[:, tile0 + lt, ch, cs * 128:(cs + 1) * 128], dt_[:, cs, lt, :],
                                    start=(i == 0), stop=(i == nmm - 1), r=["Gt", dk], w=[pk], nosig=(i != nmm - 1))
                            i += 1
                    self.cp("act" if ch == 0 else "dve", ct[:, ch, :], pt[:, 0:256], r=[pk], w=[ck])
                self.out_proj(b, l, last, wo, "wo", 2, lambda j, off, n, ct=ct: ct[:, j, 0:n], [ck],
                              blocks=[(isctx, lb * 256, 256)])

    def qk_post(self, pt, pk, n, gT, l, rope_off, dst, dkey, tmp, rope):
        G = self.G
        sq, sk = tmp["sq"]
        self.act(sq[:, 0:n], pt[:, 0:n], AF.Square, r=[pk], w=[sk])
        p2, p2k = self.ps()
        self.mm(p2[:, 0:n], G["blk1"][:], sq[:, 0:n], r=[sk, "blk1"], w=[p2k])
        rs, rk = tmp["rstd"]
        self.act(rs[:, 0:n], p2[:, 0:n], AF.Ln, r=[p2k], w=[rk], scale=1.0 / 64, bias=self.epsb[:, 0:1])
        self.act(rs[:, 0:n], rs[:, 0:n], AF.Exp, r=[rk], w=[rk], scale=-0.5)
        qn, qk = tmp["qn"]
        if rope_off is None:
            self.stt("dve", dst, pt[:, 0:n], gT[:, l:l + 1], rs[:, 0:n], ALU.mult, ALU.mult, r=[pk, rk], w=[dkey])
            return
        self.stt("dve", qn[:, 0:n], pt[:, 0:n], gT[:, l:l + 1], rs[:, 0:n], ALU.mult, ALU.mult, r=[pk, rk], w=[qk])
        p3, p3k = self.ps()
        self.mm(p3[:, 0:n], G["rotm"][:], qn[:, 0:n], r=[qk, "rotm"], w=[p3k])
        (ct, st), rpk = rope
        t1, t1k = tmp["t1"]
        t2, t2k = tmp["t2"]
        self.tt("pool", t1[:, 0:n], qn[:, 0:n], ct[:, 0:n], ALU.mult, r=[qk, rpk], w=[t1k])
        self.tt("dve", t2[:, 0:n], p3[:, 0:n], st[:, 0:n], ALU.mult, r=[p3k, rpk], w=[t2k])
        self.tt("pool", dst, t1[:, 0:n], t2[:, 0:n], ALU.add, r=[t1k, t2k], w=[dkey])

    def phase_attn(self, b, l, last):
        PB, d, G = self.PB, self.d, self.G
        hT = self.hT
        o = PB + 36864
        self.stg_i = 0
        stg = [(self.sb(f"astg{i}", [128, 2048], F32, o + i * 8192), f"stg{i}") for i in range(1)]
        o += 8192
        wq = self.sb("wq", [128, 8, 256], BF16, o); o += 4096
        wk = self.sb("wk", [128, 8, 128], BF16, o); o += 2048
        wv = self.sb("wv", [128, 8, 64], BF16, o); o += 1024
        wo = self.sb("wo_a", [128, 2, D], BF16, o); o += 4096
        qT = self.sb("qT", [128, 2, L + LC], BF16, o); o += 9216
        kT = self.sb("kT", [128, L + LC], BF16, o); o += 4608
        VA = self.sb("VA", [128, 18, 128], BF16, o); o += 4608
        VB = self.sb("VB", [128, 18, 128], BF16, o); o += 4608
        PT = [(self.sb(f"PT{i}", [128, 18, 256], BF16, o + i * 9216), f"PT{i}") for i in range(2)]
        o += 18432
        tmp = {}
        for i, nm in enumerate(["sq", "rstd", "qn", "t1", "t2"]):
            tmp[nm] = (self.sb(f"a_{nm}", [128, 512], F32, o + i * 2048), f"a_{nm}")
        o += 10240
        ropeb = []
        for i in range(2):
            ropeb.append(((self.sb(f"rc{i}", [128, 512], F32, o + i * 4096), self.sb(f"rs{i}", [128, 512], F32, o + i * 4096 + 2048)),
                          f"rope{i}"))
        o += 8192
        catA = [(self.sb(f"catA{i}", [128, 2, 256], BF16, o + i * 1024), f"catA{i}") for i in range(2)]
        o += 2048
        rsum = [(self.sb(f"rsum{i}", [128, 256], F32, o + i * 1024), f"rsum{i}") for i in range(2)]
        o += 2048
        assert o <= 229376, o
        ri = 0
        for g in range(2):
            self.load_w(wq, "wq", d["w_in"][l], stg, 8, 256, 256 + g * 256)
            for half in range(2):
                srcv = d["w_in"][l].rearrange("(k p) n -> p k n", p=128)
                st, skey = stg[self.stg_i % len(stg)]
                self.stg_i += 1
                sv = st[:, 0:512].rearrange("p (k n) -> p k n", n=64)
                self.dma(sv, srcv[:, :, 768 + g * 64:768 + (g + 1) * 64], w=[skey])
                self.cp("pool", wk[:, :, half * 64:(half + 1) * 64], sv, r=[skey], w=["wk"])
            self.load_w(wv, "wv", d["w_in"][l], stg, 8, 64, 896 + g * 64)
            self.load_w(wo, "wo", d["w_out"][l][256 + g * 256:256 + (g + 1) * 256, :], stg, 2, 1024, 0)
            self.memset("pool", VA[:, :, 64:128], 1.0, w=["VA"])
            self.memset("pool", VB[:, :, 0:64], 1.0, w=["VB"])
            for (isctx, t0, n) in self.TB:
                off = self.hoff(isctx, t0)
                hk = f"hT{off // 512}"
                rope = None
                if not isctx:
                    rope = ropeb[ri % 2]
                    ri += 1
                    self.dma(rope[0][0][:, 0:n], d["ropec"][:, t0:t0 + n], w=[rope[1]])
                    self.dma(rope[0][1][:, 0:n], d["ropes"][:, t0:t0 + n], w=[rope[1]])
                pt, pk = self.ps()
                for k in range(8):
                    self.mm(pt[:, 0:n], wk[:, k, :], hT[:, k, off:off + n], start=(k == 0), stop=(k == 7),
                            r=["wk", hk], w=[pk], nosig=(k != 7))
                self.qk_post(pt, pk, n, G["kgT"], l, None if isctx else t0, kT[:, off:off + n], "kT", tmp, rope)
                if not (isctx and last):
                    for qc in range(2):
                        pt, pk = self.ps()
                        for k in range(8):
                            self.mm(pt[:, 0:n], wq[:, k, qc * 128:(qc + 1) * 128], hT[:, k, off:off + n], start=(k == 0),
                                    stop=(k == 7), r=["wq", hk], w=[pk], nosig=(k != 7))
                        self.qk_post(pt, pk, n, G["qgT"], l, None if isctx else t0, qT[:, qc, off:off + n], "qT", tmp, rope)
                for j in range(n // 128):
                    tile = off // 128 + j
                    pt, pk = self.ps()
                    for k in range(8):
                        self.mm(pt[:, 0:64], hT[:, k, off + j * 128:off + (j + 1) * 128], wv[:, k, :], start=(k == 0),
                                stop=(k == 7), r=["wv", hk], w=[pk], nosig=(k != 7))
                    self.cp("act", VA[:, tile, 0:64], pt[:, 0:64], r=[pk], w=["VA"])
                    self.cp("dve", VB[:, tile, 64:128], pt[:, 0:64], r=[pk], w=["VB"])
            qblocks = [(False, q0) for q0 in range(0, L, 256)]
            if not last:
                qblocks.append((True, 0))
            pi = 0
            ci = 0
            for (isctx, q0) in qblocks:
                qoff = self.hoff(isctx, q0)
                ktiles = list(range(16, 18)) if isctx else list(range(18))
                ct, ck = catA[ci % 2]
                ci += 1
                pend = None
                heads = list(range(4))
                for hh in heads + [None]:
                    if hh is not None:
                        qc, hl = hh // 2, hh % 2
                        P, Pk = PT[pi % 2]
                        pi += 1
                        pr = slice(hl * 64, (hl + 1) * 64)
                        for ii in range(0, len(ktiles), 2):
                            pt, pk = self.ps()
                            kk = ktiles[ii:ii + 2]
                            for jj, kt in enumerate(kk):
                                self.mm(pt[:, jj * 256:(jj + 1) * 256], kT[pr, kt * 128:(kt + 1) * 128],
                                        qT[pr, qc, qoff:qoff + 256], r=["kT", "qT"], w=[pk], nosig=(jj != len(kk) - 1))
                            self.act(P[:, kk[0]:kk[0] + len(kk), :].rearrange("p a n -> p (a n)"), pt[:, 0:256 * len(kk)],
                                     AF.Exp, r=[pk], w=[Pk], scale=0.125)
                    if pend is not None:
                        (phh, pP, pPk) = pend
                        pqc, phl = phh // 2, phh % 2
                        Vt, Vk = (VA, "VA") if phl == 0 else (VB, "VB")
                        pt, pk = self.ps()
                        for ii, kt in enumerate(ktiles):
                            self.mm(pt[:, 0:256], Vt[:, kt, :], pP[:, kt, :], start=(ii == 0), stop=(ii == len(ktiles) - 1),
                                    r=[Vk, pPk], w=[pk], nosig=(ii != len(ktiles) - 1))
                        orow = slice(phl * 64, (phl + 1) * 64)
                        srow = slice((1 - phl) * 64, (2 - phl) * 64)
                        rs_, rsk = rsum[phh % 2]
                        self.cp("act", rs_[orow, :], pt[srow, 0:256], r=[pk], w=[rsk])
                        self.S.op("dve", (lambda e, a=rs_[orow, :]: e.reciprocal(out=a, in_=a)), reads=[rsk], writes=[rsk])
                        self.tt("dve", ct[orow, pqc, :], pt[orow, 0:256], rs_[orow, :], ALU.mult, r=[pk, rsk], w=[ck])
                    pend = (hh, P, Pk) if hh is not None else None
                self.out_proj(b, l, last, wo, "wo", 2, lambda j, off, n, ct=ct: ct[:, j, 0:n], [ck],
                              blocks=[(isctx, q0, 256)])

    def phase_gdn(self, b, l, last):
        PB, d, G = self.PB, self.d, self.G
        END = 229376 - 2048
        NT = L + LC
        catD = self.sb("catD", [128, 2, NT], BF16, END - 9216)
        self.catD = catD
        gcon = G["gcon"]
        tri4, minc4, mincT4, id4, ctri4 = (gcon[:, i, :, :] for i in range(5))
        ident, ones64, negones = G["ident"], G["onesf"][0:64, 0:64], G["negones"]
        for hp in range(2):
            o = PB
            qkv = self.sb("qkv", [128, 3, NT], F32, o); o += 27648
            dzs = self.sb("dzs", [128, NT], BF16, o); o += 4608
            G64 = self.sb("G64", [64, 36, 16], F32, o); o += 2304
            BETA = self.sb("BETA", [64, 36, 8], F32, o); o += 1152
            GG = self.sb("GG", [64, 36, 8], F32, o); o += 1152
            OV = o
            self.stg_i = 0
            stg = [(self.sb("gstg", [128, 2048], F32, o), "stg0")]; o += 8192
            win = self.sb("win", [128, 8, 528], BF16, o); o += 8448
            hblk = self.sb("hblk", [128, 8, 450], BF16, o); o += 7232
            tmp = self.norm_tmp(o); o += 12288
            raw = self.sb("raw", [128, 3, 450], F32, o); o += 5632
            cvt = self.sb("cvt", [128, 3, 448], F32, o); o += 5632
            assert o <= END - 9216
            for j, c0 in enumerate((1024, 1280, 1536, 1792)):
                self.load_w(win[:, :, j * 128:(j + 1) * 128], "win", d["w_in"][l], stg, 8, 128, c0 + hp * 128)
            self.load_w(win[:, :, 512:528], "win", d["w_in"][l], stg, 8, 16, 2048)
            blocks = [(True, 0, 256)] + [(False, s0, min(448, L - s0)) for s0 in range(0, L, 448)]
            for (isctx, s0, m) in blocks:
                Ls = LC if isctx else L
                lo, hi = max(s0 - 1, 0), min(s0 + m + 1, Ls)
                ncol = hi - lo
                c_lo = lo - (s0 - 1)
                g0 = (0 if isctx else LC) + s0
                self.norm_block(b, l, 1, isctx, lo, ncol, hblk[:, :, 0:ncol], "hblk", tmp)
                if s0 == 0:
                    self.memset("pool", raw[:, :, 0:1], 0.0, w=["raw"])
                if s0 + m == Ls:
                    self.memset("pool", raw[:, :, m + 1:m + 2], 0.0, w=["raw"])
                for j in range(3):
                    pt, pk = self.ps()
                    for k in range(8):
                        self.mm(pt[:, 0:ncol], win[:, k, j * 128:(j + 1) * 128], hblk[:, k, 0:ncol], start=(k == 0),
                                stop=(k == 7), r=["win", "hblk"], w=[pk], nosig=(k != 7))
                    self.cp("act" if j != 1 else "dve", raw[:, j, c_lo:c_lo + ncol], pt[:, 0:ncol], r=[pk], w=["raw"])
                    cw = G["convT"][:, l, j * 2 + hp, :]
                    eng = "dve" if j != 1 else "pool"
                    self.ts(eng, cvt[:, j, 0:m], raw[:, j, 1:1 + m], cw[:, 1:2], ALU.mult, r=["raw", "convT"], w=[f"cvt{j}"])
                    self.stt(eng, cvt[:, j, 0:m], raw[:, j, 0:m], cw[:, 0:1], cvt[:, j, 0:m], ALU.mult, ALU.add,
                             r=["raw", "convT", f"cvt{j}"], w=[f"cvt{j}"])
                    self.stt(eng, cvt[:, j, 0:m], raw[:, j, 2:2 + m], cw[:, 2:3], cvt[:, j, 0:m], ALU.mult, ALU.add,
                             r=["raw", "convT", f"cvt{j}"], w=[f"cvt{j}"])
                    self.act(qkv[:, j, g0:g0 + m], cvt[:, j, 0:m], AF.Silu, r=[f"cvt{j}"], w=["qkv"])
                pt, pk = self.ps()
                for k in range(8):
                    self.mm(pt[:, 0:ncol], win[:, k, 384:512], hblk[:, k, 0:ncol], start=(k == 0), stop=(k == 7),
                            r=["win", "hblk"], w=[pk], nosig=(k != 7))
                cs = s0 - lo
                self.act(dzs[:, g0:g0 + m], pt[:, cs:cs + m], AF.Silu, r=[pk], w=["dzs"])
                pt, pk = self.ps()
                nch = m // 64
                for ci in range(nch):
                    for k in range(8):
                        self.mm(pt[0:64, ci * 16:(ci + 1) * 16], hblk[:, k, cs + ci * 64:cs + (ci + 1) * 64], win[:, k, 512:528],
                                start=(k == 0), stop=(k == 7), r=["win", "hblk"], w=[pk], nosig=not (k == 7 and ci == nch - 1))
                self.cp("dve", G64[:, g0 // 64:g0 // 64 + nch, :].rearrange("p a n -> p (a n)"), pt[0:64, 0:nch * 16],
                        r=[pk], w=["G64"])
            gstage = self.cfg.get("gstage", 0)
            if gstage == 1:
                self.barrier()
                continue
            for j in range(2 if gstage != 22 else 0):
                for t0 in range(0, NT, 512):
                    n = min(512, NT - t0)
                    sq, sk = tmp["sq"][(t0 // 512) % 3]
                    self.act(sq[:, 0:n], qkv[:, j, t0:t0 + n], AF.Square, r=["qkv"], w=[sk])
                    pt, pk = self.ps()
                    self.mm(pt[:, 0:n], G["blk1"][:], sq[:, 0:n], r=[sk, "blk1"], w=[pk])
                    rs, rk = tmp["rstd"]
                    self.act(rs[:, 0:n], pt[:, 0:n], AF.Ln, r=[pk], w=[rk], bias=self.epsb[:, 0:1])
                    self.act(rs[:, 0:n], rs[:, 0:n], AF.Exp, r=[rk], w=[rk], scale=-0.5)
                    if j == 0:
                        self.stt("dve", qkv[:, j, t0:t0 + n], qkv[:, j, t0:t0 + n], 0.125, rs[:, 0:n], ALU.mult, ALU.mult,
                                 r=["qkv", rk], w=["qkv"])
                    else:
                        self.tt("dve", qkv[:, j, t0:t0 + n], qkv[:, j, t0:t0 + n], rs[:, 0:n], ALU.mult, r=["qkv", rk], w=["qkv"])
            if gstage == 21:
                self.barrier()
                continue
            self.act(BETA[:], G64[:, :, 0:8], AF.Sigmoid, r=["G64"], w=["BETA"])
            self.tt("dve", GG[:], G64[:, :, 8:16], G["dtbb"][:, l, :].unsqueeze(1).broadcast_to([64, 36, 8]), ALU.add,
                    r=["G64", "dtbb"], w=["GG"])
            self.act(GG[:], GG[:], AF.Exp, r=["GG"], w=["GG"])
            self.act(GG[:], GG[:], AF.Ln, r=["GG"], w=["GG"], bias=1.0)
            self.tt("dve", GG[:], GG[:], G["nexpa"][:, l, :].unsqueeze(1).broadcast_to([64, 36, 8]), ALU.mult,
                    r=["GG", "nexpa"], w=["GG"])
            self.barrier()
            if gstage in (2, 22):
                continue
            o = OV

            def m4(name, shape=(64, 4, 64)):
                nonlocal o
                t = self.sb(name, list(shape), F32, o)
                o += (int(np.prod(shape[1:])) * 4 + 31) // 32 * 32
                return t
            obuf = m4("obuf", (64, 36, 128))
            Sst = m4("Sst")
            lane_base = o
            NLANE = self.cfg.get("lanes", 3)
            lanes = []
            for li in range(NLANE):
                Lb = {"i": li}
                Lb["tok"] = m4(f"tok{li}", (64, 4, 3, 64))
                for nm in ("Gd", "tD1", "tD2", "dgc", "Idec", "QKmT"):
                    Lb[nm] = m4(f"{nm}{li}")
                Lb["AT"], Lb["Bm"], Lb["QpT"] = Lb["tD1"], Lb["tD2"], Lb["dgc"]
                Lb["XR"] = [m4(f"XR{li}_{i}", (64, 4, 192)) for i in range(2)]
                Lb["Yb"] = [m4(f"Yb{li}_{i}", (64, 4, 64)) for i in range(2)]
                Lb["ex12"] = m4(f"ex12{li}", (64, 12))
                for nm in ("b4", "nbe", "g4"):
                    Lb[nm] = m4(f"{nm}{li}", (64, 4))
                lanes.append(Lb)
            assert o <= END - 9216, o
            self.memset("pool", Sst[:], 0.0, w=["Sst"])

            def bc(ap4):
                return ap4.unsqueeze(2).broadcast_to([64, 4, 64])

            def chunks_of(s):
                nf = s
                nb_ = (3 - s) if s < 4 else (39 - s)
                return (nf, nb_)

            def pre(s, Lb):
                li = Lb["i"]

                def K(n):
                    return f"{n}_{li}"
                tok, Gd, tD1, tD2, dgc, Idec = Lb["tok"], Lb["Gd"], Lb["tD1"], Lb["tD2"], Lb["dgc"], Lb["Idec"]
                AT, Bm, QKmT, QpT, XR, Yb = Lb["AT"], Lb["Bm"], Lb["QKmT"], Lb["QpT"], Lb["XR"], Lb["Yb"]
                Rf = XR[0]
                ex12, b4, nbe, g4 = Lb["ex12"], Lb["b4"], Lb["nbe"], Lb["g4"]
                kdec = Gd
                cnt = [0]

                def lps():
                    i = 2 * li + (cnt[0] % 2)
                    cnt[0] += 1
                    return self.psum[i], f"ps{i}"
                nchd = chunks_of(s)
                gcol = (2 * hp, 4 + 2 * hp)
                b4v = b4[:].rearrange("p (h d) -> p d h", d=2)
                g4v = g4[:].rearrange("p (h d) -> p d h", d=2)
                for dd in range(2):
                    n_ = nchd[dd]
                    self.cp("pool", b4v[:, dd, :], BETA[:, n_, gcol[dd]:gcol[dd] + 2], r=["BETA"], w=[K("b4")])
                    self.cp("pool", g4v[:, dd, :], GG[:, n_, gcol[dd]:gcol[dd] + 2], r=["GG"], w=[K("g4")])
                for hl in range(2):
                    pt, pk = lps()
                    pr = slice(hl * 64, (hl + 1) * 64)
                    for dd in range(2):
                        t0 = nchd[dd] * 64
                        for j in range(3):
                            self.mm(pt[0:64, (dd * 3 + j) * 64:(dd * 3 + j + 1) * 64], qkv[pr, j, t0:t0 + 64], ident[pr, pr],
                                    r=["qkv", "ident"], w=[pk], nosig=not (dd == 1 and j == 2))
                    self.cp("act" if hl == 0 else "dve", tok[:, 2 * hl:2 * hl + 2, :, :].rearrange("p a j n -> p (a j n)"),
                            pt[0:64, 0:384], r=[pk], w=[K("tok")])
                yield
                self.tt("pool", Gd[:], tri4, bc(g4[:]), ALU.mult, r=["gcon", K("g4")], w=[K("Gd")])
                yield
                pD, pDk = lps()
                self.mm(pD[0:64, 0:256], negones[:], Gd[:].rearrange("p a n -> p (a n)"), start=True, stop=False, r=[K("Gd"), "negones"], w=[pDk], nosig=True)
                for p in range(4):
                    self.mm(pD[0:64, p * 64:(p + 1) * 64], Gd[:, p, :], ones64, start=False, stop=True, r=[K("Gd"), "onesf"], w=[pDk], nosig=True)
                for p in range(4):
                    gsl = g4[:, p:p + 1]
                    self.mm(pD[0:64, 256 + p:257 + p], tri4[:, p, :], gsl, r=["gcon", K("g4")], w=[pDk], nosig=True)
                    self.mm(pD[0:64, 260 + p:261 + p], ctri4[:, p, :], gsl, r=["gcon", K("g4")], w=[pDk], nosig=True)
                self.mm(pD[0:64, 264:268], ones64, g4[:, 0:4], r=["onesf", K("g4")], w=[pDk])
                yield
                pDv = pD[0:64, 0:256].rearrange("p (a n) -> p a n", n=64)
                self.ts("dve", tD1[:], pDv, 0.0, ALU.min, r=[pDk], w=[K("tD1")])
                self.ts("dve", tD2[:], pDv, -1.0, ALU.mult, 0.0, ALU.min, r=[pDk], w=[K("tD2")])
                self.act(ex12[:], pD[0:64, 256:268], AF.Exp, r=[pDk], w=[K("ex12")])
                yield
                self.act(tD1[:], tD1[:], AF.Exp, r=[K("tD1")], w=[K("tD1")])
                self.act(tD2[:], tD2[:], AF.Exp, r=[K("tD2")], w=[K("tD2")])
                self.stt("dve", nbe[:], b4[:], -1.0, ex12[:, 0:4], ALU.mult, ALU.mult, r=[K("b4"), K("ex12")], w=[K("nbe")])
                yield
                self.tt("pool", tD1[:], tD1[:], minc4, ALU.mult, r=[K("tD1"), "gcon"], w=[K("tD1")])
                self.tt("pool", tD2[:], tD2[:], mincT4, ALU.mult, r=[K("tD2"), "gcon"], w=[K("tD2")])
                self.tt("pool", Rf[:, :, 64:128], tok[:, :, 2, :], bc(b4[:]), ALU.mult, r=[K("tok"), K("b4")], w=[K("XR0")])
                yield
                self.tt("pool", tD1[:], tD1[:], id4, ALU.subtract, r=[K("tD1"), "gcon"], w=[K("tD1")])
                self.tt("pool", Rf[:, :, 128:192], tok[:, :, 1, :], bc(nbe[:]), ALU.mult, r=[K("tok"), K("nbe")], w=[K("XR0")])
                yield
                X0, Y0 = XR[0][:, :, 0:64], Yb[0][:, :, :]
                for hl in range(2):
                    pK, pKk = lps()
                    pr = slice(hl * 64, (hl + 1) * 64)
                    for dd in range(2):
                        t0 = nchd[dd] * 64
                        self.mm(pK[0:64, (2 * dd) * 64:(2 * dd + 1) * 64], qkv[pr, 1, t0:t0 + 64], qkv[pr, 1, t0:t0 + 64], r=["qkv"], w=[pKk], nosig=True)
                        self.mm(pK[0:64, (2 * dd + 1) * 64:(2 * dd + 2) * 64], qkv[pr, 1, t0:t0 + 64], qkv[pr, 0, t0:t0 + 64], r=["qkv"], w=[pKk],
                                nosig=(dd != 1))
                    pKv = pK[0:64, 0:256].rearrange("p (a t n) -> p a t n", t=2, n=64)
                    ps_ = slice(2 * hl, 2 * hl + 2)
                    self.tt("dve", XR[0][:, ps_, 0:64], pKv[:, :, 0, :], tD1[:, ps_, :], ALU.mult, r=[pKk, K("tD1")], w=[K("XR0")])
                    self.tt("dve", QKmT[:, ps_, :], pKv[:, :, 1, :], tD2[:, ps_, :], ALU.mult, r=[pKk, K("tD2")], w=[K("QKmT")])
                self.tt("pool", kdec[:], tok[:, :, 1, :], bc(ex12[:, 4:8]), ALU.mult, r=[K("tok"), K("ex12")], w=[K("Gd")])
                yield
                self.stt("dve", X0, X0, -1.0, bc(b4[:]), ALU.mult, ALU.mult, r=[K("XR0"), K("b4")], w=[K("XR0")])
                self.tt("pool", dgc[:], id4, bc(ex12[:, 0:4]), ALU.mult, r=["gcon", K("ex12")], w=[K("dgc")])
                self.tt("pool", Idec[:], id4, bc(ex12[:, 8:12]), ALU.mult, r=["gcon", K("ex12")], w=[K("Idec")])
                yield
                pY, pYk = lps()
                for p in range(4):
                    self.mm(pY[0:64, p * 64:(p + 1) * 64], XR[0][:, p, 0:64], ident[0:64, 0:64], r=[K("XR0"), "ident"], w=[pYk], nosig=(p != 3))
                yield
                self.cp("act", Y0, pY[0:64, 0:256].rearrange("p (a n) -> p a n", n=64), r=[pYk], w=[K("Yb0")])
                yield
                for kk in range(6):
                    c_, n_i = kk % 2, (kk + 1) % 2
                    cX, cY, nX, nY = XR[c_], Yb[c_], XR[n_i], Yb[n_i]
                    cXk, cYk, nXk, nYk = K(f"XR{c_}"), K(f"Yb{c_}"), K(f"XR{n_i}"), K(f"Yb{n_i}")
                    if kk < 5:
                        banks = []
                        for pair in range(2):
                            pB, pBk = lps()
                            for q_ in range(2):
                                p = 2 * pair + q_
                                self.mm(pB[0:64, q_ * 192:(q_ + 1) * 192], cY[:, p, :], cX[:, p, :], r=[cYk, cXk], w=[pBk], nosig=True)
                                self.mm(pB[0:64, 384 + q_ * 64:384 + (q_ + 1) * 64], cX[:, p, 0:64], cY[:, p, :], r=[cYk, cXk], w=[pBk], nosig=(q_ != 1))
                            banks.append((pB, pBk))
                        yield
                        for pair in range(2):
                            pB, pBk = banks[pair]
                            pp = slice(2 * pair, 2 * pair + 2)
                            pv = pB[0:64, 0:384].rearrange("p (a n) -> p a n", n=192)
                            self.cp("act", nX[:, pp, 0:64], pv[:, :, 0:64], r=[pBk], w=[nXk])
                            self.tt("dve", nX[:, pp, 64:192], pv[:, :, 64:192], cX[:, pp, 64:192], ALU.add, r=[pBk, cXk], w=[nXk])
                            self.cp("act", nY[:, pp, :], pB[0:64, 384:512].rearrange("p (a n) -> p a n", n=64), r=[pBk], w=[nYk])
                        yield
                    else:
                        pR, pRk = lps()
                        for p in range(4):
                            self.mm(pR[0:64, p * 128:(p + 1) * 128], cY[:, p, :], cX[:, p, 64:192], r=[cYk, cXk], w=[pRk], nosig=(p != 3))
                        yield
                        self.tt("dve", nX[:, :, 64:192], pR[0:64, :].rearrange("p (a n) -> p a n", n=128), cX[:, :, 64:192], ALU.add,
                                r=[pRk, cXk], w=[nXk])
                        yield
                pA, pAk = lps()
                for p in range(4):
                    self.mm(pA[0:64, p * 64:(p + 1) * 64], Rf[:, p, 128:192], kdec[:, p, :], r=[K("XR0"), K("Gd")], w=[pAk], nosig=True)
                    self.mm(pA[0:64, (4 + p) * 64:(5 + p) * 64], kdec[:, p, :], Rf[:, p, 64:128], r=[K("XR0"), K("Gd")], w=[pAk], nosig=(p != 3))
                pQ, pQk = lps()
                for p in range(4):
                    self.mm(pQ[0:64, p * 64:(p + 1) * 64], tok[:, p, 0, :], dgc[:, p, :], start=True, stop=False, r=[K("tok"), K("dgc")], w=[pQk], nosig=True)
                    self.mm(pQ[0:64, p * 64:(p + 1) * 64], Rf[:, p, 128:192], QKmT[:, p, :], start=False, stop=True, r=[K("XR0"), K("QKmT")], w=[pQk],
                            nosig=(p != 3))
                yield
                pAv = pA[0:64, :].rearrange("p (t a n) -> p t a n", t=2, n=64)
                self.tt("dve", AT[:], pAv[:, 0, :, :], Idec[:], ALU.add, r=[pAk, K("Idec")], w=[K("tD1")])
                self.cp("act", QpT[:], pQ[0:64, 0:256].rearrange("p (a n) -> p a n", n=64), r=[pQk], w=[K("dgc")])
                yield
                self.cp("act", Bm[:], pAv[:, 1, :, :], r=[pAk], w=[K("tD2")])
                yield

            def scan(s, Lb):
                li = Lb["i"]

                def K(n):
                    return f"{n}_{li}"
                AT, Bm, QKmT, QpT, Rf = Lb["AT"], Lb["Bm"], Lb["QKmT"], Lb["QpT"], Lb["XR"][0]
                nchd = chunks_of(s)
                pO, pOk = self.psum[6], "ps6"
                for p in range(4):
                    self.mm(pO[0:64, p * 64:(p + 1) * 64], QpT[:, p, :], Sst[:, p, :], start=True, stop=False, r=[K("dgc"), "Sst"], w=[pOk], nosig=True)
                    self.mm(pO[0:64, p * 64:(p + 1) * 64], QKmT[:, p, :], Rf[:, p, 64:128], start=False, stop=True, r=[K("QKmT"), K("XR0")], w=[pOk],
                            nosig=(p != 3))
                pN, pNk = self.psum[7], "ps7"
                for p in range(4):
                    self.mm(pN[0:64, p * 64:(p + 1) * 64], AT[:, p, :], Sst[:, p, :], r=[K("tD1"), "Sst"], w=[pNk], nosig=(p != 3))
                self.tt("dve", Sst[:], pN[0:64, 0:256].rearrange("p (a n) -> p a n", n=64), Bm[:], ALU.add, r=[pNk, K("tD2")], w=["Sst"])
                pOv = pO[0:64, 0:256].rearrange("p (h d n) -> p d h n", d=2, n=64)
                for dd in range(2):
                    n_ = nchd[dd]
                    sb_other = (3 - n_) if n_ < 4 else (39 - n_)
                    ostep = sb_other if dd == 0 else n_
                    first = s < ostep
                    src = pOv[:, dd, :, :]
                    dst = obuf[:, n_, :].rearrange("p (h n) -> p h n", n=64)
                    if first:
                        self.cp("act", dst, src, r=[pOk], w=["obuf"])
                    else:
                        self.tt("dve", dst, src, dst, ALU.add, r=[pOk, "obuf"], w=["obuf"])

            nsteps = self.cfg.get("gsteps", 36)
            NSEG = 30
            stagger = self.cfg.get("stagger", NSEG // NLANE)
            gens = [None] * NLANE
            step_of = [None] * NLANE
            start_tick = [i * stagger for i in range(NLANE)]
            next_step = 0
            done = 0
            tick = 0
            while done < nsteps:
                for li in range(NLANE):
                    if gens[li] is None:
                        if next_step < nsteps and tick >= start_tick[li]:
                            gens[li] = pre(next_step, lanes[li])
                            step_of[li] = next_step
                            next_step += 1
                        else:
                            continue
                    try:
                        next(gens[li])
                    except StopIteration:
                        scan(step_of[li], lanes[li])
                        gens[li] = None
                        done += 1
                tick += 1
            o = lane_base
            self.barrier()
            if gstage == 3:
                continue
            o2 = o
            osq = self.sb("osq", [64, 36 * 128], F32, o2); o2 += 36 * 128 * 4
            oss = self.sb("oss", [64, 72], F32, o2); o2 += 288
            assert o2 <= END - 9216, o2
            ov = obuf[:].rearrange("p a n -> p (a n)")
            self.tt("pool", osq[:], ov, ov, ALU.mult, r=["obuf"], w=["osq"])
            self.S.op("dve", (lambda e: e.tensor_reduce(out=oss[:], in_=osq[:].rearrange("p (a n) -> p a n", n=64), axis=AX.X, op=ALU.add)),
                      reads=["osq"], writes=["oss"])
            self.act(oss[:], oss[:], AF.Ln, r=["oss"], w=["oss"], scale=1.0 / 64, bias=self.epsb[0:64, 0:1])
            self.act(oss[:], oss[:], AF.Exp, r=["oss"], w=["oss"], scale=-0.5)
            self.tt("dve", obuf[:].rearrange("p a (h n) -> p (a h) n", n=64), obuf[:].rearrange("p a (h n) -> p (a h) n", n=64),
                    oss[:].unsqueeze(2).broadcast_to([64, 72, 64]), ALU.mult, r=["obuf", "oss"], w=["obuf"])
            if gstage == 4:
                self.barrier()
                continue
            for c8 in range(0, 36, 8):
                nchk = min(8, 36 - c8)
                if last and c8 + nchk <= 4:
                    continue
                pt, pk = self.ps()
                for ci in range(nchk):
                    self.mm(pt[:, ci * 64:(ci + 1) * 64], obuf[:, c8 + ci, :], ident[0:64, 0:64], r=["obuf", "ident"], w=[pk], nosig=(ci != nchk - 1))
                segs = []
                g0, g1 = c8 * 64, (c8 + nchk) * 64
                if g0 < LC:
                    segs.append((g0, min(g1, LC), L + g0))
                if g1 > LC:
                    a = max(g0, LC)
                    segs.append((a, g1, a - LC))
                for (a, e_, fo) in segs:
                    if gstage in (5, 6):
                        self.ts("dve", catD[:, hp, fo:fo + (e_ - a)], pt[:, a - g0:e_ - g0], G["ogT"][:, l:l + 1], ALU.mult, r=[pk, "ogT"], w=["catD"])
                        continue
                    self.stt("dve", catD[:, hp, fo:fo + (e_ - a)], pt[:, a - g0:e_ - g0], G["ogT"][:, l:l + 1], dzs[:, a:e_], ALU.mult, ALU.mult,
                             r=[pk, "ogT", "dzs"], w=["catD"])
            self.barrier()


def _prep_inputs(inp, core):
    cst = _consts()
    b0 = 2 * core
    f = lambda a: np.ascontiguousarray(a, dtype=np.float32)
    m = {}
    m["x"] = f(inp["x"][b0:b0 + 2])
    m["ctx"] = f(inp["ctx"][b0:b0 + 2])
    rows = np.stack([inp["c"][b0], inp["c"][b0 + 1], inp["c_ctx"]], axis=0)
    m["cT"] = f(rows.reshape(3, 8, 128).transpose(2, 1, 0))
    m["w_mod"] = f(inp["w_mod"])
    m["b_modT"] = f(inp["b_mod"].reshape(2, 48, 128).transpose(2, 0, 1))
    m["n1T"] = f(inp["norm1_g"].reshape(2, 8, 128).transpose(2, 0, 1))
    m["n2T"] = f(inp["norm2_g"].reshape(2, 8, 128).transpose(2, 0, 1))
    m["w_in"] = f(inp["w_in"])
    m["convT"] = f(inp["conv_w"].reshape(2, 3, 6, 128).transpose(3, 0, 2, 1))
    m["qgT"] = f(np.tile(inp["q_norm_g"], (1, 2)).T)
    m["kgT"] = f(np.tile(inp["k_norm_g"], (1, 2)).T)
    m["ogT"] = f(np.tile(inp["o_norm_g"], (1, 2)).T)
    m["alog_b"] = f(np.broadcast_to(inp["a_log"].reshape(1, 2, 8), (64, 2, 8)))
    m["dtb_b"] = f(np.broadcast_to(inp["dt_bias"].reshape(1, 2, 8), (64, 2, 8)))
    m["w_out"] = f(inp["w_out"])
    m["w_ff1"] = f(inp["w_ff1"])
    m["w_ff2"] = f(inp["w_ff2"])
    for k in ["ident", "blk1", "rotm", "chd", "cl", "sl", "cc", "scx", "ropec", "ropes", "gcon"]:
        m[k] = cst[k]
    return m


_PROG = {}


def kernel(**inputs):
    inp = {k: np.asarray(v) for k, v in inputs.items()}
    cfg = dict(_CFG)
    key = repr(sorted(cfg.items()))
    if key not in _PROG:
        _PROG[key] = Prog(cfg).build()
    nc = _PROG[key]
    in_maps = [_prep_inputs(inp, c) for c in range(NCORES)]
    res = run_bass_kernel_spmd(nc, in_maps, core_ids=list(range(NCORES)))
    out = np.concatenate([np.asarray(r["out"]) for r in res.results], axis=0)
    return out.astype(np.float32)


_CFG = {"nb": 2, "nl": 2, "fam": "GFAM"}
```

```python
import bisect
import numpy as np
import ml_dtypes
import concourse.bass as bass
import concourse.mybir as mybir
from concourse.bass_utils import run_bass_kernel_spmd

F32 = mybir.dt.float32
BF16 = mybir.dt.bfloat16
AF = mybir.ActivationFunctionType
ALU = mybir.AluOpType
AX = mybir.AxisListType

EPOCH = 20000
NDS = 12
EPS = 1e-6
D = 1024
L = 2048
LC = 256
NCORES = 8


class _Op:
    __slots__ = ("waits", "fn", "sig", "dma")

    def __init__(self, fn):
        self.waits = []
        self.fn = fn
        self.sig = None
        self.dma = None


class Sched:
    CE = ("pe", "act", "dve", "pool")
    ALL = ("pe", "act", "dve", "pool", "sp")

    def __init__(self, nc):
        self.nc = nc
        self.ops = {e: [] for e in self.ALL}
        self.sems = {e: [] for e in self.CE}
        self.nsig = {e: 0 for e in self.CE}
        self.sig_idx = {e: [] for e in self.CE}
        self.waited = {e: {} for e in self.ALL}
        self.last_w = {}
        self.readers = {}
        self.dsem = {}
        self.dval = {}
        self.dnext = {}

    def _csem(self, e, k):
        while len(self.sems[e]) <= k:
            self.sems[e].append(self.nc.alloc_semaphore(name=f"s_{e}_{len(self.sems[e])}"))
        return self.sems[e][k]

    def _resolve(self, tok):
        if tok[0] == "d":
            return tok[1], tok[2]
        _, e, idx = tok
        lst = self.sig_idx[e]
        p = bisect.bisect_left(lst, idx)
        if p < len(lst):
            return self.ops[e][lst[p]].sig
        op = self.ops[e][idx]
        n = self.nsig[e]
        self.nsig[e] = n + 1
        op.sig = (self._csem(e, n // EPOCH), (n % EPOCH) + 1)
        lst.append(idx)
        return op.sig

    def _add_wait(self, eng, op, tok):
        if tok is None:
            return
        if tok[0] == "c" and tok[1] == eng and eng == "pe":
            return
        sem, val = self._resolve(tok)
        w = self.waited[eng]
        key = id(sem)
        if w.get(key, 0) >= val:
            return
        w[key] = val
        op.waits.append((sem, val))

    def _deps(self, eng, op, reads, writes):
        toks = []
        for k in reads:
            toks.append(self.last_w.get(k))
        for k in writes:
            toks.append(self.last_w.get(k))
            for r in self.readers.get(k, ()):
                if r[0] == "c" and r[1] == eng:
                    continue
                toks.append(r)
        for t in toks:
            self._add_wait(eng, op, t)

    def op(self, eng, fn, reads=(), writes=(), nosig=False):
        if eng != "pe":
            xs = [k for k in reads if k.startswith("ps") and k not in writes]
            if xs:
                writes = list(writes) + xs
        o = _Op(fn)
        self._deps(eng, o, reads, writes)
        idx = len(self.ops[eng])
        self.ops[eng].append(o)
        if not nosig:
            n = self.nsig[eng]
            self.nsig[eng] = n + 1
            o.sig = (self._csem(eng, n // EPOCH), (n % EPOCH) + 1)
            self.sig_idx[eng].append(idx)
        tok = ("c", eng, idx)
        for k in reads:
            self.readers.setdefault(k, []).append(tok)
        for k in writes:
            self.last_w[k] = tok
            self.readers[k] = []
        return o

    def dma(self, fn, reads=(), writes=(), eng="sp"):
        o = _Op(fn)
        if eng not in self.dsem:
            self.dsem[eng] = [self.nc.alloc_semaphore(name=f"d_{eng}_{i}") for i in range(NDS)]
            self.dval[eng] = [0] * NDS
            self.dnext[eng] = 0
        k = self.dnext[eng]
        self.dnext[eng] = (k + 1) % NDS
        sem = self.dsem[eng][k]
        if self.dval[eng][k] > 0:
            self._add_wait(eng, o, ("d", sem, self.dval[eng][k]))
        self._deps(eng, o, reads, writes)
        self.dval[eng][k] += 16
        o.dma = sem
        self.ops[eng].append(o)
        tok = ("d", sem, self.dval[eng][k])
        for kk in reads:
            self.readers.setdefault(kk, []).append(tok)
        for kk in writes:
            self.last_w[kk] = tok
            self.readers[kk] = []
        return o

    def wait_keys(self, eng, keys):
        o = _Op(None)
        for k in keys:
            self._add_wait(eng, o, self.last_w.get(k))
        self.ops[eng].append(o)

    def finish(self):
        o = _Op(None)
        for eng in self.dsem:
            for k, sem in enumerate(self.dsem[eng]):
                if self.dval[eng][k] > 0:
                    o.waits.append((sem, self.dval[eng][k]))
        self.ops["sp"].append(o)

    def emit(self):
        nc = self.nc
        sched = self
        allsems = [s for e in self.CE for s in self.sems[e]] + [s for e in self.dsem for s in self.dsem[e]]

        with nc.Block() as block0:
            @block0.gpsimd
            def _(e):
                for s in allsems:
                    e.sem_clear(s)

        def run(name, e):
            for o in sched.ops[name]:
                for sem, val in o.waits:
                    e.wait_ge(sem, val)
                if o.fn is None:
                    continue
                ins = o.fn(e)
                if o.sig is not None:
                    ins.then_inc(o.sig[0], 1)
                if o.dma is not None:
                    ins.then_inc(o.dma, 16)

        with nc.Block() as block:
            @block.tensor
            def _(e):
                run("pe", e)

            @block.scalar
            def _(e):
                run("act", e)

            @block.vector
            def _(e):
                run("dve", e)

            @block.gpsimd
            def _(e):
                run("pool", e)

            @block.sync
            def _(e):
                run("sp", e)

    def stats(self):
        return {e: (len(self.ops[e]), sum(len(o.waits) for o in self.ops[e])) for e in self.ALL}


def _host_consts():
    c = {}
    c["ident"] = np.eye(128, dtype=np.float32)
    blk = np.zeros((128, 128), np.float32)
    blk[:64, :64] = 1.0
    blk[64:, 64:] = 1.0
    c["blk1"] = blk
    rm = np.zeros((64, 64), np.float32)
    for a in range(2):
        for p in range(16):
            d0 = a * 32 + p
            d1 = a * 32 + 16 + p
            rm[d1, d0] = -1.0
            rm[d0, d1] = 1.0
    R = np.zeros((128, 128), np.float32)
    R[:64, :64] = rm
    R[64:, 64:] = rm
    c["rotm"] = R
    t = np.arange(L)
    t_row = (t // 64).astype(np.float64)
    t_col = (t % 64).astype(np.float64)
    inv_freq = (10000.0 ** (-np.arange(16, dtype=np.float64) * 2.0 / 32.0))
    inv_freq = inv_freq.astype(np.float32).astype(np.float64)
    ang_r = (t_row[:, None].astype(np.float32) * inv_freq[None, :].astype(np.float32)).astype(np.float64)
    ang_c = (t_col[:, None].astype(np.float32) * inv_freq[None, :].astype(np.float32)).astype(np.float64)
    ang = np.concatenate([ang_r, ang_r, ang_c, ang_c], axis=-1)
    c["ropec"] = np.ascontiguousarray(np.tile(np.cos(ang).T, (2, 1))).astype(np.float32)
    c["ropes"] = np.ascontiguousarray(np.tile(np.sin(ang).T, (2, 1))).astype(np.float32)
    k = np.arange(64)
    a64 = 2.0 * np.pi * ((k[:, None] * k[None, :]) % 64) / 64.0
    chd = np.zeros((128, 256), np.float64)
    for g in range(2):
        chd[g * 64:(g + 1) * 64, g * 64:(g + 1) * 64] = np.cos(a64) / 8.0
        chd[g * 64:(g + 1) * 64, 128 + g * 64:128 + (g + 1) * 64] = np.sin(a64) / 8.0
    c["chd"] = chd.astype(ml_dtypes.bfloat16)

    def dft(n):
        i = np.arange(n)
        a = 2.0 * np.pi * ((i[:, None] * i[None, :]) % n) / n
        return ((np.cos(a) / np.sqrt(n)).astype(ml_dtypes.bfloat16),
                ((-np.sin(a)) / np.sqrt(n)).astype(ml_dtypes.bfloat16))
    c["cl"], c["sl"] = dft(L)
    c["cc"], c["scx"] = dft(LC)
    i = np.arange(64)
    ge = (i[:, None] >= i[None, :]).astype(np.float32)
    le = (i[:, None] <= i[None, :]).astype(np.float32)
    eye = np.eye(64, dtype=np.float32)
    g = np.zeros((64, 5, 4, 64), np.float32)
    for p in range(4):
        fwd = (p % 2 == 0)
        g[:, 0, p, :] = le if fwd else ge
        g[:, 1, p, :] = ge if fwd else le
        g[:, 2, p, :] = le if fwd else ge
        g[:, 3, p, :] = eye
        g[:, 4, p, :] = 1.0 - (le if fwd else ge)
    c["gcon"] = g
    return c


_CONSTS = None


def _consts():
    global _CONSTS
    if _CONSTS is None:
        _CONSTS = _host_consts()
    return _CONSTS


class Prog:
    def __init__(self, cfg):
        self.cfg = cfg
        self.nc = bass.Bass("TRN2", target_bir_lowering=False)
        self.S = Sched(self.nc)
        self.ps_i = 0
        self.uid = 0

    def din(self, name, shape, dt=F32):
        return self.nc.dram_tensor(name, list(shape), dt, kind="ExternalInput").ap()

    def dout(self, name, shape, dt=F32):
        return self.nc.dram_tensor(name, list(shape), dt, kind="ExternalOutput").ap()

    def sb(self, name, shape, dt, off):
        self.uid += 1
        n = int(np.prod(shape[1:])) * (2 if dt == BF16 else 4)
        assert off % 32 == 0, (name, off)
        assert off + n <= 229376 - 2048, (name, off, n)
        t = self.nc.alloc_sbuf_tensor_at(f"{name}_{self.uid}", list(shape), dt, offset=off)
        return t

    def ps(self):
        i = self.ps_i
        self.ps_i = (i + 1) % 8
        return self.psum[i], f"ps{i}"

    def mm(self, out, lhsT, rhs, start=True, stop=True, r=(), w=(), nosig=False):
        def fn(e):
            return e.matmul(out, lhsT=lhsT, rhs=rhs, start=start, stop=stop)
        return self.S.op("pe", fn, reads=r, writes=w, nosig=nosig)

    def act(self, out, in_, func, r=(), w=(), scale=1.0, bias=0.0, eng="act"):
        def fn(e):
            return e.activation(out=out, in_=in_, func=func, scale=scale, bias=bias)
        return self.S.op("act", fn, reads=r, writes=w)

    def tt(self, eng, out, in0, in1, op, r=(), w=()):
        def fn(e):
            return e.tensor_tensor(out=out, in0=in0, in1=in1, op=op)
        return self.S.op(eng, fn, reads=r, writes=w)

    def ts(self, eng, out, in0, s1, op0, s2=None, op1=None, r=(), w=()):
        def fn(e):
            if op1 is None:
                return e.tensor_scalar(out=out, in0=in0, scalar1=s1, scalar2=None, op0=op0)
            return e.tensor_scalar(out=out, in0=in0, scalar1=s1, scalar2=s2, op0=op0, op1=op1)
        return self.S.op(eng, fn, reads=r, writes=w)

    def stt(self, eng, out, in0, scalar, in1, op0, op1, r=(), w=()):
        eng = "dve"

        def fn(e):
            return e.scalar_tensor_tensor(out=out, in0=in0, scalar=scalar, in1=in1, op0=op0, op1=op1)
        return self.S.op(eng, fn, reads=r, writes=w)

    def cp(self, eng, out, in_, r=(), w=()):
        if eng == "act":
            return self.act(out, in_, AF.Copy, r=r, w=w)

        def fn(e):
            return e.tensor_copy(out=out, in_=in_)
        return self.S.op(eng, fn, reads=r, writes=w)

    def memset(self, eng, ap, val, w=()):
        def fn(e):
            return e.memset(ap, val)
        return self.S.op(eng, fn, writes=w)

    def dma(self, out, in_, r=(), w=(), eng="sp"):
        def fn(e):
            return e.dma_start(out=out, in_=in_)
        return self.S.dma(fn, reads=r, writes=w, eng=eng)

    def build(self):
        nc = self.nc
        cfg = self.cfg
        NB = cfg.get("nb", 2)
        NL = cfg.get("nl", 2)
        FAM = cfg.get("fam", "GFAM")
        d = {}
        d["x"] = self.din("x", [2, L, D])
        d["ctx"] = self.din("ctx", [2, LC, D])
        d["cT"] = self.din("cT", [128, 8, 3])
        d["w_mod"] = self.din("w_mod", [2, D, 6 * D])
        d["b_modT"] = self.din("b_modT", [128, 2, 48])
        d["n1T"] = self.din("n1T", [128, 2, 8])
        d["n2T"] = self.din("n2T", [128, 2, 8])
        d["w_in"] = self.din("w_in", [2, D, 2064])
        d["convT"] = self.din("convT", [128, 2, 6, 3])
        d["qgT"] = self.din("qgT", [128, 2])
        d["kgT"] = self.din("kgT", [128, 2])
        d["ogT"] = self.din("ogT", [128, 2])
        d["alog_b"] = self.din("alog_b", [64, 2, 8])
        d["dtb_b"] = self.din("dtb_b", [64, 2, 8])
        d["w_out"] = self.din("w_out", [2, D, D])
        d["w_ff1"] = self.din("w_ff1", [2, D, 4 * D])
        d["w_ff2"] = self.din("w_ff2", [2, 4 * D, D])
        d["ident"] = self.din("ident", [128, 128])
        d["blk1"] = self.din("blk1", [128, 128])
        d["rotm"] = self.din("rotm", [128, 128])
        d["chd"] = self.din("chd", [128, 256], BF16)
        d["cl"] = self.din("cl", [L, L], BF16)
        d["sl"] = self.din("sl", [L, L], BF16)
        d["cc"] = self.din("cc", [LC, LC], BF16)
        d["scx"] = self.din("scx", [LC, LC], BF16)
        d["ropec"] = self.din("ropec", [128, L])
        d["ropes"] = self.din("ropes", [128, L])
        d["gcon"] = self.din("gcon", [64, 5, 4, 64])
        d["out"] = self.dout("out", [2, L, D])
        self.d = d
        self.psum = [nc.alloc_psum_tensor(f"psum{i}", [128, 512], F32) for i in range(8)]

        o = 16640
        G = {}

        def galloc(name, shape, dt=F32):
            nonlocal o
            n = int(np.prod(shape[1:])) * (2 if dt == BF16 else 4)
            t = self.sb(name, shape, dt, o)
            o += (n + 31) // 32 * 32
            G[name] = t
            return t
        ident = galloc("ident", [128, 128])
        onesf = galloc("onesf", [128, 128])
        negones = galloc("negones", [64, 64])
        blk1 = galloc("blk1", [128, 128])
        rotm = galloc("rotm", [128, 128])
        chd = galloc("chd", [128, 256], BF16)
        gcon = galloc("gcon", [64, 5, 4, 64])
        modp = galloc("modp", [128, 2, 6, 8, 3])
        n1T = galloc("n1T", [128, 2, 8])
        n2T = galloc("n2T", [128, 2, 8])
        convT = galloc("convT", [128, 2, 6, 3])
        qgT = galloc("qgT", [128, 2])
        kgT = galloc("kgT", [128, 2])
        ogT = galloc("ogT", [128, 2])
        alogb = galloc("alogb", [64, 2, 8])
        dtbb = galloc("dtbb", [64, 2, 8])
        nexpa = galloc("nexpa", [64, 2, 8])
        self.epsb = galloc("epsb", [128, 1])
        self.scr = galloc("scr", [128, 8])
        assert o <= 16640 + 10240, o
        XB = 16640 + 10240
        xT = self.sb("xT", [128, 8, L], F32, XB)
        xcT = self.sb("xcT", [128, 8, LC], F32, XB + 8 * L * 4)
        PB = XB + 8 * L * 4 + 8 * LC * 4
        self.PB = PB
        self.G = G
        self.xT, self.xcT = xT, xcT

        for nm in ["ident", "blk1", "rotm", "chd", "gcon", "n1T", "n2T", "convT", "qgT", "kgT", "ogT"]:
            self.dma(G[nm][:], d[nm], w=[nm])
        self.dma(alogb[:], d["alog_b"], w=["alogb"])
        self.dma(dtbb[:], d["dtb_b"], w=["dtbb"])
        self.memset("pool", onesf[:], 1.0, w=["onesf"])
        self.memset("pool", negones[:], -1.0, w=["negones"])
        self.act(nexpa[:], alogb[:], AF.Exp, r=["alogb"], w=["nexpa"])
        self.ts("pool", nexpa[:], nexpa[:], -1.0, ALU.mult, r=["nexpa"], w=["nexpa"])

        self.phase_mod()
        self.barrier()
        for b in range(NB):
            self.load_x(b)
            self.barrier()
            for l in range(NL):
                last = (l == 1)
                if "G" in FAM:
                    self.phase_gdn(b, l, last)
                    self.barrier()
                self.phase_h(b, l, last, gdn=("G" in FAM and cfg.get("gstage", 0) in (0, 5)))
                self.barrier()
                if "F" in FAM:
                    self.phase_fourier(b, l, last)
                    self.barrier()
                if "A" in FAM:
                    self.phase_attn(b, l, last)
                    self.barrier()
                if "M" in FAM:
                    self.phase_mlp(b, l, last)
                    self.barrier()
            self.store_x(b)
            self.barrier()
        self.S.finish()
        self.S.emit()
        return nc

    def barrier(self):
        scr, ident = self.scr, self.G["ident"]
        self.mm(self.psum[7][0:1, 0:1], ident[0:1, 0:1], ident[0:1, 0:1], r=["ident"], w=["ps7", "bar_pe"])
        self.memset("dve", scr[:, 0:1], 0.0, w=["bar_dve"])
        self.memset("pool", scr[:, 1:2], 0.0, w=["bar_pool"])
        self.act(scr[:, 2:3], self.epsb[:, 0:1], AF.Copy, r=["epsb"], w=["bar_act"])
        for e in Sched.ALL:
            self.S.wait_keys(e, ["bar_pe", "bar_dve", "bar_pool", "bar_act"])

    def xs(self, isctx, c, t0, n):
        return (self.xcT if isctx else self.xT)[:, c, t0:t0 + n]

    def xkey(self, isctx, t0):
        return ("xc" if isctx else f"x{t0 // 512}")

    def load_w(self, dst, dkey, src, stg, kc, ncols, c0, eng="pool"):
        srcv = src.rearrange("(k p) n -> p k n", p=128)
        per = max(1, 2048 // ncols)
        k = 0
        while k < kc:
            kk = min(per, kc - k)
            st, skey = stg[self.stg_i % len(stg)]
            self.stg_i += 1
            sv = st[:, 0:kk * ncols].rearrange("p (k n) -> p k n", n=ncols)
            self.dma(sv, srcv[:, k:k + kk, c0:c0 + ncols], w=[skey])
            self.cp(eng, dst[:, k:k + kk, 0:ncols], sv, r=[skey], w=[dkey])
            k += kk

    def norm_block(self, b, l, which, isctx, t0, n, hout, hkey, tmp):
        G = self.G
        r = 2 if isctx else b
        kA, kB = (0, 1) if which == 1 else (3, 4)
        modp = G["modp"]
        xk = self.xkey(isctx, t0)
        xk2 = self.xkey(isctx, t0 + n - 1)
        xr = [xk] if xk == xk2 else [xk, xk2]
        sq, rstd, xn = tmp["sq"], tmp["rstd"], tmp["xn"]
        pt, pk = self.ps()
        for c in range(8):
            s, sk = sq[c % len(sq)]
            self.act(s[:, 0:n], self.xs(isctx, c, t0, n), AF.Square, r=xr, w=[sk])
            self.mm(pt[:, 0:n], G["onesf"][:], s[:, 0:n], start=(c == 0), stop=(c == 7), r=[sk, "onesf"], w=[pk],
                    nosig=(c != 7))
        rs, rk = rstd
        self.act(rs[:, 0:n], pt[:, 0:n], AF.Ln, r=[pk], w=[rk], scale=1.0 / D, bias=self.epsb[:, 0:1])
        self.act(rs[:, 0:n], rs[:, 0:n], AF.Exp, r=[rk], w=[rk], scale=-0.5)
        for c in range(8):
            t, tk = xn[c % len(xn)]
            self.tt("dve", t[:, 0:n], self.xs(isctx, c, t0, n), rs[:, 0:n], ALU.mult, r=xr + [rk], w=[tk])
            self.ts("pool", hout[:, c, 0:n], t[:, 0:n], modp[:, l, kA, c, r:r + 1], ALU.mult,
                    modp[:, l, kB, c, r:r + 1], ALU.add, r=[tk, "modp"], w=[hkey])

    def phase_mod(self):
        G, d, PB = self.G, self.d, self.PB
        modp = G["modp"]
        self.memset("pool", self.epsb[:], EPS, w=["epsb"])
        cT = self.sb("cT", [128, 8, 3], F32, PB + 64)
        bmT = self.sb("bmT", [128, 2, 48], F32, PB + 256)
        mraw = self.sb("mraw", [128, 2, 48, 3], F32, PB + 1024)
        stg = [(self.sb(f"mstg{i}", [128, 8, 256], F32, PB + 4096 + i * 8192), f"mstg{i}") for i in range(4)]
        self.dma(cT[:], d["cT"], w=["cT"])
        self.dma(bmT[:], d["b_modT"], w=["bmT"])
        self.act(cT[:], cT[:], AF.Silu, r=["cT"], w=["cT"])
        si = 0
        for l in range(2):
            wv = d["w_mod"][l].rearrange("(k p) n -> p k n", p=128)
            pt, pk = self.ps()
            for piece in range(24):
                st, sk = stg[si % 4]
                si += 1
                self.dma(st[:], wv[:, :, piece * 256:(piece + 1) * 256], w=[sk])
                for jj in range(2):
                    j = piece * 2 + jj
                    for k in range(8):
                        self.mm(pt[:, j * 3:(j + 1) * 3], st[:, k, jj * 128:(jj + 1) * 128], cT[:, k, :],
                                start=(k == 0), stop=(k == 7), r=[sk, "cT"], w=[pk], nosig=not (k == 7 and jj == 1))
            self.tt("dve", mraw[:, l, :, :], pt[:, 0:144].rearrange("p (j r) -> p j r", r=3),
                    bmT[:, l, :].unsqueeze(2).broadcast_to([128, 48, 3]), ALU.add, r=[pk, "bmT"], w=["mraw"])
            for (kind, scj, shj, gj, nrm) in ((0, 8, 0, 16, "n1T"), (3, 32, 24, 40, "n2T")):
                self.ts("dve", modp[:, l, kind, :, :], mraw[:, l, scj:scj + 8, :], 1.0, ALU.add, r=["mraw"], w=["modp"])
                self.tt("dve", modp[:, l, kind, :, :], modp[:, l, kind, :, :],
                        G[nrm][:, l, :].unsqueeze(2).broadcast_to([128, 8, 3]), ALU.mult, r=["modp", nrm], w=["modp"])
                self.cp("dve", modp[:, l, kind + 1, :, :], mraw[:, l, shj:shj + 8, :], r=["mraw"], w=["modp"])
                self.cp("dve", modp[:, l, kind + 2, :, :], mraw[:, l, gj:gj + 8, :], r=["mraw"], w=["modp"])

    def load_x(self, b):
        G, d, PB = self.G, self.d, self.PB
        stg = [(self.sb(f"xstg{i}", [128, D], F32, PB + 64 + i * 4096), f"xstg{i}") for i in range(4)]
        si = 0
        for isctx, n_t in ((False, L), (True, LC)):
            src = d["ctx"][b] if isctx else d["x"][b]
            for t0 in range(0, n_t, 512):
                nt = min(512, n_t - t0)
                tiles = []
                for j in range(nt // 128):
                    st, sk = stg[si % 4]
                    si += 1
                    self.dma(st[:], src[t0 + j * 128:t0 + (j + 1) * 128, :], w=[sk])
                    tiles.append((st, sk))
                for c in range(8):
                    pt, pk = self.ps()
                    for j, (st, sk) in enumerate(tiles):
                        self.mm(pt[:, j * 128:(j + 1) * 128], st[:, c * 128:(c + 1) * 128], G["ident"][:],
                                r=[sk, "ident"], w=[pk], nosig=(j != len(tiles) - 1))
                    eng = "dve" if c % 2 == 0 else "act"
                    self.cp(eng, self.xs(isctx, c, t0, nt), pt[:, 0:nt], r=[pk], w=[self.xkey(isctx, t0)])

    def store_x(self, b):
        G, d, PB = self.G, self.d, self.PB
        stg = [(self.sb(f"ostg{i}", [128, D], F32, PB + 64 + i * 4096), f"xstg{i}") for i in range(4)]
        si = 0
        for tt in range(L // 128):
            st, sk = stg[si % 4]
            si += 1
            for half in range(2):
                pt, pk = self.ps()
                for cc in range(4):
                    c = half * 4 + cc
                    self.mm(pt[:, cc * 128:(cc + 1) * 128], self.xT[:, c, tt * 128:(tt + 1) * 128], G["ident"][:],
                            r=[self.xkey(False, tt * 128), "ident"], w=[pk], nosig=(cc != 3))
                eng = "dve" if half == 0 else "act"
                self.cp(eng, st[:, half * 512:(half + 1) * 512], pt[:, :], r=[pk], w=[sk])
            self.dma(d["out"][b][tt * 128:(tt + 1) * 128, :], st[:], r=[sk])

    TB = ((False, 0, 512), (False, 512, 512), (False, 1024, 512), (False, 1536, 512), (True, 0, 256))

    @staticmethod
    def hoff(isctx, t0):
        return (L + t0) if isctx else t0

    def norm_tmp(self, base):
        sq = [(self.sb(f"sq{i}", [128, 512], F32, base + i * 2048), f"sq{i}") for i in range(3)]
        rstd = (self.sb("rstd", [128, 512], F32, base + 6144), "rstd")
        xn = [(self.sb(f"xn{i}", [128, 512], F32, base + 8192 + i * 2048), f"xn{i}") for i in range(2)]
        return {"sq": sq, "rstd": rstd, "xn": xn}

    def residual(self, l, gkind, r, isctx, t0, n, pt, pk, c, eng="dve"):
        xa = self.xs(isctx, c, t0, n)
        xk = self.xkey(isctx, t0)
        self.stt(eng, xa, pt, self.G["modp"][:, l, gkind, c, r:r + 1], xa, ALU.mult, ALU.add,
                 r=[pk, "modp", xk], w=[xk])

    def phase_h(self, b, l, last, gdn):
        PB, d = self.PB, self.d
        hT = self.sb("hT", [128, 8, L + LC], BF16, PB)
        self.hT = hT
        o = PB + 36864
        tmp = self.norm_tmp(o)
        o += 12288
        for (isctx, t0, n) in self.TB:
            self.norm_block(b, l, 1, isctx, t0, n, hT[:, :, self.hoff(isctx, t0):self.hoff(isctx, t0) + n],
                            f"hT{self.hoff(isctx, t0) // 512}", tmp)
        if gdn:
            catD = self.catD
            self.stg_i = 0
            stg = [(self.sb(f"hstg{i}", [128, 2048], F32, o + i * 8192), f"stg{i}") for i in range(2)]
            o += 16384
            wo = self.sb("wo_d", [128, 2, D], BF16, o)
            self.load_w(wo, "wo", d["w_out"][l][768:1024, :], stg, 2, 1024, 0)
            self.out_proj(b, l, last, wo, "wo", 2, lambda j, off, n: catD[:, j, off:off + n], ["catD"])

    def out_proj(self, b, l, last, wo, wokey, nk, catfn, catkeys, blocks=None):
        for (isctx, t0, n) in (blocks or self.TB):
            if isctx and last:
                continue
            r = 2 if isctx else b
            off = self.hoff(isctx, t0)
            for c in range(8):
                pt, pk = self.ps()
                for j in range(nk):
                    self.mm(pt[:, 0:n], wo[:, j, c * 128:(c + 1) * 128], catfn(j, off, n), start=(j == 0),
                            stop=(j == nk - 1), r=[wokey] + catkeys, w=[pk], nosig=(j != nk - 1))
                self.residual(l, 2, r, isctx, t0, n, pt[:, 0:n], pk, c, eng="dve")

    def phase_mlp(self, b, l, last):
        PB, d = self.PB, self.d
        h2 = self.sb("h2T", [128, 8, L + LC], BF16, PB)
        o = PB + 36864
        tmp = self.norm_tmp(o)
        o += 12288
        self.stg_i = 0
        stg = [(self.sb(f"mstg{i}", [128, 2048], F32, o + i * 8192), f"stg{i}") for i in range(2)]
        o += 16384
        w1 = [(self.sb(f"w1e{i}", [128, 8, 512], BF16, o + i * 8192), f"w1e{i}") for i in range(2)]
        o += 16384
        w2 = [(self.sb(f"w2e{i}", [128, 4, D], BF16, o + i * 8192), f"w2e{i}") for i in range(2)]
        o += 16384
        uT = [(self.sb(f"uT{i}", [128, 4, 512], BF16, o + i * 4096), f"uT{i}") for i in range(2)]
        o += 8192
        blocks = [tb for tb in self.TB if not (tb[0] and last)]
        for (isctx, t0, n) in blocks:
            off = self.hoff(isctx, t0)
            self.norm_block(b, l, 2, isctx, t0, n, h2[:, :, off:off + n], f"h2T{off // 512}", tmp)
        ui = 0
        for e8 in range(8):
            w1t, w1k = w1[e8 % 2]
            w2t, w2k = w2[e8 % 2]
            self.load_w(w1t, w1k, d["w_ff1"][l], stg, 8, 512, e8 * 512)
            self.load_w(w2t, w2k, d["w_ff2"][l][e8 * 512:(e8 + 1) * 512, :], stg, 4, 1024, 0)
            for (isctx, t0, n) in blocks:
                off = self.hoff(isctx, t0)
                r = 2 if isctx else b
                ut, uk = uT[ui % 2]
                ui += 1
                for fc in range(4):
                    pt, pk = self.ps()
                    for k in range(8):
                        self.mm(pt[:, 0:n], w1t[:, k, fc * 128:(fc + 1) * 128], h2[:, k, off:off + n], start=(k == 0),
                                stop=(k == 7), r=[w1k, f"h2T{off // 512}"], w=[pk], nosig=(k != 7))
                    tq, tk = tmp["sq"][fc % 3]
                    self.act(tq[:, 0:n], pt[:, 0:n], AF.Relu, r=[pk], w=[tk])
                    self.tt("pool", ut[:, fc, 0:n], tq[:, 0:n], tq[:, 0:n], ALU.mult, r=[tk], w=[uk])
                for c in range(8):
                    pt, pk = self.ps()
                    for fc in range(4):
                        self.mm(pt[:, 0:n], w2t[:, fc, c * 128:(c + 1) * 128], ut[:, fc, 0:n], start=(fc == 0),
                                stop=(fc == 3), r=[w2k, uk], w=[pk], nosig=(fc != 3))
                    self.residual(l, 5, r, isctx, t0, n, pt[:, 0:n], pk, c)

    def phase_fourier(self, b, l, last):
        PB, d, G = self.PB, self.d, self.G
        hT = self.hT
        o = PB + 36864
        self.stg_i = 0
        stg = [(self.sb(f"fstg{i}", [128, 2048], F32, o + i * 8192), f"stg{i}") for i in range(2)]
        o += 16384
        wf = self.sb("wf", [128, 8, 256], BF16, o)
        o += 4096
        wo = self.sb("wo_f", [128, 2, D], BF16, o)
        o += 4096
        fT = [(self.sb(f"fT{i}", [128, 2, 512], BF16, o + i * 2048), f"fT{i}") for i in range(2)]
        o += 4096
        Gt = self.sb("Gt", [128, 18, 2, 256], BF16, o)
        o += 18432
        dft = [(self.sb(f"dft{i}", [128, 2, 16, 256], BF16, o + i * 16384), f"dft{i}") for i in range(2)]
        o += 32768
        catF = [(self.sb(f"catF{i}", [128, 2, 256], BF16, o + i * 1024), f"catF{i}") for i in range(2)]
        o += 2048
        assert o <= 229376
        self.load_w(wf, "wf", d["w_in"][l], stg, 8, 256, 0)
        self.load_w(wo, "wo", d["w_out"][l][0:256, :], stg, 2, 1024, 0)
        blocks = [tb for tb in self.TB if not (tb[0] and last)]
        fi = 0
        for (isctx, t0, n) in blocks:
            off = self.hoff(isctx, t0)
            ft, fk = fT[fi % 2]
            fi += 1
            for ch in range(2):
                pt, pk = self.ps()
                for k in range(8):
                    self.mm(pt[:, 0:n], wf[:, k, ch * 128:(ch + 1) * 128], hT[:, k, off:off + n], start=(k == 0),
                            stop=(k == 7), r=["wf", f"hT{off // 512}"], w=[pk], nosig=(k != 7))
                self.cp("act" if ch == 0 else "dve", ft[:, ch, 0:n], pt[:, 0:n], r=[pk], w=[fk])
            for j in range(n // 128):
                tile = off // 128 + j
                pt, pk = self.ps()
                for ch in range(2):
                    self.mm(pt[:, ch * 256:(ch + 1) * 256], ft[:, ch, j * 128:(j + 1) * 128], G["chd"][:],
                            r=[fk, "chd"], w=[pk], nosig=(ch == 0))
                self.cp("act" if j % 2 == 0 else "dve", Gt[:, tile, :, :].rearrange("p c n -> p (c n)"), pt[:, :],
                        r=[pk], w=["Gt"])
        di = 0
        ci = 0
        for (isctx, nl, tile0, ctab, stab) in ((False, L, 0, d["cl"], d["sl"]), (True, LC, 16, d["cc"], d["scx"])):
            if isctx and last:
                continue
            nlt = nl // 128
            cv = ctab.rearrange("(k p) n -> p k n", p=128)
            sv = stab.rearrange("(k p) n -> p k n", p=128)
            for lb in range(nl // 256):
                dt_, dk = dft[di % 2]
                di += 1
                self.dma(dt_[:, 0, 0:nlt, :], cv[:, :, lb * 256:(lb + 1) * 256], w=[dk])
                self.dma(dt_[:, 1, 0:nlt, :], sv[:, :, lb * 256:(lb + 1) * 256], w=[dk])
                ct, ck = catF[ci % 2]
                ci += 1
                for ch in range(2):
                    pt, pk = self.ps()
                    nmm = 2 * nlt
                    i = 0
                    for lt in range(nlt):
                        for cs in range(2):
                            self.mm(pt[:, 0:256], Gt[:, tile0 + lt, ch, cs * 128:(cs + 1) * 128], dt_[:, cs, lt, :],
                                    start=(i == 0), stop=(i == nmm - 1), r=["Gt", dk], w=[pk], nosig=(i != nmm - 1))
                            i += 1
                    self.cp("act" if ch == 0 else "dve", ct[:, ch, :], pt[:, 0:256], r=[pk], w=[ck])
                self.out_proj(b, l, last, wo, "wo", 2, lambda j, off, n, ct=ct: ct[:, j, 0:n], [ck],
                              blocks=[(isctx, lb * 256, 256)])

    def qk_post(self, pt, pk, n, gT, l, rope_off, dst, dkey, tmp, rope):
        G = self.G
        sq, sk = tmp["sq"]
        self.act(sq[:, 0:n], pt[:, 0:n], AF.Square, r=[pk], w=[sk])
        p2, p2k = self.ps()
        self.mm(p2[:, 0:n], G["blk1"][:], sq[:, 0:n], r=[sk, "blk1"], w=[p2k])
        rs, rk = tmp["rstd"]
        self.act(rs[:, 0:n], p2[:, 0:n], AF.Ln, r=[p2k], w=[rk], scale=1.0 / 64, bias=self.epsb[:, 0:1])
        self.act(rs[:, 0:n], rs[:, 0:n], AF.Exp, r=[rk], w=[rk], scale=-0.5)
        qn, qk = tmp["qn"]
        if rope_off is None:
            self.stt("dve", dst, pt[:, 0:n], gT[:, l:l + 1], rs[:, 0:n], ALU.mult, ALU.mult, r=[pk, rk], w=[dkey])
            return
        self.stt("dve", qn[:, 0:n], pt[:, 0:n], gT[:, l:l + 1], rs[:, 0:n], ALU.mult, ALU.mult, r=[pk, rk], w=[qk])
        p3, p3k = self.ps()
        self.mm(p3[:, 0:n], G["rotm"][:], qn[:, 0:n], r=[qk, "rotm"], w=[p3k])
        (ct, st), rpk = rope
        t1, t1k = tmp["t1"]
        t2, t2k = tmp["t2"]
        self.tt("pool", t1[:, 0:n], qn[:, 0:n], ct[:, 0:n], ALU.mult, r=[qk, rpk], w=[t1k])
        self.tt("dve", t2[:, 0:n], p3[:, 0:n], st[:, 0:n], ALU.mult, r=[p3k, rpk], w=[t2k])
        self.tt("pool", dst, t1[:, 0:n], t2[:, 0:n], ALU.add, r=[t1k, t2k], w=[dkey])

    def phase_attn(self, b, l, last):
        PB, d, G = self.PB, self.d, self.G
        hT = self.hT
        o = PB + 36864
        self.stg_i = 0
        stg = [(self.sb(f"astg{i}", [128, 2048], F32, o + i * 8192), f"stg{i}") for i in range(1)]
        o += 8192
        wq = self.sb("wq", [128, 8, 256], BF16, o); o += 4096
        wk = self.sb("wk", [128, 8, 128], BF16, o); o += 2048
        wv = self.sb("wv", [128, 8, 64], BF16, o); o += 1024
        wo = self.sb("wo_a", [128, 2, D], BF16, o); o += 4096
        qT = self.sb("qT", [128, 2, L + LC], BF16, o); o += 9216
        kT = self.sb("kT", [128, L + LC], BF16, o); o += 4608
        VA = self.sb("VA", [128, 18, 128], BF16, o); o += 4608
        VB = self.sb("VB", [128, 18, 128], BF16, o); o += 4608
        PT = [(self.sb(f"PT{i}", [128, 18, 256], BF16, o + i * 9216), f"PT{i}") for i in range(2)]
        o += 18432
        tmp = {}
        for i, nm in enumerate(["sq", "rstd", "qn", "t1", "t2"]):
            tmp[nm] = (self.sb(f"a_{nm}", [128, 512], F32, o + i * 2048), f"a_{nm}")
        o += 10240
        ropeb = []
        for i in range(2):
            ropeb.append(((self.sb(f"rc{i}", [128, 512], F32, o + i * 4096), self.sb(f"rs{i}", [128, 512], F32, o + i * 4096 + 2048)),
                          f"rope{i}"))
        o += 8192
        catA = [(self.sb(f"catA{i}", [128, 2, 256], BF16, o + i * 1024), f"catA{i}") for i in range(2)]
        o += 2048
        rsum = [(self.sb(f"rsum{i}", [128, 256], F32, o + i * 1024), f"rsum{i}") for i in range(2)]
        o += 2048
        assert o <= 229376, o
        ri = 0
        for g in range(2):
            self.load_w(wq, "wq", d["w_in"][l], stg, 8, 256, 256 + g * 256)
            for half in range(2):
                srcv = d["w_in"][l].rearrange("(k p) n -> p k n", p=128)
                st, skey = stg[self.stg_i % len(stg)]
                self.stg_i += 1
                sv = st[:, 0:512].rearrange("p (k n) -> p k n", n=64)
                self.dma(sv, srcv[:, :, 768 + g * 64:768 + (g + 1) * 64], w=[skey])
                self.cp("pool", wk[:, :, half * 64:(half + 1) * 64], sv, r=[skey], w=["wk"])
            self.load_w(wv, "wv", d["w_in"][l], stg, 8, 64, 896 + g * 64)
            self.load_w(wo, "wo", d["w_out"][l][256 + g * 256:256 + (g + 1) * 256, :], stg, 2, 1024, 0)
            self.memset("pool", VA[:, :, 64:128], 1.0, w=["VA"])
            self.memset("pool", VB[:, :, 0:64], 1.0, w=["VB"])
            for (isctx, t0, n) in self.TB:
                off = self.hoff(isctx, t0)
                hk = f"hT{off // 512}"
                rope = None
                if not isctx:
                    rope = ropeb[ri % 2]
                    ri += 1
                    self.dma(rope[0][0][:, 0:n], d["ropec"][:, t0:t0 + n], w=[rope[1]])
                    self.dma(rope[0][1][:, 0:n], d["ropes"][:, t0:t0 + n], w=[rope[1]])
                pt, pk = self.ps()
                for k in range(8):
                    self.mm(pt[:, 0:n], wk[:, k, :], hT[:, k, off:off + n], start=(k == 0), stop=(k == 7),
                            r=["wk", hk], w=[pk], nosig=(k != 7))
                self.qk_post(pt, pk, n, G["kgT"], l, None if isctx else t0, kT[:, off:off + n], "kT", tmp, rope)
                if not (isctx and last):
                    for qc in range(2):
                        pt, pk = self.ps()
                        for k in range(8):
                            self.mm(pt[:, 0:n], wq[:, k, qc * 128:(qc + 1) * 128], hT[:, k, off:off + n], start=(k == 0),
                                    stop=(k == 7), r=["wq", hk], w=[pk], nosig=(k != 7))
                        self.qk_post(pt, pk, n, G["qgT"], l, None if isctx else t0, qT[:, qc, off:off + n], "qT", tmp, rope)
                for j in range(n // 128):
                    tile = off // 128 + j
                    pt, pk = self.ps()
                    for k in range(8):
                        self.mm(pt[:, 0:64], hT[:, k, off + j * 128:off + (j + 1) * 128], wv[:, k, :], start=(k == 0),
                                stop=(k == 7), r=["wv", hk], w=[pk], nosig=(k != 7))
                    self.cp("act", VA[:, tile, 0:64], pt[:, 0:64], r=[pk], w=["VA"])
                    self.cp("dve", VB[:, tile, 64:128], pt[:, 0:64], r=[pk], w=["VB"])
            qblocks = [(False, q0) for q0 in range(0, L, 256)]
            if not last:
                qblocks.append((True, 0))
            pi = 0
            ci = 0
            for (isctx, q0) in qblocks:
                qoff = self.hoff(isctx, q0)
                ktiles = list(range(16, 18)) if isctx else list(range(18))
                ct, ck = catA[ci % 2]
                ci += 1
                pend = None
                heads = list(range(4))
                for hh in heads + [None]:
                    if hh is not None:
                        qc, hl = hh // 2, hh % 2
                        P, Pk = PT[pi % 2]
                        pi += 1
                        pr = slice(hl * 64, (hl + 1) * 64)
                        for ii in range(0, len(ktiles), 2):
                            pt, pk = self.ps()
                            kk = ktiles[ii:ii + 2]
                            for jj, kt in enumerate(kk):
                                self.mm(pt[:, jj * 256:(jj + 1) * 256], kT[pr, kt * 128:(kt + 1) * 128],
                                        qT[pr, qc, qoff:qoff + 256], r=["kT", "qT"], w=[pk], nosig=(jj != len(kk) - 1))
                            self.act(P[:, kk[0]:kk[0] + len(kk), :].rearrange("p a n -> p (a n)"), pt[:, 0:256 * len(kk)],
                                     AF.Exp, r=[pk], w=[Pk], scale=0.125)
                    if pend is not None:
                        (phh, pP, pPk) = pend
                        pqc, phl = phh // 2, phh % 2
                        Vt, Vk = (VA, "VA") if phl == 0 else (VB, "VB")
                        pt, pk = self.ps()
                        for ii, kt in enumerate(ktiles):
                            self.mm(pt[:, 0:256], Vt[:, kt, :], pP[:, kt, :], start=(ii == 0), stop=(ii == len(ktiles) - 1),
                                    r=[Vk, pPk], w=[pk], nosig=(ii != len(ktiles) - 1))
                        orow = slice(phl * 64, (phl + 1) * 64)
                        srow = slice((1 - phl) * 64, (2 - phl) * 64)
                        rs_, rsk = rsum[phh % 2]
                        self.cp("act", rs_[orow, :], pt[srow, 0:256], r=[pk], w=[rsk])
                        self.S.op("dve", (lambda e, a=rs_[orow, :]: e.reciprocal(out=a, in_=a)), reads=[rsk], writes=[rsk])
                        self.tt("dve", ct[orow, pqc, :], pt[orow, 0:256], rs_[orow, :], ALU.mult, r=[pk, rsk], w=[ck])
                    pend = (hh, P, Pk) if hh is not None else None
                self.out_proj(b, l, last, wo, "wo", 2, lambda j, off, n, ct=ct: ct[:, j, 0:n], [ck],
                              blocks=[(isctx, q0, 256)])

    def phase_gdn(self, b, l, last):
        PB, d, G = self.PB, self.d, self.G
        END = 229376 - 2048
        NT = L + LC
        catD = self.sb("catD", [128, 2, NT], BF16, END - 9216)
        self.catD = catD
        gcon = G["gcon"]
        tri4, minc4, mincT4, id4, ctri4 = (gcon[:, i, :, :] for i in range(5))
        ident, ones64, negones = G["ident"], G["onesf"][0:64, 0:64], G["negones"]
        for hp in range(2):
            o = PB
            qkv = self.sb("qkv", [128, 3, NT], F32, o); o += 27648
            dzs = self.sb("dzs", [128, NT], BF16, o); o += 4608
            G64 = self.sb("G64", [64, 36, 16], F32, o); o += 2304
            BETA = self.sb("BETA", [64, 36, 8], F32, o); o += 1152
            GG = self.sb("GG", [64, 36, 8], F32, o); o += 1152
            OV = o
            self.stg_i = 0
            stg = [(self.sb("gstg", [128, 2048], F32, o), "stg0")]; o += 8192
            win = self.sb("win", [128, 8, 528], BF16, o); o += 8448
            hblk = self.sb("hblk", [128, 8, 450], BF16, o); o += 7232
            tmp = self.norm_tmp(o); o += 12288
            raw = self.sb("raw", [128, 3, 450], F32, o); o += 5632
            cvt = self.sb("cvt", [128, 3, 448], F32, o); o += 5632
            assert o <= END - 9216
            for j, c0 in enumerate((1024, 1280, 1536, 1792)):
                self.load_w(win[:, :, j * 128:(j + 1) * 128], "win", d["w_in"][l], stg, 8, 128, c0 + hp * 128)
            self.load_w(win[:, :, 512:528], "win", d["w_in"][l], stg, 8, 16, 2048)
            blocks = [(True, 0, 256)] + [(False, s0, min(448, L - s0)) for s0 in range(0, L, 448)]
            for (isctx, s0, m) in blocks:
                Ls = LC if isctx else L
                lo, hi = max(s0 - 1, 0), min(s0 + m + 1, Ls)
                ncol = hi - lo
                c_lo = lo - (s0 - 1)
                g0 = (0 if isctx else LC) + s0
                self.norm_block(b, l, 1, isctx, lo, ncol, hblk[:, :, 0:ncol], "hblk", tmp)
                if s0 == 0:
                    self.memset("pool", raw[:, :, 0:1], 0.0, w=["raw"])
                if s0 + m == Ls:
                    self.memset("pool", raw[:, :, m + 1:m + 2], 0.0, w=["raw"])
                for j in range(3):
                    pt, pk = self.ps()
                    for k in range(8):
                        self.mm(pt[:, 0:ncol], win[:, k, j * 128:(j + 1) * 128], hblk[:, k, 0:ncol], start=(k == 0),
                                stop=(k == 7), r=["win", "hblk"], w=[pk], nosig=(k != 7))
                    self.cp("act" if j != 1 else "dve", raw[:, j, c_lo:c_lo + ncol], pt[:, 0:ncol], r=[pk], w=["raw"])
                    cw = G["convT"][:, l, j * 2 + hp, :]
                    eng = "dve" if j != 1 else "pool"
                    self.ts(eng, cvt[:, j, 0:m], raw[:, j, 1:1 + m], cw[:, 1:2], ALU.mult, r=["raw", "convT"], w=[f"cvt{j}"])
                    self.stt(eng, cvt[:, j, 0:m], raw[:, j, 0:m], cw[:, 0:1], cvt[:, j, 0:m], ALU.mult, ALU.add,
                             r=["raw", "convT", f"cvt{j}"], w=[f"cvt{j}"])
                    self.stt(eng, cvt[:, j, 0:m], raw[:, j, 2:2 + m], cw[:, 2:3], cvt[:, j, 0:m], ALU.mult, ALU.add,
                             r=["raw", "convT", f"cvt{j}"], w=[f"cvt{j}"])
                    self.act(qkv[:, j, g0:g0 + m], cvt[:, j, 0:m], AF.Silu, r=[f"cvt{j}"], w=["qkv"])
                pt, pk = self.ps()
                for k in range(8):
                    self.mm(pt[:, 0:ncol], win[:, k, 384:512], hblk[:, k, 0:ncol], start=(k == 0), stop=(k == 7),
                            r=["win", "hblk"], w=[pk], nosig=(k != 7))
                cs = s0 - lo
                self.act(dzs[:, g0:g0 + m], pt[:, cs:cs + m], AF.Silu, r=[pk], w=["dzs"])
                pt, pk = self.ps()
                nch = m // 64
                for ci in range(nch):
                    for k in range(8):
                        self.mm(pt[0:64, ci * 16:(ci + 1) * 16], hblk[:, k, cs + ci * 64:cs + (ci + 1) * 64], win[:, k, 512:528],
                                start=(k == 0), stop=(k == 7), r=["win", "hblk"], w=[pk], nosig=not (k == 7 and ci == nch - 1))
                self.cp("dve", G64[:, g0 // 64:g0 // 64 + nch, :].rearrange("p a n -> p (a n)"), pt[0:64, 0:nch * 16],
                        r=[pk], w=["G64"])
            gstage = self.cfg.get("gstage", 0)
            if gstage == 1:
                self.barrier()
                continue
            for j in range(2 if gstage != 22 else 0):
                for t0 in range(0, NT, 512):
                    n = min(512, NT - t0)
                    sq, sk = tmp["sq"][(t0 // 512) % 3]
                    self.act(sq[:, 0:n], qkv[:, j, t0:t0 + n], AF.Square, r=["qkv"], w=[sk])
                    pt, pk = self.ps()
                    self.mm(pt[:, 0:n], G["blk1"][:], sq[:, 0:n], r=[sk, "blk1"], w=[pk])
                    rs, rk = tmp["rstd"]
                    self.act(rs[:, 0:n], pt[:, 0:n], AF.Ln, r=[pk], w=[rk], bias=self.epsb[:, 0:1])
                    self.act(rs[:, 0:n], rs[:, 0:n], AF.Exp, r=[rk], w=[rk], scale=-0.5)
                    if j == 0:
                        self.stt("dve", qkv[:, j, t0:t0 + n], qkv[:, j, t0:t0 + n], 0.125, rs[:, 0:n], ALU.mult, ALU.mult,
                                 r=["qkv", rk], w=["qkv"])
                    else:
                        self.tt("dve", qkv[:, j, t0:t0 + n], qkv[:, j, t0:t0 + n], rs[:, 0:n], ALU.mult, r=["qkv", rk], w=["qkv"])
            if gstage == 21:
                self.barrier()
                continue
            self.act(BETA[:], G64[:, :, 0:8], AF.Sigmoid, r=["G64"], w=["BETA"])
            self.tt("dve", GG[:], G64[:, :, 8:16], G["dtbb"][:, l, :].unsqueeze(1).broadcast_to([64, 36, 8]), ALU.add,
                    r=["G64", "dtbb"], w=["GG"])
            self.act(GG[:], GG[:], AF.Exp, r=["GG"], w=["GG"])
            self.act(GG[:], GG[:], AF.Ln, r=["GG"], w=["GG"], bias=1.0)
            self.tt("dve", GG[:], GG[:], G["nexpa"][:, l, :].unsqueeze(1).broadcast_to([64, 36, 8]), ALU.mult,
                    r=["GG", "nexpa"], w=["GG"])
            self.barrier()
            if gstage in (2, 22):
                continue
            o = OV

            def m4(name, shape=(64, 4, 64)):
                nonlocal o
                t = self.sb(name, list(shape), F32, o)
                o += (int(np.prod(shape[1:])) * 4 + 31) // 32 * 32
                return t
            obuf = m4("obuf", (64, 36, 128))
            Sst = m4("Sst")
            lane_base = o
            NLANE = self.cfg.get("lanes", 3)
            lanes = []
            for li in range(NLANE):
                Lb = {"i": li}
                Lb["tok"] = m4(f"tok{li}", (64, 4, 3, 64))
                for nm in ("Gd", "tD1", "tD2", "dgc", "Idec", "AT", "Bm", "QKmT", "QpT"):
                    Lb[nm] = m4(f"{nm}{li}")
                Lb["XY"] = [m4(f"XY{li}_{i}", (64, 2, 4, 64)) for i in range(2)]
                Lb["Rb"] = m4(f"Rb{li}", (64, 4, 128))
                Lb["ex12"] = m4(f"ex12{li}", (64, 12))
                for nm in ("b4", "nbe", "g4"):
                    Lb[nm] = m4(f"{nm}{li}", (64, 4))
                lanes.append(Lb)
            assert o <= END - 9216, o
            self.memset("pool", Sst[:], 0.0, w=["Sst"])

            def bc(ap4):
                return ap4.unsqueeze(2).broadcast_to([64, 4, 64])

            def chunks_of(s):
                nf = s
                nb_ = (3 - s) if s < 4 else (39 - s)
                return (nf, nb_)

            def pre(s, Lb):
                li = Lb["i"]

                def K(n):
                    return f"{n}_{li}"
                tok, Gd, tD1, tD2, dgc, Idec = Lb["tok"], Lb["Gd"], Lb["tD1"], Lb["tD2"], Lb["dgc"], Lb["Idec"]
                AT, Bm, QKmT, QpT, XY, Rb = Lb["AT"], Lb["Bm"], Lb["QKmT"], Lb["QpT"], Lb["XY"], Lb["Rb"]
                ex12, b4, nbe, g4 = Lb["ex12"], Lb["b4"], Lb["nbe"], Lb["g4"]
                kdec = Gd
                cnt = [0]

                def lps():
                    i = 2 * li + (cnt[0] % 2)
                    cnt[0] += 1
                    return self.psum[i], f"ps{i}"
                nchd = chunks_of(s)
                gcol = (2 * hp, 4 + 2 * hp)
                b4v = b4[:].rearrange("p (h d) -> p d h", d=2)
                g4v = g4[:].rearrange("p (h d) -> p d h", d=2)
                for dd in range(2):
                    n_ = nchd[dd]
                    self.cp("pool", b4v[:, dd, :], BETA[:, n_, gcol[dd]:gcol[dd] + 2], r=["BETA"], w=[K("b4")])
                    self.cp("pool", g4v[:, dd, :], GG[:, n_, gcol[dd]:gcol[dd] + 2], r=["GG"], w=[K("g4")])
                for hl in range(2):
                    pt, pk = lps()
                    pr = slice(hl * 64, (hl + 1) * 64)
                    for dd in range(2):
                        t0 = nchd[dd] * 64
                        for j in range(3):
                            self.mm(pt[0:64, (dd * 3 + j) * 64:(dd * 3 + j + 1) * 64], qkv[pr, j, t0:t0 + 64], ident[pr, pr],
                                    r=["qkv", "ident"], w=[pk], nosig=not (dd == 1 and j == 2))
                    self.cp("act" if hl == 0 else "dve", tok[:, 2 * hl:2 * hl + 2, :, :].rearrange("p a j n -> p (a j n)"),
                            pt[0:64, 0:384], r=[pk], w=[K("tok")])
                yield
                self.tt("pool", Gd[:], tri4, bc(g4[:]), ALU.mult, r=["gcon", K("g4")], w=[K("Gd")])
                yield
                pD, pDk = lps()
                self.mm(pD[0:64, 0:256], negones[:], Gd[:].rearrange("p a n -> p (a n)"), start=True, stop=False, r=[K("Gd"), "negones"], w=[pDk], nosig=True)
                for p in range(4):
                    self.mm(pD[0:64, p * 64:(p + 1) * 64], Gd[:, p, :], ones64, start=False, stop=True, r=[K("Gd"), "onesf"], w=[pDk], nosig=True)
                for p in range(4):
                    gsl = g4[:, p:p + 1]
                    self.mm(pD[0:64, 256 + p:257 + p], tri4[:, p, :], gsl, r=["gcon", K("g4")], w=[pDk], nosig=True)
                    self.mm(pD[0:64, 260 + p:261 + p], ctri4[:, p, :], gsl, r=["gcon", K("g4")], w=[pDk], nosig=True)
                self.mm(pD[0:64, 264:268], ones64, g4[:, 0:4], r=["onesf", K("g4")], w=[pDk])
                yield
                pDv = pD[0:64, 0:256].rearrange("p (a n) -> p a n", n=64)
                self.ts("dve", tD1[:], pDv, 0.0, ALU.min, r=[pDk], w=[K("tD1")])
                self.ts("dve", tD2[:], pDv, -1.0, ALU.mult, 0.0, ALU.min, r=[pDk], w=[K("tD2")])
                self.act(ex12[:], pD[0:64, 256:268], AF.Exp, r=[pDk], w=[K("ex12")])
                yield
                self.act(tD1[:], tD1[:], AF.Exp, r=[K("tD1")], w=[K("tD1")])
                self.act(tD2[:], tD2[:], AF.Exp, r=[K("tD2")], w=[K("tD2")])
                self.stt("dve", nbe[:], b4[:], -1.0, ex12[:, 0:4], ALU.mult, ALU.mult, r=[K("b4"), K("ex12")], w=[K("nbe")])
                yield
                self.tt("pool", tD1[:], tD1[:], minc4, ALU.mult, r=[K("tD1"), "gcon"], w=[K("tD1")])
                self.tt("pool", tD2[:], tD2[:], mincT4, ALU.mult, r=[K("tD2"), "gcon"], w=[K("tD2")])
                self.tt("pool", Rb[:, :, 0:64], tok[:, :, 2, :], bc(b4[:]), ALU.mult, r=[K("tok"), K("b4")], w=[K("Rb")])
                yield
                self.tt("pool", tD1[:], tD1[:], id4, ALU.subtract, r=[K("tD1"), "gcon"], w=[K("tD1")])
                self.tt("pool", Rb[:, :, 64:128], tok[:, :, 1, :], bc(nbe[:]), ALU.mult, r=[K("tok"), K("nbe")], w=[K("Rb")])
                yield
                X0, Y0 = XY[0][:, 0, :, :], XY[0][:, 1, :, :]
                for hl in range(2):
                    pK, pKk = lps()
                    pr = slice(hl * 64, (hl + 1) * 64)
                    for dd in range(2):
                        t0 = nchd[dd] * 64
                        self.mm(pK[0:64, (2 * dd) * 64:(2 * dd + 1) * 64], qkv[pr, 1, t0:t0 + 64], qkv[pr, 1, t0:t0 + 64], r=["qkv"], w=[pKk], nosig=True)
                        self.mm(pK[0:64, (2 * dd + 1) * 64:(2 * dd + 2) * 64], qkv[pr, 1, t0:t0 + 64], qkv[pr, 0, t0:t0 + 64], r=["qkv"], w=[pKk],
                                nosig=(dd != 1))
                    pKv = pK[0:64, 0:256].rearrange("p (a t n) -> p a t n", t=2, n=64)
                    ps_ = slice(2 * hl, 2 * hl + 2)
                    self.tt("dve", XY[0][:, 0, ps_, :], pKv[:, :, 0, :], tD1[:, ps_, :], ALU.mult, r=[pKk, K("tD1")], w=[K("XY0")])
                    self.tt("dve", QKmT[:, ps_, :], pKv[:, :, 1, :], tD2[:, ps_, :], ALU.mult, r=[pKk, K("tD2")], w=[K("QKmT")])
                self.tt("pool", kdec[:], tok[:, :, 1, :], bc(ex12[:, 4:8]), ALU.mult, r=[K("tok"), K("ex12")], w=[K("Gd")])
                yield
                self.stt("dve", X0, X0, -1.0, bc(b4[:]), ALU.mult, ALU.mult, r=[K("XY0"), K("b4")], w=[K("XY0")])
                self.tt("pool", dgc[:], id4, bc(ex12[:, 0:4]), ALU.mult, r=["gcon", K("ex12")], w=[K("dgc")])
                self.tt("pool", Idec[:], id4, bc(ex12[:, 8:12]), ALU.mult, r=["gcon", K("ex12")], w=[K("Idec")])
                yield
                pY, pYk = lps()
                for p in range(4):
                    self.mm(pY[0:64, p * 64:(p + 1) * 64], XY[0][:, 0, p, :], ident[0:64, 0:64], r=[K("XY0"), "ident"], w=[pYk], nosig=(p != 3))
                yield
                self.cp("act", Y0, pY[0:64, 0:256].rearrange("p (a n) -> p a n", n=64), r=[pYk], w=[K("XY0")])
                yield
                for kk in range(6):
                    cur, nxt = XY[kk % 2], XY[(kk + 1) % 2]
                    ck, nk = K(f"XY{kk % 2}"), K(f"XY{(kk + 1) % 2}")
                    pR, pRk = lps()
                    for p in range(4):
                        self.mm(pR[0:64, p * 128:(p + 1) * 128], cur[:, 1, p, :], Rb[:, p, :], r=[ck, K("Rb")], w=[pRk], nosig=(p != 3))
                    if kk < 5:
                        pS, pSk = lps()
                        for p in range(4):
                            self.mm(pS[0:64, p * 64:(p + 1) * 64], cur[:, 1, p, :], cur[:, 0, p, :], r=[ck], w=[pSk], nosig=True)
                            self.mm(pS[0:64, (4 + p) * 64:(5 + p) * 64], cur[:, 0, p, :], cur[:, 1, p, :], r=[ck], w=[pSk], nosig=(p != 3))
                    yield
                    if kk < 5:
                        self.cp("act", nxt[:].rearrange("p t a n -> p (t a n)"), pS[0:64, :], r=[pSk], w=[nk])
                    self.tt("dve", Rb[:].rearrange("p a n -> p (a n)"), pR[0:64, :], Rb[:].rearrange("p a n -> p (a n)"), ALU.add,
                            r=[pRk, K("Rb")], w=[K("Rb")])
                    yield
                pA, pAk = lps()
                for p in range(4):
                    self.mm(pA[0:64, p * 64:(p + 1) * 64], Rb[:, p, 64:128], kdec[:, p, :], r=[K("Rb"), K("Gd")], w=[pAk], nosig=True)
                    self.mm(pA[0:64, (4 + p) * 64:(5 + p) * 64], kdec[:, p, :], Rb[:, p, 0:64], r=[K("Rb"), K("Gd")], w=[pAk], nosig=(p != 3))
                pQ, pQk = lps()
                for p in range(4):
                    self.mm(pQ[0:64, p * 64:(p + 1) * 64], tok[:, p, 0, :], dgc[:, p, :], start=True, stop=False, r=[K("tok"), K("dgc")], w=[pQk], nosig=True)
                    self.mm(pQ[0:64, p * 64:(p + 1) * 64], Rb[:, p, 64:128], QKmT[:, p, :], start=False, stop=True, r=[K("Rb"), K("QKmT")], w=[pQk],
                            nosig=(p != 3))
                yield
                pAv = pA[0:64, :].rearrange("p (t a n) -> p t a n", t=2, n=64)
                self.tt("dve", AT[:], pAv[:, 0, :, :], Idec[:], ALU.add, r=[pAk, K("Idec")], w=[K("AT")])
                self.cp("act", QpT[:], pQ[0:64, 0:256].rearrange("p (a n) -> p a n", n=64), r=[pQk], w=[K("QpT")])
                yield
                self.cp("act", Bm[:], pAv[:, 1, :, :], r=[pAk], w=[K("Bm")])
                yield

            def scan(s, Lb):
                li = Lb["i"]

                def K(n):
                    return f"{n}_{li}"
                AT, Bm, QKmT, QpT, Rb = Lb["AT"], Lb["Bm"], Lb["QKmT"], Lb["QpT"], Lb["Rb"]
                nchd = chunks_of(s)
                pO, pOk = self.psum[6], "ps6"
                for p in range(4):
                    self.mm(pO[0:64, p * 64:(p + 1) * 64], QpT[:, p, :], Sst[:, p, :], start=True, stop=False, r=[K("QpT"), "Sst"], w=[pOk], nosig=True)
                    self.mm(pO[0:64, p * 64:(p + 1) * 64], QKmT[:, p, :], Rb[:, p, 0:64], start=False, stop=True, r=[K("QKmT"), K("Rb")], w=[pOk],
                            nosig=(p != 3))
                pN, pNk = self.psum[7], "ps7"
                for p in range(4):
                    self.mm(pN[0:64, p * 64:(p + 1) * 64], AT[:, p, :], Sst[:, p, :], r=[K("AT"), "Sst"], w=[pNk], nosig=(p != 3))
                self.tt("dve", Sst[:], pN[0:64, 0:256].rearrange("p (a n) -> p a n", n=64), Bm[:], ALU.add, r=[pNk, K("Bm")], w=["Sst"])
                pOv = pO[0:64, 0:256].rearrange("p (h d n) -> p d h n", d=2, n=64)
                for dd in range(2):
                    n_ = nchd[dd]
                    sb_other = (3 - n_) if n_ < 4 else (39 - n_)
                    ostep = sb_other if dd == 0 else n_
                    first = s < ostep
                    src = pOv[:, dd, :, :]
                    dst = obuf[:, n_, :].rearrange("p (h n) -> p h n", n=64)
                    if first:
                        self.cp("act", dst, src, r=[pOk], w=["obuf"])
                    else:
                        self.tt("dve", dst, src, dst, ALU.add, r=[pOk, "obuf"], w=["obuf"])

            nsteps = self.cfg.get("gsteps", 36)
            NSEG = 30
            stagger = self.cfg.get("stagger", NSEG // NLANE)
            gens = [None] * NLANE
            step_of = [None] * NLANE
            start_tick = [i * stagger for i in range(NLANE)]
            next_step = 0
            done = 0
            tick = 0
            while done < nsteps:
                for li in range(NLANE):
                    if gens[li] is None:
                        if next_step < nsteps and tick >= start_tick[li]:
                            gens[li] = pre(next_step, lanes[li])
                            step_of[li] = next_step
                            next_step += 1
                        else:
                            continue
                    try:
                        next(gens[li])
                    except StopIteration:
                        scan(step_of[li], lanes[li])
                        gens[li] = None
                        done += 1
                tick += 1
            o = lane_base
            self.barrier()
            if gstage == 3:
                continue
            o2 = o
            osq = self.sb("osq", [64, 36 * 128], F32, o2); o2 += 36 * 128 * 4
            oss = self.sb("oss", [64, 72], F32, o2); o2 += 288
            assert o2 <= END - 9216, o2
            ov = obuf[:].rearrange("p a n -> p (a n)")
            self.tt("pool", osq[:], ov, ov, ALU.mult, r=["obuf"], w=["osq"])
            self.S.op("dve", (lambda e: e.tensor_reduce(out=oss[:], in_=osq[:].rearrange("p (a n) -> p a n", n=64), axis=AX.X, op=ALU.add)),
                      reads=["osq"], writes=["oss"])
            self.act(oss[:], oss[:], AF.Ln, r=["oss"], w=["oss"], scale=1.0 / 64, bias=self.epsb[0:64, 0:1])
            self.act(oss[:], oss[:], AF.Exp, r=["oss"], w=["oss"], scale=-0.5)
            self.tt("dve", obuf[:].rearrange("p a (h n) -> p (a h) n", n=64), obuf[:].rearrange("p a (h n) -> p (a h) n", n=64),
                    oss[:].unsqueeze(2).broadcast_to([64, 72, 64]), ALU.mult, r=["obuf", "oss"], w=["obuf"])
            if gstage == 4:
                self.barrier()
                continue
            for c8 in range(0, 36, 8):
                nchk = min(8, 36 - c8)
                if last and c8 + nchk <= 4:
                    continue
                pt, pk = self.ps()
                for ci in range(nchk):
                    self.mm(pt[:, ci * 64:(ci + 1) * 64], obuf[:, c8 + ci, :], ident[0:64, 0:64], r=["obuf", "ident"], w=[pk], nosig=(ci != nchk - 1))
                segs = []
                g0, g1 = c8 * 64, (c8 + nchk) * 64
                if g0 < LC:
                    segs.append((g0, min(g1, LC), L + g0))
                if g1 > LC:
                    a = max(g0, LC)
                    segs.append((a, g1, a - LC))
                for (a, e_, fo) in segs:
                    if gstage in (5, 6):
                        self.ts("dve", catD[:, hp, fo:fo + (e_ - a)], pt[:, a - g0:e_ - g0], G["ogT"][:, l:l + 1], ALU.mult, r=[pk, "ogT"], w=["catD"])
                        continue
                    self.stt("dve", catD[:, hp, fo:fo + (e_ - a)], pt[:, a - g0:e_ - g0], G["ogT"][:, l:l + 1], dzs[:, a:e_], ALU.mult, ALU.mult,
                             r=[pk, "ogT", "dzs"], w=["catD"])
            self.barrier()


def _prep_inputs(inp, core):
    cst = _consts()
    b0 = 2 * core
    f = lambda a: np.ascontiguousarray(a, dtype=np.float32)
    m = {}
    m["x"] = f(inp["x"][b0:b0 + 2])
    m["ctx"] = f(inp["ctx"][b0:b0 + 2])
    rows = np.stack([inp["c"][b0], inp["c"][b0 + 1], inp["c_ctx"]], axis=0)
    m["cT"] = f(rows.reshape(3, 8, 128).transpose(2, 1, 0))
    m["w_mod"] = f(inp["w_mod"])
    m["b_modT"] = f(inp["b_mod"].reshape(2, 48, 128).transpose(2, 0, 1))
    m["n1T"] = f(inp["norm1_g"].reshape(2, 8, 128).transpose(2, 0, 1))
    m["n2T"] = f(inp["norm2_g"].reshape(2, 8, 128).transpose(2, 0, 1))
    m["w_in"] = f(inp["w_in"])
    m["convT"] = f(inp["conv_w"].reshape(2, 3, 6, 128).transpose(3, 0, 2, 1))
    m["qgT"] = f(np.tile(inp["q_norm_g"], (1, 2)).T)
    m["kgT"] = f(np.tile(inp["k_norm_g"], (1, 2)).T)
    m["ogT"] = f(np.tile(inp["o_norm_g"], (1, 2)).T)
    m["alog_b"] = f(np.broadcast_to(inp["a_log"].reshape(1, 2, 8), (64, 2, 8)))
    m["dtb_b"] = f(np.broadcast_to(inp["dt_bias"].reshape(1, 2, 8), (64, 2, 8)))
    m["w_out"] = f(inp["w_out"])
    m["w_ff1"] = f(inp["w_ff1"])
    m["w_ff2"] = f(inp["w_ff2"])
    for k in ["ident", "blk1", "rotm", "chd", "cl", "sl", "cc", "scx", "ropec", "ropes", "gcon"]:
        m[k] = cst[k]
    return m


_PROG = {}


def kernel(**inputs):
    inp = {k: np.asarray(v) for k, v in inputs.items()}
    cfg = dict(_CFG)
    key = repr(sorted(cfg.items()))
    if key not in _PROG:
        _PROG[key] = Prog(cfg).build()
    nc = _PROG[key]
    in_maps = [_prep_inputs(inp, c) for c in range(NCORES)]
    res = run_bass_kernel_spmd(nc, in_maps, core_ids=list(range(NCORES)))
    out = np.concatenate([np.asarray(r["out"]) for r in res.results], axis=0)
    return out.astype(np.float32)


_CFG = {"nb": 2, "nl": 2, "fam": "GFAM"}
```

```python
import bisect
import numpy as np
import ml_dtypes
import concourse.bass as bass
import concourse.mybir as mybir
from concourse.bass_utils import run_bass_kernel_spmd

F32 = mybir.dt.float32
BF16 = mybir.dt.bfloat16
AF = mybir.ActivationFunctionType
ALU = mybir.AluOpType
AX = mybir.AxisListType

EPOCH = 20000
NDS = 12
EPS = 1e-6
D = 1024
L = 2048
LC = 256
NCORES = 8


class _Op:
    __slots__ = ("waits", "fn", "sig", "dma")

    def __init__(self, fn):
        self.waits = []
        self.fn = fn
        self.sig = None
        self.dma = None


class Sched:
    CE = ("pe", "act", "dve", "pool")
    ALL = ("pe", "act", "dve", "pool", "sp")

    def __init__(self, nc):
        self.nc = nc
        self.ops = {e: [] for e in self.ALL}
        self.sems = {e: [] for e in self.CE}
        self.nsig = {e: 0 for e in self.CE}
        self.sig_idx = {e: [] for e in self.CE}
        self.waited = {e: {} for e in self.ALL}
        self.last_w = {}
        self.readers = {}
        self.dsem = {}
        self.dval = {}
        self.dnext = {}

    def _csem(self, e, k):
        while len(self.sems[e]) <= k:
            self.sems[e].append(self.nc.alloc_semaphore(name=f"s_{e}_{len(self.sems[e])}"))
        return self.sems[e][k]

    def _resolve(self, tok):
        if tok[0] == "d":
            return tok[1], tok[2]
        _, e, idx = tok
        lst = self.sig_idx[e]
        p = bisect.bisect_left(lst, idx)
        if p < len(lst):
            return self.ops[e][lst[p]].sig
        op = self.ops[e][idx]
        n = self.nsig[e]
        self.nsig[e] = n + 1
        op.sig = (self._csem(e, n // EPOCH), (n % EPOCH) + 1)
        lst.append(idx)
        return op.sig

    def _add_wait(self, eng, op, tok):
        if tok is None:
            return
        if tok[0] == "c" and tok[1] == eng and eng == "pe":
            return
        sem, val = self._resolve(tok)
        w = self.waited[eng]
        key = id(sem)
        if w.get(key, 0) >= val:
            return
        w[key] = val
        op.waits.append((sem, val))

    def _deps(self, eng, op, reads, writes):
        toks = []
        for k in reads:
            toks.append(self.last_w.get(k))
        for k in writes:
            toks.append(self.last_w.get(k))
            for r in self.readers.get(k, ()):
                if r[0] == "c" and r[1] == eng:
                    continue
                toks.append(r)
        for t in toks:
            self._add_wait(eng, op, t)

    def op(self, eng, fn, reads=(), writes=(), nosig=False):
        if eng != "pe":
            xs = [k for k in reads if k.startswith("ps") and k not in writes]
            if xs:
                writes = list(writes) + xs
        o = _Op(fn)
        self._deps(eng, o, reads, writes)
        idx = len(self.ops[eng])
        self.ops[eng].append(o)
        if not nosig:
            n = self.nsig[eng]
            self.nsig[eng] = n + 1
            o.sig = (self._csem(eng, n // EPOCH), (n % EPOCH) + 1)
            self.sig_idx[eng].append(idx)
        tok = ("c", eng, idx)
        for k in reads:
            self.readers.setdefault(k, []).append(tok)
        for k in writes:
            self.last_w[k] = tok
            self.readers[k] = []
        return o

    def dma(self, fn, reads=(), writes=(), eng="sp"):
        o = _Op(fn)
        if eng not in self.dsem:
            self.dsem[eng] = [self.nc.alloc_semaphore(name=f"d_{eng}_{i}") for i in range(NDS)]
            self.dval[eng] = [0] * NDS
            self.dnext[eng] = 0
        k = self.dnext[eng]
        self.dnext[eng] = (k + 1) % NDS
        sem = self.dsem[eng][k]
        if self.dval[eng][k] > 0:
            self._add_wait(eng, o, ("d", sem, self.dval[eng][k]))
        self._deps(eng, o, reads, writes)
        self.dval[eng][k] += 16
        o.dma = sem
        self.ops[eng].append(o)
        tok = ("d", sem, self.dval[eng][k])
        for kk in reads:
            self.readers.setdefault(kk, []).append(tok)
        for kk in writes:
            self.last_w[kk] = tok
            self.readers[kk] = []
        return o

    def wait_keys(self, eng, keys):
        o = _Op(None)
        for k in keys:
            self._add_wait(eng, o, self.last_w.get(k))
        self.ops[eng].append(o)

    def finish(self):
        o = _Op(None)
        for eng in self.dsem:
            for k, sem in enumerate(self.dsem[eng]):
                if self.dval[eng][k] > 0:
                    o.waits.append((sem, self.dval[eng][k]))
        self.ops["sp"].append(o)

    def emit(self):
        nc = self.nc
        sched = self
        allsems = [s for e in self.CE for s in self.sems[e]] + [s for e in self.dsem for s in self.dsem[e]]

        with nc.Block() as block0:
            @block0.gpsimd
            def _(e):
                for s in allsems:
                    e.sem_clear(s)

        def run(name, e):
            for o in sched.ops[name]:
                for sem, val in o.waits:
                    e.wait_ge(sem, val)
                if o.fn is None:
                    continue
                ins = o.fn(e)
                if o.sig is not None:
                    ins.then_inc(o.sig[0], 1)
                if o.dma is not None:
                    ins.then_inc(o.dma, 16)

        with nc.Block() as block:
            @block.tensor
            def _(e):
                run("pe", e)

            @block.scalar
            def _(e):
                run("act", e)

            @block.vector
            def _(e):
                run("dve", e)

            @block.gpsimd
            def _(e):
                run("pool", e)

            @block.sync
            def _(e):
                run("sp", e)

    def stats(self):
        return {e: (len(self.ops[e]), sum(len(o.waits) for o in self.ops[e])) for e in self.ALL}


def _host_consts():
    c = {}
    c["ident"] = np.eye(128, dtype=np.float32)
    blk = np.zeros((128, 128), np.float32)
    blk[:64, :64] = 1.0
    blk[64:, 64:] = 1.0
    c["blk1"] = blk
    rm = np.zeros((64, 64), np.float32)
    for a in range(2):
        for p in range(16):
            d0 = a * 32 + p
            d1 = a * 32 + 16 + p
            rm[d1, d0] = -1.0
            rm[d0, d1] = 1.0
    R = np.zeros((128, 128), np.float32)
    R[:64, :64] = rm
    R[64:, 64:] = rm
    c["rotm"] = R
    t = np.arange(L)
    t_row = (t // 64).astype(np.float64)
    t_col = (t % 64).astype(np.float64)
    inv_freq = (10000.0 ** (-np.arange(16, dtype=np.float64) * 2.0 / 32.0))
    inv_freq = inv_freq.astype(np.float32).astype(np.float64)
    ang_r = (t_row[:, None].astype(np.float32) * inv_freq[None, :].astype(np.float32)).astype(np.float64)
    ang_c = (t_col[:, None].astype(np.float32) * inv_freq[None, :].astype(np.float32)).astype(np.float64)
    ang = np.concatenate([ang_r, ang_r, ang_c, ang_c], axis=-1)
    c["ropec"] = np.ascontiguousarray(np.tile(np.cos(ang).T, (2, 1))).astype(np.float32)
    c["ropes"] = np.ascontiguousarray(np.tile(np.sin(ang).T, (2, 1))).astype(np.float32)
    k = np.arange(64)
    a64 = 2.0 * np.pi * ((k[:, None] * k[None, :]) % 64) / 64.0
    chd = np.zeros((128, 256), np.float64)
    for g in range(2):
        chd[g * 64:(g + 1) * 64, g * 64:(g + 1) * 64] = np.cos(a64) / 8.0
        chd[g * 64:(g + 1) * 64, 128 + g * 64:128 + (g + 1) * 64] = np.sin(a64) / 8.0
    c["chd"] = chd.astype(ml_dtypes.bfloat16)

    def dft(n):
        i = np.arange(n)
        a = 2.0 * np.pi * ((i[:, None] * i[None, :]) % n) / n
        return ((np.cos(a) / np.sqrt(n)).astype(ml_dtypes.bfloat16),
                ((-np.sin(a)) / np.sqrt(n)).astype(ml_dtypes.bfloat16))
    c["cl"], c["sl"] = dft(L)
    c["cc"], c["scx"] = dft(LC)
    i = np.arange(64)
    ge = (i[:, None] >= i[None, :]).astype(np.float32)
    le = (i[:, None] <= i[None, :]).astype(np.float32)
    eye = np.eye(64, dtype=np.float32)
    g = np.zeros((64, 5, 4, 64), np.float32)
    for p in range(4):
        fwd = (p % 2 == 0)
        g[:, 0, p, :] = le if fwd else ge
        g[:, 1, p, :] = ge if fwd else le
        g[:, 2, p, :] = le if fwd else ge
        g[:, 3, p, :] = eye
        g[:, 4, p, :] = 1.0 - (le if fwd else ge)
    c["gcon"] = g
    return c


_CONSTS = None


def _consts():
    global _CONSTS
    if _CONSTS is None:
        _CONSTS = _host_consts()
    return _CONSTS


class Prog:
    def __init__(self, cfg):
        self.cfg = cfg
        self.nc = bass.Bass("TRN2", target_bir_lowering=False)
        self.S = Sched(self.nc)
        self.ps_i = 0
        self.uid = 0

    def din(self, name, shape, dt=F32):
        return self.nc.dram_tensor(name, list(shape), dt, kind="ExternalInput").ap()

    def dout(self, name, shape, dt=F32):
        return self.nc.dram_tensor(name, list(shape), dt, kind="ExternalOutput").ap()

    def sb(self, name, shape, dt, off):
        self.uid += 1
        n = int(np.prod(shape[1:])) * (2 if dt == BF16 else 4)
        assert off % 32 == 0, (name, off)
        assert off + n <= 229376 - 2048, (name, off, n)
        t = self.nc.alloc_sbuf_tensor_at(f"{name}_{self.uid}", list(shape), dt, offset=off)
        return t

    def ps(self):
        i = self.ps_i
        self.ps_i = (i + 1) % 8
        return self.psum[i], f"ps{i}"

    def mm(self, out, lhsT, rhs, start=True, stop=True, r=(), w=(), nosig=False):
        def fn(e):
            return e.matmul(out, lhsT=lhsT, rhs=rhs, start=start, stop=stop)
        return self.S.op("pe", fn, reads=r, writes=w, nosig=nosig)

    def act(self, out, in_, func, r=(), w=(), scale=1.0, bias=0.0, eng="act"):
        def fn(e):
            return e.activation(out=out, in_=in_, func=func, scale=scale, bias=bias)
        return self.S.op("act", fn, reads=r, writes=w)

    def tt(self, eng, out, in0, in1, op, r=(), w=()):
        def fn(e):
            return e.tensor_tensor(out=out, in0=in0, in1=in1, op=op)
        return self.S.op(eng, fn, reads=r, writes=w)

    def ts(self, eng, out, in0, s1, op0, s2=None, op1=None, r=(), w=()):
        def fn(e):
            if op1 is None:
                return e.tensor_scalar(out=out, in0=in0, scalar1=s1, scalar2=None, op0=op0)
            return e.tensor_scalar(out=out, in0=in0, scalar1=s1, scalar2=s2, op0=op0, op1=op1)
        return self.S.op(eng, fn, reads=r, writes=w)

    def stt(self, eng, out, in0, scalar, in1, op0, op1, r=(), w=()):
        eng = "dve"

        def fn(e):
            return e.scalar_tensor_tensor(out=out, in0=in0, scalar=scalar, in1=in1, op0=op0, op1=op1)
        return self.S.op(eng, fn, reads=r, writes=w)

    def cp(self, eng, out, in_, r=(), w=()):
        if eng == "act":
            return self.act(out, in_, AF.Copy, r=r, w=w)

        def fn(e):
            return e.tensor_copy(out=out, in_=in_)
        return self.S.op(eng, fn, reads=r, writes=w)

    def memset(self, eng, ap, val, w=()):
        def fn(e):
            return e.memset(ap, val)
        return self.S.op(eng, fn, writes=w)

    def dma(self, out, in_, r=(), w=(), eng="sp"):
        def fn(e):
            return e.dma_start(out=out, in_=in_)
        return self.S.dma(fn, reads=r, writes=w, eng=eng)

    def build(self):
        nc = self.nc
        cfg = self.cfg
        NB = cfg.get("nb", 2)
        NL = cfg.get("nl", 2)
        FAM = cfg.get("fam", "GFAM")
        d = {}
        d["x"] = self.din("x", [2, L, D])
        d["ctx"] = self.din("ctx", [2, LC, D])
        d["cT"] = self.din("cT", [128, 8, 3])
        d["w_mod"] = self.din("w_mod", [2, D, 6 * D])
        d["b_modT"] = self.din("b_modT", [128, 2, 48])
        d["n1T"] = self.din("n1T", [128, 2, 8])
        d["n2T"] = self.din("n2T", [128, 2, 8])
        d["w_in"] = self.din("w_in", [2, D, 2064])
        d["convT"] = self.din("convT", [128, 2, 6, 3])
        d["qgT"] = self.din("qgT", [128, 2])
        d["kgT"] = self.din("kgT", [128, 2])
        d["ogT"] = self.din("ogT", [128, 2])
        d["alog_b"] = self.din("alog_b", [64, 2, 8])
        d["dtb_b"] = self.din("dtb_b", [64, 2, 8])
        d["w_out"] = self.din("w_out", [2, D, D])
        d["w_ff1"] = self.din("w_ff1", [2, D, 4 * D])
        d["w_ff2"] = self.din("w_ff2", [2, 4 * D, D])
        d["ident"] = self.din("ident", [128, 128])
        d["blk1"] = self.din("blk1", [128, 128])
        d["rotm"] = self.din("rotm", [128, 128])
        d["chd"] = self.din("chd", [128, 256], BF16)
        d["cl"] = self.din("cl", [L, L], BF16)
        d["sl"] = self.din("sl", [L, L], BF16)
        d["cc"] = self.din("cc", [LC, LC], BF16)
        d["scx"] = self.din("scx", [LC, LC], BF16)
        d["ropec"] = self.din("ropec", [128, L])
        d["ropes"] = self.din("ropes", [128, L])
        d["gcon"] = self.din("gcon", [64, 5, 4, 64])
        d["out"] = self.dout("out", [2, L, D])
        self.d = d
        self.psum = [nc.alloc_psum_tensor(f"psum{i}", [128, 512], F32) for i in range(8)]

        o = 16640
        G = {}

        def galloc(name, shape, dt=F32):
            nonlocal o
            n = int(np.prod(shape[1:])) * (2 if dt == BF16 else 4)
            t = self.sb(name, shape, dt, o)
            o += (n + 31) // 32 * 32
            G[name] = t
            return t
        ident = galloc("ident", [128, 128])
        onesf = galloc("onesf", [128, 128])
        negones = galloc("negones", [64, 64])
        blk1 = galloc("blk1", [128, 128])
        rotm = galloc("rotm", [128, 128])
        chd = galloc("chd", [128, 256], BF16)
        gcon = galloc("gcon", [64, 5, 4, 64])
        modp = galloc("modp", [128, 2, 6, 8, 3])
        n1T = galloc("n1T", [128, 2, 8])
        n2T = galloc("n2T", [128, 2, 8])
        convT = galloc("convT", [128, 2, 6, 3])
        qgT = galloc("qgT", [128, 2])
        kgT = galloc("kgT", [128, 2])
        ogT = galloc("ogT", [128, 2])
        alogb = galloc("alogb", [64, 2, 8])
        dtbb = galloc("dtbb", [64, 2, 8])
        nexpa = galloc("nexpa", [64, 2, 8])
        self.epsb = galloc("epsb", [128, 1])
        self.scr = galloc("scr", [128, 8])
        assert o <= 16640 + 10240, o
        XB = 16640 + 10240
        xT = self.sb("xT", [128, 8, L], F32, XB)
        xcT = self.sb("xcT", [128, 8, LC], F32, XB + 8 * L * 4)
        PB = XB + 8 * L * 4 + 8 * LC * 4
        self.PB = PB
        self.G = G
        self.xT, self.xcT = xT, xcT

        for nm in ["ident", "blk1", "rotm", "chd", "gcon", "n1T", "n2T", "convT", "qgT", "kgT", "ogT"]:
            self.dma(G[nm][:], d[nm], w=[nm])
        self.dma(alogb[:], d["alog_b"], w=["alogb"])
        self.dma(dtbb[:], d["dtb_b"], w=["dtbb"])
        self.memset("pool", onesf[:], 1.0, w=["onesf"])
        self.memset("pool", negones[:], -1.0, w=["negones"])
        self.act(nexpa[:], alogb[:], AF.Exp, r=["alogb"], w=["nexpa"])
        self.ts("pool", nexpa[:], nexpa[:], -1.0, ALU.mult, r=["nexpa"], w=["nexpa"])

        self.phase_mod()
        self.barrier()
        for b in range(NB):
            self.load_x(b)
            self.barrier()
            for l in range(NL):
                last = (l == 1)
                if "G" in FAM:
                    self.phase_gdn(b, l, last)
                    self.barrier()
                self.phase_h(b, l, last, gdn=("G" in FAM and cfg.get("gstage", 0) in (0, 5)))
                self.barrier()
                if "F" in FAM:
                    self.phase_fourier(b, l, last)
                    self.barrier()
                if "A" in FAM:
                    self.phase_attn(b, l, last)
                    self.barrier()
                if "M" in FAM:
                    self.phase_mlp(b, l, last)
                    self.barrier()
            self.store_x(b)
            self.barrier()
        self.S.finish()
        self.S.emit()
        return nc

    def barrier(self):
        scr, ident = self.scr, self.G["ident"]
        self.mm(self.psum[7][0:1, 0:1], ident[0:1, 0:1], ident[0:1, 0:1], r=["ident"], w=["ps7", "bar_pe"])
        self.memset("dve", scr[:, 0:1], 0.0, w=["bar_dve"])
        self.memset("pool", scr[:, 1:2], 0.0, w=["bar_pool"])
        self.act(scr[:, 2:3], self.epsb[:, 0:1], AF.Copy, r=["epsb"], w=["bar_act"])
        for e in Sched.ALL:
            self.S.wait_keys(e, ["bar_pe", "bar_dve", "bar_pool", "bar_act"])

    def xs(self, isctx, c, t0, n):
        return (self.xcT if isctx else self.xT)[:, c, t0:t0 + n]

    def xkey(self, isctx, t0):
        return ("xc" if isctx else f"x{t0 // 512}")

    def load_w(self, dst, dkey, src, stg, kc, ncols, c0, eng="pool"):
        srcv = src.rearrange("(k p) n -> p k n", p=128)
        per = max(1, 2048 // ncols)
        k = 0
        while k < kc:
            kk = min(per, kc - k)
            st, skey = stg[self.stg_i % len(stg)]
            self.stg_i += 1
            sv = st[:, 0:kk * ncols].rearrange("p (k n) -> p k n", n=ncols)
            self.dma(sv, srcv[:, k:k + kk, c0:c0 + ncols], w=[skey])
            self.cp(eng, dst[:, k:k + kk, 0:ncols], sv, r=[skey], w=[dkey])
            k += kk

    def norm_block(self, b, l, which, isctx, t0, n, hout, hkey, tmp):
        G = self.G
        r = 2 if isctx else b
        kA, kB = (0, 1) if which == 1 else (3, 4)
        modp = G["modp"]
        xk = self.xkey(isctx, t0)
        xk2 = self.xkey(isctx, t0 + n - 1)
        xr = [xk] if xk == xk2 else [xk, xk2]
        sq, rstd, xn = tmp["sq"], tmp["rstd"], tmp["xn"]
        pt, pk = self.ps()
        for c in range(8):
            s, sk = sq[c % len(sq)]
            self.act(s[:, 0:n], self.xs(isctx, c, t0, n), AF.Square, r=xr, w=[sk])
            self.mm(pt[:, 0:n], G["onesf"][:], s[:, 0:n], start=(c == 0), stop=(c == 7), r=[sk, "onesf"], w=[pk],
                    nosig=(c != 7))
        rs, rk = rstd
        self.act(rs[:, 0:n], pt[:, 0:n], AF.Ln, r=[pk], w=[rk], scale=1.0 / D, bias=self.epsb[:, 0:1])
        self.act(rs[:, 0:n], rs[:, 0:n], AF.Exp, r=[rk], w=[rk], scale=-0.5)
        for c in range(8):
            t, tk = xn[c % len(xn)]
            self.tt("dve", t[:, 0:n], self.xs(isctx, c, t0, n), rs[:, 0:n], ALU.mult, r=xr + [rk], w=[tk])
            self.ts("pool", hout[:, c, 0:n], t[:, 0:n], modp[:, l, kA, c, r:r + 1], ALU.mult,
                    modp[:, l, kB, c, r:r + 1], ALU.add, r=[tk, "modp"], w=[hkey])

    def phase_mod(self):
        G, d, PB = self.G, self.d, self.PB
        modp = G["modp"]
        self.memset("pool", self.epsb[:], EPS, w=["epsb"])
        cT = self.sb("cT", [128, 8, 3], F32, PB + 64)
        bmT = self.sb("bmT", [128, 2, 48], F32, PB + 256)
        mraw = self.sb("mraw", [128, 2, 48, 3], F32, PB + 1024)
        stg = [(self.sb(f"mstg{i}", [128, 8, 256], F32, PB + 4096 + i * 8192), f"mstg{i}") for i in range(4)]
        self.dma(cT[:], d["cT"], w=["cT"])
        self.dma(bmT[:], d["b_modT"], w=["bmT"])
        self.act(cT[:], cT[:], AF.Silu, r=["cT"], w=["cT"])
        si = 0
        for l in range(2):
            wv = d["w_mod"][l].rearrange("(k p) n -> p k n", p=128)
            pt, pk = self.ps()
            for piece in range(24):
                st, sk = stg[si % 4]
                si += 1
                self.dma(st[:], wv[:, :, piece * 256:(piece + 1) * 256], w=[sk])
                for jj in range(2):
                    j = piece * 2 + jj
                    for k in range(8):
                        self.mm(pt[:, j * 3:(j + 1) * 3], st[:, k, jj * 128:(jj + 1) * 128], cT[:, k, :],
                                start=(k == 0), stop=(k == 7), r=[sk, "cT"], w=[pk], nosig=not (k == 7 and jj == 1))
            self.tt("dve", mraw[:, l, :, :], pt[:, 0:144].rearrange("p (j r) -> p j r", r=3),
                    bmT[:, l, :].unsqueeze(2).broadcast_to([128, 48, 3]), ALU.add, r=[pk, "bmT"], w=["mraw"])
            for (kind, scj, shj, gj, nrm) in ((0, 8, 0, 16, "n1T"), (3, 32, 24, 40, "n2T")):
                self.ts("dve", modp[:, l, kind, :, :], mraw[:, l, scj:scj + 8, :], 1.0, ALU.add, r=["mraw"], w=["modp"])
                self.tt("dve", modp[:, l, kind, :, :], modp[:, l, kind, :, :],
                        G[nrm][:, l, :].unsqueeze(2).broadcast_to([128, 8, 3]), ALU.mult, r=["modp", nrm], w=["modp"])
                self.cp("dve", modp[:, l, kind + 1, :, :], mraw[:, l, shj:shj + 8, :], r=["mraw"], w=["modp"])
                self.cp("dve", modp[:, l, kind + 2, :, :], mraw[:, l, gj:gj + 8, :], r=["mraw"], w=["modp"])

    def load_x(self, b):
        G, d, PB = self.G, self.d, self.PB
        stg = [(self.sb(f"xstg{i}", [128, D], F32, PB + 64 + i * 4096), f"xstg{i}") for i in range(4)]
        si = 0
        for isctx, n_t in ((False, L), (True, LC)):
            src = d["ctx"][b] if isctx else d["x"][b]
            for t0 in range(0, n_t, 512):
                nt = min(512, n_t - t0)
                tiles = []
                for j in range(nt // 128):
                    st, sk = stg[si % 4]
                    si += 1
                    self.dma(st[:], src[t0 + j * 128:t0 + (j + 1) * 128, :], w=[sk])
                    tiles.append((st, sk))
                for c in range(8):
                    pt, pk = self.ps()
                    for j, (st, sk) in enumerate(tiles):
                        self.mm(pt[:, j * 128:(j + 1) * 128], st[:, c * 128:(c + 1) * 128], G["ident"][:],
                                r=[sk, "ident"], w=[pk], nosig=(j != len(tiles) - 1))
                    eng = "dve" if c % 2 == 0 else "act"
                    self.cp(eng, self.xs(isctx, c, t0, nt), pt[:, 0:nt], r=[pk], w=[self.xkey(isctx, t0)])

    def store_x(self, b):
        G, d, PB = self.G, self.d, self.PB
        stg = [(self.sb(f"ostg{i}", [128, D], F32, PB + 64 + i * 4096), f"xstg{i}") for i in range(4)]
        si = 0
        for tt in range(L // 128):
            st, sk = stg[si % 4]
            si += 1
            for half in range(2):
                pt, pk = self.ps()
                for cc in range(4):
                    c = half * 4 + cc
                    self.mm(pt[:, cc * 128:(cc + 1) * 128], self.xT[:, c, tt * 128:(tt + 1) * 128], G["ident"][:],
                            r=[self.xkey(False, tt * 128), "ident"], w=[pk], nosig=(cc != 3))
                eng = "dve" if half == 0 else "act"
                self.cp(eng, st[:, half * 512:(half + 1) * 512], pt[:, :], r=[pk], w=[sk])
            self.dma(d["out"][b][tt * 128:(tt + 1) * 128, :], st[:], r=[sk])

    TB = ((False, 0, 512), (False, 512, 512), (False, 1024, 512), (False, 1536, 512), (True, 0, 256))

    @staticmethod
    def hoff(isctx, t0):
        return (L + t0) if isctx else t0

    def norm_tmp(self, base):
        sq = [(self.sb(f"sq{i}", [128, 512], F32, base + i * 2048), f"sq{i}") for i in range(3)]
        rstd = (self.sb("rstd", [128, 512], F32, base + 6144), "rstd")
        xn = [(self.sb(f"xn{i}", [128, 512], F32, base + 8192 + i * 2048), f"xn{i}") for i in range(2)]
        return {"sq": sq, "rstd": rstd, "xn": xn}

    def residual(self, l, gkind, r, isctx, t0, n, pt, pk, c, eng="dve"):
        xa = self.xs(isctx, c, t0, n)
        xk = self.xkey(isctx, t0)
        self.stt(eng, xa, pt, self.G["modp"][:, l, gkind, c, r:r + 1], xa, ALU.mult, ALU.add,
                 r=[pk, "modp", xk], w=[xk])

    def phase_h(self, b, l, last, gdn):
        PB, d = self.PB, self.d
        hT = self.sb("hT", [128, 8, L + LC], BF16, PB)
        self.hT = hT
        o = PB + 36864
        tmp = self.norm_tmp(o)
        o += 12288
        for (isctx, t0, n) in self.TB:
            self.norm_block(b, l, 1, isctx, t0, n, hT[:, :, self.hoff(isctx, t0):self.hoff(isctx, t0) + n],
                            f"hT{self.hoff(isctx, t0) // 512}", tmp)
        if gdn:
            catD = self.catD
            self.stg_i = 0
            stg = [(self.sb(f"hstg{i}", [128, 2048], F32, o + i * 8192), f"stg{i}") for i in range(2)]
            o += 16384
            wo = self.sb("wo_d", [128, 2, D], BF16, o)
            self.load_w(wo, "wo", d["w_out"][l][768:1024, :], stg, 2, 1024, 0)
            self.out_proj(b, l, last, wo, "wo", 2, lambda j, off, n: catD[:, j, off:off + n], ["catD"])

    def out_proj(self, b, l, last, wo, wokey, nk, catfn, catkeys, blocks=None):
        for (isctx, t0, n) in (blocks or self.TB):
            if isctx and last:
                continue
            r = 2 if isctx else b
            off = self.hoff(isctx, t0)
            for c in range(8):
                pt, pk = self.ps()
                for j in range(nk):
                    self.mm(pt[:, 0:n], wo[:, j, c * 128:(c + 1) * 128], catfn(j, off, n), start=(j == 0),
                            stop=(j == nk - 1), r=[wokey] + catkeys, w=[pk], nosig=(j != nk - 1))
                self.residual(l, 2, r, isctx, t0, n, pt[:, 0:n], pk, c, eng="dve")

    def phase_mlp(self, b, l, last):
        PB, d = self.PB, self.d
        h2 = self.sb("h2T", [128, 8, L + LC], BF16, PB)
        o = PB + 36864
        tmp = self.norm_tmp(o)
        o += 12288
        self.stg_i = 0
        stg = [(self.sb(f"mstg{i}", [128, 2048], F32, o + i * 8192), f"stg{i}") for i in range(2)]
        o += 16384
        w1 = [(self.sb(f"w1e{i}", [128, 8, 512], BF16, o + i * 8192), f"w1e{i}") for i in range(2)]
        o += 16384
        w2 = [(self.sb(f"w2e{i}", [128, 4, D], BF16, o + i * 8192), f"w2e{i}") for i in range(2)]
        o += 16384
        uT = [(self.sb(f"uT{i}", [128, 4, 512], BF16, o + i * 4096), f"uT{i}") for i in range(2)]
        o += 8192
        blocks = [tb for tb in self.TB if not (tb[0] and last)]
        for (isctx, t0, n) in blocks:
            off = self.hoff(isctx, t0)
            self.norm_block(b, l, 2, isctx, t0, n, h2[:, :, off:off + n], f"h2T{off // 512}", tmp)
        ui = 0
        for e8 in range(8):
            w1t, w1k = w1[e8 % 2]
            w2t, w2k = w2[e8 % 2]
            self.load_w(w1t, w1k, d["w_ff1"][l], stg, 8, 512, e8 * 512)
            self.load_w(w2t, w2k, d["w_ff2"][l][e8 * 512:(e8 + 1) * 512, :], stg, 4, 1024, 0)
            for (isctx, t0, n) in blocks:
                off = self.hoff(isctx, t0)
                r = 2 if isctx else b
                ut, uk = uT[ui % 2]
                ui += 1
                for fc in range(4):
                    pt, pk = self.ps()
                    for k in range(8):
                        self.mm(pt[:, 0:n], w1t[:, k, fc * 128:(fc + 1) * 128], h2[:, k, off:off + n], start=(k == 0),
                                stop=(k == 7), r=[w1k, f"h2T{off // 512}"], w=[pk], nosig=(k != 7))
                    tq, tk = tmp["sq"][fc % 3]
                    self.act(tq[:, 0:n], pt[:, 0:n], AF.Relu, r=[pk], w=[tk])
                    self.tt("pool", ut[:, fc, 0:n], tq[:, 0:n], tq[:, 0:n], ALU.mult, r=[tk], w=[uk])
                for c in range(8):
                    pt, pk = self.ps()
                    for fc in range(4):
                        self.mm(pt[:, 0:n], w2t[:, fc, c * 128:(c + 1) * 128], ut[:, fc, 0:n], start=(fc == 0),
                                stop=(fc == 3), r=[w2k, uk], w=[pk], nosig=(fc != 3))
                    self.residual(l, 5, r, isctx, t0, n, pt[:, 0:n], pk, c)

    def phase_fourier(self, b, l, last):
        PB, d, G = self.PB, self.d, self.G
        hT = self.hT
        o = PB + 36864
        self.stg_i = 0
        stg = [(self.sb(f"fstg{i}", [128, 2048], F32, o + i * 8192), f"stg{i}") for i in range(2)]
        o += 16384
        wf = self.sb("wf", [128, 8, 256], BF16, o)
        o += 4096
        wo = self.sb("wo_f", [128, 2, D], BF16, o)
        o += 4096
        fT = [(self.sb(f"fT{i}", [128, 2, 512], BF16, o + i * 2048), f"fT{i}") for i in range(2)]
        o += 4096
        Gt = self.sb("Gt", [128, 18, 2, 256], BF16, o)
        o += 18432
        dft = [(self.sb(f"dft{i}", [128, 2, 16, 256], BF16, o + i * 16384), f"dft{i}") for i in range(2)]
        o += 32768
        catF = [(self.sb(f"catF{i}", [128, 2, 256], BF16, o + i * 1024), f"catF{i}") for i in range(2)]
        o += 2048
        assert o <= 229376
        self.load_w(wf, "wf", d["w_in"][l], stg, 8, 256, 0)
        self.load_w(wo, "wo", d["w_out"][l][0:256, :], stg, 2, 1024, 0)
        blocks = [tb for tb in self.TB if not (tb[0] and last)]
        fi = 0
        for (isctx, t0, n) in blocks:
            off = self.hoff(isctx, t0)
            ft, fk = fT[fi % 2]
            fi += 1
            for ch in range(2):
                pt, pk = self.ps()
                for k in range(8):
                    self.mm(pt[:, 0:n], wf[:, k, ch * 128:(ch + 1) * 128], hT[:, k, off:off + n], start=(k == 0),
                            stop=(k == 7), r=["wf", f"hT{off // 512}"], w=[pk], nosig=(k != 7))
                self.cp("act" if ch == 0 else "dve", ft[:, ch, 0:n], pt[:, 0:n], r=[pk], w=[fk])
            for j in range(n // 128):
                tile = off // 128 + j
                pt, pk = self.ps()
                for ch in range(2):
                    self.mm(pt[:, ch * 256:(ch + 1) * 256], ft[:, ch, j * 128:(j + 1) * 128], G["chd"][:],
                            r=[fk, "chd"], w=[pk], nosig=(ch == 0))
                self.cp("act" if j % 2 == 0 else "dve", Gt[:, tile, :, :].rearrange("p c n -> p (c n)"), pt[:, :],
                        r=[pk], w=["Gt"])
        di = 0
        ci = 0
        for (isctx, nl, tile0, ctab, stab) in ((False, L, 0, d["cl"], d["sl"]), (True, LC, 16, d["cc"], d["scx"])):
            if isctx and last:
                continue
            nlt = nl // 128
            cv = ctab.rearrange("(k p) n -> p k n", p=128)
            sv = stab.rearrange("(k p) n -> p k n", p=128)
            for lb in range(nl // 256):
                dt_, dk = dft[di % 2]
                di += 1
                self.dma(dt_[:, 0, 0:nlt, :], cv[:, :, lb * 256:(lb + 1) * 256], w=[dk])
                self.dma(dt_[:, 1, 0:nlt, :], sv[:, :, lb * 256:(lb + 1) * 256], w=[dk])
                ct, ck = catF[ci % 2]
                ci += 1
                for ch in range(2):
                    pt, pk = self.ps()
                    nmm = 2 * nlt
                    i = 0
                    for lt in range(nlt):
                        for cs in range(2):
                            self.mm(pt[:, 0:256], Gt[:, tile0 + lt, ch, cs * 128:(cs + 1) * 128], dt_[:, cs, lt, :],
                                    start=(i == 0), stop=(i == nmm - 1), r=["Gt", dk], w=[pk], nosig=(i != nmm - 1))
                            i += 1
                    self.cp("act" if ch == 0 else "dve", ct[:, ch, :], pt[:, 0:256], r=[pk], w=[ck])
                self.out_proj(b, l, last, wo, "wo", 2, lambda j, off, n, ct=ct: ct[:, j, 0:n], [ck],
                              blocks=[(isctx, lb * 256, 256)])

    def qk_post(self, pt, pk, n, gT, l, rope_off, dst, dkey, tmp, rope):
        G = self.G
        sq, sk = tmp["sq"]
        self.act(sq[:, 0:n], pt[:, 0:n], AF.Square, r=[pk], w=[sk])
        p2, p2k = self.ps()
        self.mm(p2[:, 0:n], G["blk1"][:], sq[:, 0:n], r=[sk, "blk1"], w=[p2k])
        rs, rk = tmp["rstd"]
        self.act(rs[:, 0:n], p2[:, 0:n], AF.Ln, r=[p2k], w=[rk], scale=1.0 / 64, bias=self.epsb[:, 0:1])
        self.act(rs[:, 0:n], rs[:, 0:n], AF.Exp, r=[rk], w=[rk], scale=-0.5)
        qn, qk = tmp["qn"]
        if rope_off is None:
            self.stt("dve", dst, pt[:, 0:n], gT[:, l:l + 1], rs[:, 0:n], ALU.mult, ALU.mult, r=[pk, rk], w=[dkey])
            return
        self.stt("dve", qn[:, 0:n], pt[:, 0:n], gT[:, l:l + 1], rs[:, 0:n], ALU.mult, ALU.mult, r=[pk, rk], w=[qk])
        p3, p3k = self.ps()
        self.mm(p3[:, 0:n], G["rotm"][:], qn[:, 0:n], r=[qk, "rotm"], w=[p3k])
        (ct, st), rpk = rope
        t1, t1k = tmp["t1"]
        t2, t2k = tmp["t2"]
        self.tt("pool", t1[:, 0:n], qn[:, 0:n], ct[:, 0:n], ALU.mult, r=[qk, rpk], w=[t1k])
        self.tt("dve", t2[:, 0:n], p3[:, 0:n], st[:, 0:n], ALU.mult, r=[p3k, rpk], w=[t2k])
        self.tt("pool", dst, t1[:, 0:n], t2[:, 0:n], ALU.add, r=[t1k, t2k], w=[dkey])

    def phase_attn(self, b, l, last):
        PB, d, G = self.PB, self.d, self.G
        hT = self.hT
        o = PB + 36864
        self.stg_i = 0
        stg = [(self.sb(f"astg{i}", [128, 2048], F32, o + i * 8192), f"stg{i}") for i in range(1)]
        o += 8192
        wq = self.sb("wq", [128, 8, 256], BF16, o); o += 4096
        wk = self.sb("wk", [128, 8, 128], BF16, o); o += 2048
        wv = self.sb("wv", [128, 8, 64], BF16, o); o += 1024
        wo = self.sb("wo_a", [128, 2, D], BF16, o); o += 4096
        qT = self.sb("qT", [128, 2, L + LC], BF16, o); o += 9216
        kT = self.sb("kT", [128, L + LC], BF16, o); o += 4608
        VA = self.sb("VA", [128, 18, 128], BF16, o); o += 4608
        VB = self.sb("VB", [128, 18, 128], BF16, o); o += 4608
        PT = [(self.sb(f"PT{i}", [128, 18, 256], BF16, o + i * 9216), f"PT{i}") for i in range(2)]
        o += 18432
        tmp = {}
        for i, nm in enumerate(["sq", "rstd", "qn", "t1", "t2"]):
            tmp[nm] = (self.sb(f"a_{nm}", [128, 512], F32, o + i * 2048), f"a_{nm}")
        o += 10240
        ropeb = []
        for i in range(2):
            ropeb.append(((self.sb(f"rc{i}", [128, 512], F32, o + i * 4096), self.sb(f"rs{i}", [128, 512], F32, o + i * 4096 + 2048)),
                          f"rope{i}"))
        o += 8192
        catA = [(self.sb(f"catA{i}", [128, 2, 256], BF16, o + i * 1024), f"catA{i}") for i in range(2)]
        o += 2048
        rsum = [(self.sb(f"rsum{i}", [128, 256], F32, o + i * 1024), f"rsum{i}") for i in range(2)]
        o += 2048
        assert o <= 229376, o
        ri = 0
        for g in range(2):
            self.load_w(wq, "wq", d["w_in"][l], stg, 8, 256, 256 + g * 256)
            for half in range(2):
                srcv = d["w_in"][l].rearrange("(k p) n -> p k n", p=128)
                st, skey = stg[self.stg_i % len(stg)]
                self.stg_i += 1
                sv = st[:, 0:512].rearrange("p (k n) -> p k n", n=64)
                self.dma(sv, srcv[:, :, 768 + g * 64:768 + (g + 1) * 64], w=[skey])
                self.cp("pool", wk[:, :, half * 64:(half + 1) * 64], sv, r=[skey], w=["wk"])
            self.load_w(wv, "wv", d["w_in"][l], stg, 8, 64, 896 + g * 64)
            self.load_w(wo, "wo", d["w_out"][l][256 + g * 256:256 + (g + 1) * 256, :], stg, 2, 1024, 0)
            self.memset("pool", VA[:, :, 64:128], 1.0, w=["VA"])
            self.memset("pool", VB[:, :, 0:64], 1.0, w=["VB"])
            for (isctx, t0, n) in self.TB:
                off = self.hoff(isctx, t0)
                hk = f"hT{off // 512}"
                rope = None
                if not isctx:
                    rope = ropeb[ri % 2]
                    ri += 1
                    self.dma(rope[0][0][:, 0:n], d["ropec"][:, t0:t0 + n], w=[rope[1]])
                    self.dma(rope[0][1][:, 0:n], d["ropes"][:, t0:t0 + n], w=[rope[1]])
                pt, pk = self.ps()
                for k in range(8):
                    self.mm(pt[:, 0:n], wk[:, k, :], hT[:, k, off:off + n], start=(k == 0), stop=(k == 7),
                            r=["wk", hk], w=[pk], nosig=(k != 7))
                self.qk_post(pt, pk, n, G["kgT"], l, None if isctx else t0, kT[:, off:off + n], "kT", tmp, rope)
                if not (isctx and last):
                    for qc in range(2):
                        pt, pk = self.ps()
                        for k in range(8):
                            self.mm(pt[:, 0:n], wq[:, k, qc * 128:(qc + 1) * 128], hT[:, k, off:off + n], start=(k == 0),
                                    stop=(k == 7), r=["wq", hk], w=[pk], nosig=(k != 7))
                        self.qk_post(pt, pk, n, G["qgT"], l, None if isctx else t0, qT[:, qc, off:off + n], "qT", tmp, rope)
                for j in range(n // 128):
                    tile = off // 128 + j
                    pt, pk = self.ps()
                    for k in range(8):
                        self.mm(pt[:, 0:64], hT[:, k, off + j * 128:off + (j + 1) * 128], wv[:, k, :], start=(k == 0),
                                stop=(k == 7), r=["wv", hk], w=[pk], nosig=(k != 7))
                    self.cp("act", VA[:, tile, 0:64], pt[:, 0:64], r=[pk], w=["VA"])
                    self.cp("dve", VB[:, tile, 64:128], pt[:, 0:64], r=[pk], w=["VB"])
            qblocks = [(False, q0) for q0 in range(0, L, 256)]
            if not last:
                qblocks.append((True, 0))
            pi = 0
            ci = 0
            for (isctx, q0) in qblocks:
                qoff = self.hoff(isctx, q0)
                ktiles = list(range(16, 18)) if isctx else list(range(18))
                ct, ck = catA[ci % 2]
                ci += 1
                pend = None
                heads = list(range(4))
                for hh in heads + [None]:
                    if hh is not None:
                        qc, hl = hh // 2, hh % 2
                        P, Pk = PT[pi % 2]
                        pi += 1
                        pr = slice(hl * 64, (hl + 1) * 64)
                        for ii in range(0, len(ktiles), 2):
                            pt, pk = self.ps()
                            kk = ktiles[ii:ii + 2]
                            for jj, kt in enumerate(kk):
                                self.mm(pt[:, jj * 256:(jj + 1) * 256], kT[pr, kt * 128:(kt + 1) * 128],
                                        qT[pr, qc, qoff:qoff + 256], r=["kT", "qT"], w=[pk], nosig=(jj != len(kk) - 1))
                            self.act(P[:, kk[0]:kk[0] + len(kk), :].rearrange("p a n -> p (a n)"), pt[:, 0:256 * len(kk)],
                                     AF.Exp, r=[pk], w=[Pk], scale=0.125)
                    if pend is not None:
                        (phh, pP, pPk) = pend
                        pqc, phl = phh // 2, phh % 2
                        Vt, Vk = (VA, "VA") if phl == 0 else (VB, "VB")
                        pt, pk = self.ps()
                        for ii, kt in enumerate(ktiles):
                            self.mm(pt[:, 0:256], Vt[:, kt, :], pP[:, kt, :], start=(ii == 0), stop=(ii == len(ktiles) - 1),
                                    r=[Vk, pPk], w=[pk], nosig=(ii != len(ktiles) - 1))
                        orow = slice(phl * 64, (phl + 1) * 64)
                        srow = slice((1 - phl) * 64, (2 - phl) * 64)
                        rs_, rsk = rsum[phh % 2]
                        self.cp("act", rs_[orow, :], pt[srow, 0:256], r=[pk], w=[rsk])
                        self.S.op("dve", (lambda e, a=rs_[orow, :]: e.reciprocal(out=a, in_=a)), reads=[rsk], writes=[rsk])
                        self.tt("dve", ct[orow, pqc, :], pt[orow, 0:256], rs_[orow, :], ALU.mult, r=[pk, rsk], w=[ck])
                    pend = (hh, P, Pk) if hh is not None else None
                self.out_proj(b, l, last, wo, "wo", 2, lambda j, off, n, ct=ct: ct[:, j, 0:n], [ck],
                              blocks=[(isctx, q0, 256)])

    def phase_gdn(self, b, l, last):
        PB, d, G = self.PB, self.d, self.G
        END = 229376 - 2048
        NT = L + LC
        catD = self.sb("catD", [128, 2, NT], BF16, END - 9216)
        self.catD = catD
        gcon = G["gcon"]
        tri4, minc4, mincT4, id4, ctri4 = (gcon[:, i, :, :] for i in range(5))
        ident, ones64, negones = G["ident"], G["onesf"][0:64, 0:64], G["negones"]
        for hp in range(2):
            o = PB
            qkv = self.sb("qkv", [128, 3, NT], F32, o); o += 27648
            dzs = self.sb("dzs", [128, NT], BF16, o); o += 4608
            G64 = self.sb("G64", [64, 36, 16], F32, o); o += 2304
            BETA = self.sb("BETA", [64, 36, 8], F32, o); o += 1152
            GG = self.sb("GG", [64, 36, 8], F32, o); o += 1152
            OV = o
            self.stg_i = 0
            stg = [(self.sb("gstg", [128, 2048], F32, o), "stg0")]; o += 8192
            win = self.sb("win", [128, 8, 528], BF16, o); o += 8448
            hblk = self.sb("hblk", [128, 8, 450], BF16, o); o += 7232
            tmp = self.norm_tmp(o); o += 12288
            raw = self.sb("raw", [128, 3, 450], F32, o); o += 5632
            cvt = self.sb("cvt", [128, 3, 448], F32, o); o += 5632
            assert o <= END - 9216
            for j, c0 in enumerate((1024, 1280, 1536, 1792)):
                self.load_w(win[:, :, j * 128:(j + 1) * 128], "win", d["w_in"][l], stg, 8, 128, c0 + hp * 128)
            self.load_w(win[:, :, 512:528], "win", d["w_in"][l], stg, 8, 16, 2048)
            blocks = [(True, 0, 256)] + [(False, s0, min(448, L - s0)) for s0 in range(0, L, 448)]
            for (isctx, s0, m) in blocks:
                Ls = LC if isctx else L
                lo, hi = max(s0 - 1, 0), min(s0 + m + 1, Ls)
                ncol = hi - lo
                c_lo = lo - (s0 - 1)
                g0 = (0 if isctx else LC) + s0
                self.norm_block(b, l, 1, isctx, lo, ncol, hblk[:, :, 0:ncol], "hblk", tmp)
                if s0 == 0:
                    self.memset("pool", raw[:, :, 0:1], 0.0, w=["raw"])
                if s0 + m == Ls:
                    self.memset("pool", raw[:, :, m + 1:m + 2], 0.0, w=["raw"])
                for j in range(3):
                    pt, pk = self.ps()
                    for k in range(8):
                        self.mm(pt[:, 0:ncol], win[:, k, j * 128:(j + 1) * 128], hblk[:, k, 0:ncol], start=(k == 0),
                                stop=(k == 7), r=["win", "hblk"], w=[pk], nosig=(k != 7))
                    self.cp("act" if j != 1 else "dve", raw[:, j, c_lo:c_lo + ncol], pt[:, 0:ncol], r=[pk], w=["raw"])
                    cw = G["convT"][:, l, j * 2 + hp, :]
                    eng = "dve" if j != 1 else "pool"
                    self.ts(eng, cvt[:, j, 0:m], raw[:, j, 1:1 + m], cw[:, 1:2], ALU.mult, r=["raw", "convT"], w=[f"cvt{j}"])
                    self.stt(eng, cvt[:, j, 0:m], raw[:, j, 0:m], cw[:, 0:1], cvt[:, j, 0:m], ALU.mult, ALU.add,
                             r=["raw", "convT", f"cvt{j}"], w=[f"cvt{j}"])
                    self.stt(eng, cvt[:, j, 0:m], raw[:, j, 2:2 + m], cw[:, 2:3], cvt[:, j, 0:m], ALU.mult, ALU.add,
                             r=["raw", "convT", f"cvt{j}"], w=[f"cvt{j}"])
                    self.act(qkv[:, j, g0:g0 + m], cvt[:, j, 0:m], AF.Silu, r=[f"cvt{j}"], w=["qkv"])
                pt, pk = self.ps()
                for k in range(8):
                    self.mm(pt[:, 0:ncol], win[:, k, 384:512], hblk[:, k, 0:ncol], start=(k == 0), stop=(k == 7),
                            r=["win", "hblk"], w=[pk], nosig=(k != 7))
                cs = s0 - lo
                self.act(dzs[:, g0:g0 + m], pt[:, cs:cs + m], AF.Silu, r=[pk], w=["dzs"])
                pt, pk = self.ps()
                nch = m // 64
                for ci in range(nch):
                    for k in range(8):
                        self.mm(pt[0:64, ci * 16:(ci + 1) * 16], hblk[:, k, cs + ci * 64:cs + (ci + 1) * 64], win[:, k, 512:528],
                                start=(k == 0), stop=(k == 7), r=["win", "hblk"], w=[pk], nosig=not (k == 7 and ci == nch - 1))
                self.cp("dve", G64[:, g0 // 64:g0 // 64 + nch, :].rearrange("p a n -> p (a n)"), pt[0:64, 0:nch * 16],
                        r=[pk], w=["G64"])
            gstage = self.cfg.get("gstage", 0)
            if gstage == 1:
                self.barrier()
                continue
            for j in range(2 if gstage != 22 else 0):
                for t0 in range(0, NT, 512):
                    n = min(512, NT - t0)
                    sq, sk = tmp["sq"][(t0 // 512) % 3]
                    self.act(sq[:, 0:n], qkv[:, j, t0:t0 + n], AF.Square, r=["qkv"], w=[sk])
                    pt, pk = self.ps()
                    self.mm(pt[:, 0:n], G["blk1"][:], sq[:, 0:n], r=[sk, "blk1"], w=[pk])
                    rs, rk = tmp["rstd"]
                    self.act(rs[:, 0:n], pt[:, 0:n], AF.Ln, r=[pk], w=[rk], bias=self.epsb[:, 0:1])
                    self.act(rs[:, 0:n], rs[:, 0:n], AF.Exp, r=[rk], w=[rk], scale=-0.5)
                    if j == 0:
                        self.stt("dve", qkv[:, j, t0:t0 + n], qkv[:, j, t0:t0 + n], 0.125, rs[:, 0:n], ALU.mult, ALU.mult,
                                 r=["qkv", rk], w=["qkv"])
                    else:
                        self.tt("dve", qkv[:, j, t0:t0 + n], qkv[:, j, t0:t0 + n], rs[:, 0:n], ALU.mult, r=["qkv", rk], w=["qkv"])
            if gstage == 21:
                self.barrier()
                continue
            self.act(BETA[:], G64[:, :, 0:8], AF.Sigmoid, r=["G64"], w=["BETA"])
            self.tt("dve", GG[:], G64[:, :, 8:16], G["dtbb"][:, l, :].unsqueeze(1).broadcast_to([64, 36, 8]), ALU.add,
                    r=["G64", "dtbb"], w=["GG"])
            self.act(GG[:], GG[:], AF.Exp, r=["GG"], w=["GG"])
            self.act(GG[:], GG[:], AF.Ln, r=["GG"], w=["GG"], bias=1.0)
            self.tt("dve", GG[:], GG[:], G["nexpa"][:, l, :].unsqueeze(1).broadcast_to([64, 36, 8]), ALU.mult,
                    r=["GG", "nexpa"], w=["GG"])
            self.barrier()
            if gstage in (2, 22):
                continue
            o = OV

            def m4(name, shape=(64, 4, 64)):
                nonlocal o
                t = self.sb(name, list(shape), F32, o)
                o += (int(np.prod(shape[1:])) * 4 + 31) // 32 * 32
                return t
            obuf = m4("obuf", (64, 36, 128))
            Sst = m4("Sst")
            lane_base = o
            NLANE = self.cfg.get("lanes", 3)
            lanes = []
            for li in range(NLANE):
                Lb = {"i": li}
                Lb["tok"] = m4(f"tok{li}", (64, 4, 3, 64))
                for nm in ("Gd", "tD1", "tD2", "dgc", "Idec", "AT", "Bm", "QKmT", "QpT"):
                    Lb[nm] = m4(f"{nm}{li}")
                Lb["XY"] = [m4(f"XY{li}_{i}", (64, 2, 4, 64)) for i in range(2)]
                Lb["Rb"] = m4(f"Rb{li}", (64, 4, 128))
                Lb["ex12"] = m4(f"ex12{li}", (64, 12))
                for nm in ("b4", "nbe", "g4"):
                    Lb[nm] = m4(f"{nm}{li}", (64, 4))
                lanes.append(Lb)
            assert o <= END - 9216, o
            self.memset("pool", Sst[:], 0.0, w=["Sst"])

            def bc(ap4):
                return ap4.unsqueeze(2).broadcast_to([64, 4, 64])

            def chunks_of(s):
                nf = s
                nb_ = (3 - s) if s < 4 else (39 - s)
                return (nf, nb_)

            def pre(s, Lb):
                li = Lb["i"]

                def K(n):
                    return f"{n}_{li}"
                tok, Gd, tD1, tD2, dgc, Idec = Lb["tok"], Lb["Gd"], Lb["tD1"], Lb["tD2"], Lb["dgc"], Lb["Idec"]
                AT, Bm, QKmT, QpT, XY, Rb = Lb["AT"], Lb["Bm"], Lb["QKmT"], Lb["QpT"], Lb["XY"], Lb["Rb"]
                ex12, b4, nbe, g4 = Lb["ex12"], Lb["b4"], Lb["nbe"], Lb["g4"]
                kdec = Gd
                cnt = [0]

                def lps():
                    i = 2 * li + (cnt[0] % 2)
                    cnt[0] += 1
                    return self.psum[i], f"ps{i}"
                nchd = chunks_of(s)
                gcol = (2 * hp, 4 + 2 * hp)
                b4v = b4[:].rearrange("p (h d) -> p d h", d=2)
                g4v = g4[:].rearrange("p (h d) -> p d h", d=2)
                for dd in range(2):
                    n_ = nchd[dd]
                    self.cp("pool", b4v[:, dd, :], BETA[:, n_, gcol[dd]:gcol[dd] + 2], r=["BETA"], w=[K("b4")])
                    self.cp("pool", g4v[:, dd, :], GG[:, n_, gcol[dd]:gcol[dd] + 2], r=["GG"], w=[K("g4")])
                for hl in range(2):
                    pt, pk = lps()
                    pr = slice(hl * 64, (hl + 1) * 64)
                    for dd in range(2):
                        t0 = nchd[dd] * 64
                        for j in range(3):
                            self.mm(pt[0:64, (dd * 3 + j) * 64:(dd * 3 + j + 1) * 64], qkv[pr, j, t0:t0 + 64], ident[pr, pr],
                                    r=["qkv", "ident"], w=[pk], nosig=not (dd == 1 and j == 2))
                    self.cp("act" if hl == 0 else "dve", tok[:, 2 * hl:2 * hl + 2, :, :].rearrange("p a j n -> p (a j n)"),
                            pt[0:64, 0:384], r=[pk], w=[K("tok")])
                yield
                self.tt("pool", Gd[:], tri4, bc(g4[:]), ALU.mult, r=["gcon", K("g4")], w=[K("Gd")])
                yield
                pD, pDk = lps()
                self.mm(pD[0:64, 0:256], negones[:], Gd[:].rearrange("p a n -> p (a n)"), start=True, stop=False, r=[K("Gd"), "negones"], w=[pDk], nosig=True)
                for p in range(4):
                    self.mm(pD[0:64, p * 64:(p + 1) * 64], Gd[:, p, :], ones64, start=False, stop=True, r=[K("Gd"), "onesf"], w=[pDk], nosig=True)
                for p in range(4):
                    gsl = g4[:, p:p + 1]
                    self.mm(pD[0:64, 256 + p:257 + p], tri4[:, p, :], gsl, r=["gcon", K("g4")], w=[pDk], nosig=True)
                    self.mm(pD[0:64, 260 + p:261 + p], ctri4[:, p, :], gsl, r=["gcon", K("g4")], w=[pDk], nosig=True)
                self.mm(pD[0:64, 264:268], ones64, g4[:, 0:4], r=["onesf", K("g4")], w=[pDk])
                yield
                pDv = pD[0:64, 0:256].rearrange("p (a n) -> p a n", n=64)
                self.ts("dve", tD1[:], pDv, 0.0, ALU.min, r=[pDk], w=[K("tD1")])
                self.ts("dve", tD2[:], pDv, -1.0, ALU.mult, 0.0, ALU.min, r=[pDk], w=[K("tD2")])
                self.act(ex12[:], pD[0:64, 256:268], AF.Exp, r=[pDk], w=[K("ex12")])
                yield
                self.act(tD1[:], tD1[:], AF.Exp, r=[K("tD1")], w=[K("tD1")])
                self.act(tD2[:], tD2[:], AF.Exp, r=[K("tD2")], w=[K("tD2")])
                self.stt("dve", nbe[:], b4[:], -1.0, ex12[:, 0:4], ALU.mult, ALU.mult, r=[K("b4"), K("ex12")], w=[K("nbe")])
                yield
                self.tt("pool", tD1[:], tD1[:], minc4, ALU.mult, r=[K("tD1"), "gcon"], w=[K("tD1")])
                self.tt("pool", tD2[:], tD2[:], mincT4, ALU.mult, r=[K("tD2"), "gcon"], w=[K("tD2")])
                self.tt("pool", Rb[:, :, 0:64], tok[:, :, 2, :], bc(b4[:]), ALU.mult, r=[K("tok"), K("b4")], w=[K("Rb")])
                yield
                self.tt("pool", tD1[:], tD1[:], id4, ALU.subtract, r=[K("tD1"), "gcon"], w=[K("tD1")])
                self.tt("pool", Rb[:, :, 64:128], tok[:, :, 1, :], bc(nbe[:]), ALU.mult, r=[K("tok"), K("nbe")], w=[K("Rb")])
                yield
                X0, Y0 = XY[0][:, 0, :, :], XY[0][:, 1, :, :]
                for hl in range(2):
                    pK, pKk = lps()
                    pr = slice(hl * 64, (hl + 1) * 64)
                    for dd in range(2):
                        t0 = nchd[dd] * 64
                        self.mm(pK[0:64, (2 * dd) * 64:(2 * dd + 1) * 64], qkv[pr, 1, t0:t0 + 64], qkv[pr, 1, t0:t0 + 64], r=["qkv"], w=[pKk], nosig=True)
                        self.mm(pK[0:64, (2 * dd + 1) * 64:(2 * dd + 2) * 64], qkv[pr, 1, t0:t0 + 64], qkv[pr, 0, t0:t0 + 64], r=["qkv"], w=[pKk],
                                nosig=(dd != 1))
                    pKv = pK[0:64, 0:256].rearrange("p (a t n) -> p a t n", t=2, n=64)
                    ps_ = slice(2 * hl, 2 * hl + 2)
                    self.tt("dve", XY[0][:, 0, ps_, :], pKv[:, :, 0, :], tD1[:, ps_, :], ALU.mult, r=[pKk, K("tD1")], w=[K("XY0")])
                    self.tt("dve", QKmT[:, ps_, :], pKv[:, :, 1, :], tD2[:, ps_, :], ALU.mult, r=[pKk, K("tD2")], w=[K("QKmT")])
                self.tt("pool", kdec[:], tok[:, :, 1, :], bc(ex12[:, 4:8]), ALU.mult, r=[K("tok"), K("ex12")], w=[K("Gd")])
                yield
                self.stt("dve", X0, X0, -1.0, bc(b4[:]), ALU.mult, ALU.mult, r=[K("XY0"), K("b4")], w=[K("XY0")])
                self.tt("pool", dgc[:], id4, bc(ex12[:, 0:4]), ALU.mult, r=["gcon", K("ex12")], w=[K("dgc")])
                self.tt("pool", Idec[:], id4, bc(ex12[:, 8:12]), ALU.mult, r=["gcon", K("ex12")], w=[K("Idec")])
                yield
                pY, pYk = lps()
                for p in range(4):
                    self.mm(pY[0:64, p * 64:(p + 1) * 64], XY[0][:, 0, p, :], ident[0:64, 0:64], r=[K("XY0"), "ident"], w=[pYk], nosig=(p != 3))
                yield
                self.cp("act", Y0, pY[0:64, 0:256].rearrange("p (a n) -> p a n", n=64), r=[pYk], w=[K("XY0")])
                yield
                for kk in range(6):
                    cur, nxt = XY[kk % 2], XY[(kk + 1) % 2]
                    ck, nk = K(f"XY{kk % 2}"), K(f"XY{(kk + 1) % 2}")
                    pR, pRk = lps()
                    for p in range(4):
                        self.mm(pR[0:64, p * 128:(p + 1) * 128], cur[:, 1, p, :], Rb[:, p, :], r=[ck, K("Rb")], w=[pRk], nosig=(p != 3))
                    if kk < 5:
                        pS, pSk = lps()
                        for p in range(4):
                            self.mm(pS[0:64, p * 64:(p + 1) * 64], cur[:, 1, p, :], cur[:, 0, p, :], r=[ck], w=[pSk], nosig=True)
                            self.mm(pS[0:64, (4 + p) * 64:(5 + p) * 64], cur[:, 0, p, :], cur[:, 1, p, :], r=[ck], w=[pSk], nosig=(p != 3))
                    yield
                    if kk < 5:
                        self.cp("act", nxt[:].rearrange("p t a n -> p (t a n)"), pS[0:64, :], r=[pSk], w=[nk])
                    self.tt("dve", Rb[:].rearrange("p a n -> p (a n)"), pR[0:64, :], Rb[:].rearrange("p a n -> p (a n)"), ALU.add,
                            r=[pRk, K("Rb")], w=[K("Rb")])
                    yield
                pA, pAk = lps()
                for p in range(4):
                    self.mm(pA[0:64, p * 64:(p + 1) * 64], Rb[:, p, 64:128], kdec[:, p, :], r=[K("Rb"), K("Gd")], w=[pAk], nosig=True)
                    self.mm(pA[0:64, (4 + p) * 64:(5 + p) * 64], kdec[:, p, :], Rb[:, p, 0:64], r=[K("Rb"), K("Gd")], w=[pAk], nosig=(p != 3))
                pQ, pQk = lps()
                for p in range(4):
                    self.mm(pQ[0:64, p * 64:(p + 1) * 64], tok[:, p, 0, :], dgc[:, p, :], start=True, stop=False, r=[K("tok"), K("dgc")], w=[pQk], nosig=True)
                    self.mm(pQ[0:64, p * 64:(p + 1) * 64], Rb[:, p, 64:128], QKmT[:, p, :], start=False, stop=True, r=[K("Rb"), K("QKmT")], w=[pQk],
                            nosig=(p != 3))
                yield
                pAv = pA[0:64, :].rearrange("p (t a n) -> p t a n", t=2, n=64)
                self.tt("dve", AT[:], pAv[:, 0, :, :], Idec[:], ALU.add, r=[pAk, K("Idec")], w=[K("AT")])
                self.cp("act", QpT[:], pQ[0:64, 0:256].rearrange("p (a n) -> p a n", n=64), r=[pQk], w=[K("QpT")])
                yield
                self.cp("act", Bm[:], pAv[:, 1, :, :], r=[pAk], w=[K("Bm")])
                yield

            def scan(s, Lb):
                li = Lb["i"]

                def K(n):
                    return f"{n}_{li}"
                AT, Bm, QKmT, QpT, Rb = Lb["AT"], Lb["Bm"], Lb["QKmT"], Lb["QpT"], Lb["Rb"]
                nchd = chunks_of(s)
                pO, pOk = self.psum[6], "ps6"
                for p in range(4):
                    self.mm(pO[0:64, p * 64:(p + 1) * 64], QpT[:, p, :], Sst[:, p, :], start=True, stop=False, r=[K("QpT"), "Sst"], w=[pOk], nosig=True)
                    self.mm(pO[0:64, p * 64:(p + 1) * 64], QKmT[:, p, :], Rb[:, p, 0:64], start=False, stop=True, r=[K("QKmT"), K("Rb")], w=[pOk],
                            nosig=(p != 3))
                pN, pNk = self.psum[7], "ps7"
                for p in range(4):
                    self.mm(pN[0:64, p * 64:(p + 1) * 64], AT[:, p, :], Sst[:, p, :], r=[K("AT"), "Sst"], w=[pNk], nosig=(p != 3))
                self.tt("dve", Sst[:], pN[0:64, 0:256].rearrange("p (a n) -> p a n", n=64), Bm[:], ALU.add, r=[pNk, K("Bm")], w=["Sst"])
                pOv = pO[0:64, 0:256].rearrange("p (h d n) -> p d h n", d=2, n=64)
                for dd in range(2):
                    n_ = nchd[dd]
                    sb_other = (3 - n_) if n_ < 4 else (39 - n_)
                    ostep = sb_other if dd == 0 else n_
                    first = s < ostep
                    src = pOv[:, dd, :, :]
                    dst = obuf[:, n_, :].rearrange("p (h n) -> p h n", n=64)
                    if first:
                        self.cp("act", dst, src, r=[pOk], w=["obuf"])
                    else:
                        self.tt("dve", dst, src, dst, ALU.add, r=[pOk, "obuf"], w=["obuf"])

            nsteps = self.cfg.get("gsteps", 36)
            NSEG = 30
            stagger = self.cfg.get("stagger", 5)
            gens = [None] * NLANE
            step_of = [None] * NLANE
            start_tick = [i * stagger for i in range(NLANE)]
            next_step = 0
            done = 0
            tick = 0
            while done < nsteps:
                for li in range(NLANE):
                    if gens[li] is None:
                        if next_step < nsteps and tick >= start_tick[li]:
                            gens[li] = pre(next_step, lanes[li])
                            step_of[li] = next_step
                            next_step += 1
                        else:
                            continue
                    try:
                        next(gens[li])
                    except StopIteration:
                        scan(step_of[li], lanes[li])
                        gens[li] = None
                        done += 1
                tick += 1
            o = lane_base
            self.barrier()
            if gstage == 3:
                continue
            o2 = o
            osq = self.sb("osq", [64, 36 * 128], F32, o2); o2 += 36 * 128 * 4
            oss = self.sb("oss", [64, 72], F32, o2); o2 += 288
            assert o2 <= END - 9216, o2
            ov = obuf[:].rearrange("p a n -> p (a n)")
            self.tt("pool", osq[:], ov, ov, ALU.mult, r=["obuf"], w=["osq"])
            self.S.op("dve", (lambda e: e.tensor_reduce(out=oss[:], in_=osq[:].rearrange("p (a n) -> p a n", n=64), axis=AX.X, op=ALU.add)),
                      reads=["osq"], writes=["oss"])
            self.act(oss[:], oss[:], AF.Ln, r=["oss"], w=["oss"], scale=1.0 / 64, bias=self.epsb[0:64, 0:1])
            self.act(oss[:], oss[:], AF.Exp, r=["oss"], w=["oss"], scale=-0.5)
            self.tt("dve", obuf[:].rearrange("p a (h n) -> p (a h) n", n=64), obuf[:].rearrange("p a (h n) -> p (a h) n", n=64),
                    oss[:].unsqueeze(2).broadcast_to([64, 72, 64]), ALU.mult, r=["obuf", "oss"], w=["obuf"])
            if gstage == 4:
                self.barrier()
                continue
            for c8 in range(0, 36, 8):
                nchk = min(8, 36 - c8)
                if last and c8 + nchk <= 4:
                    continue
                pt, pk = self.ps()
                for ci in range(nchk):
                    self.mm(pt[:, ci * 64:(ci + 1) * 64], obuf[:, c8 + ci, :], ident[0:64, 0:64], r=["obuf", "ident"], w=[pk], nosig=(ci != nchk - 1))
                segs = []
                g0, g1 = c8 * 64, (c8 + nchk) * 64
                if g0 < LC:
                    segs.append((g0, min(g1, LC), L + g0))
                if g1 > LC:
                    a = max(g0, LC)
                    segs.append((a, g1, a - LC))
                for (a, e_, fo) in segs:
                    if gstage in (5, 6):
                        self.ts("dve", catD[:, hp, fo:fo + (e_ - a)], pt[:, a - g0:e_ - g0], G["ogT"][:, l:l + 1], ALU.mult, r=[pk, "ogT"], w=["catD"])
                        continue
                    self.stt("dve", catD[:, hp, fo:fo + (e_ - a)], pt[:, a - g0:e_ - g0], G["ogT"][:, l:l + 1], dzs[:, a:e_], ALU.mult, ALU.mult,
                             r=[pk, "ogT", "dzs"], w=["catD"])
            self.barrier()


def _prep_inputs(inp, core):
    cst = _consts()
    b0 = 2 * core
    f = lambda a: np.ascontiguousarray(a, dtype=np.float32)
    m = {}
    m["x"] = f(inp["x"][b0:b0 + 2])
    m["ctx"] = f(inp["ctx"][b0:b0 + 2])
    rows = np.stack([inp["c"][b0], inp["c"][b0 + 1], inp["c_ctx"]], axis=0)
    m["cT"] = f(rows.reshape(3, 8, 128).transpose(2, 1, 0))
    m["w_mod"] = f(inp["w_mod"])
    m["b_modT"] = f(inp["b_mod"].reshape(2, 48, 128).transpose(2, 0, 1))
    m["n1T"] = f(inp["norm1_g"].reshape(2, 8, 128).transpose(2, 0, 1))
    m["n2T"] = f(inp["norm2_g"].reshape(2, 8, 128).transpose(2, 0, 1))
    m["w_in"] = f(inp["w_in"])
    m["convT"] = f(inp["conv_w"].reshape(2, 3, 6, 128).transpose(3, 0, 2, 1))
    m["qgT"] = f(np.tile(inp["q_norm_g"], (1, 2)).T)
    m["kgT"] = f(np.tile(inp["k_norm_g"], (1, 2)).T)
    m["ogT"] = f(np.tile(inp["o_norm_g"], (1, 2)).T)
    m["alog_b"] = f(np.broadcast_to(inp["a_log"].reshape(1, 2, 8), (64, 2, 8)))
    m["dtb_b"] = f(np.broadcast_to(inp["dt_bias"].reshape(1, 2, 8), (64, 2, 8)))
    m["w_out"] = f(inp["w_out"])
    m["w_ff1"] = f(inp["w_ff1"])
    m["w_ff2"] = f(inp["w_ff2"])
    for k in ["ident", "blk1", "rotm", "chd", "cl", "sl", "cc", "scx", "ropec", "ropes", "gcon"]:
        m[k] = cst[k]
    return m


_PROG = {}


def kernel(**inputs):
    inp = {k: np.asarray(v) for k, v in inputs.items()}
    cfg = dict(_CFG)
    key = repr(sorted(cfg.items()))
    if key not in _PROG:
        _PROG[key] = Prog(cfg).build()
    nc = _PROG[key]
    in_maps = [_prep_inputs(inp, c) for c in range(NCORES)]
    res = run_bass_kernel_spmd(nc, in_maps, core_ids=list(range(NCORES)))
    out = np.concatenate([np.asarray(r["out"]) for r in res.results], axis=0)
    return out.astype(np.float32)


_CFG = {"nb": 2, "nl": 2, "fam": "GFAM"}
```
